# Optimizing a Trainium2 kernel written in Bass

```python
import jax, jax.numpy as jnp
from jax import lax
import numpy as np

D_MODEL = 1024
BATCH = 2
SEQ = 16384
DEPTH = 4

CHUNK = 64
N_META = 16
N_EVEN = (DEPTH + 1) // 2
N_ODD = DEPTH // 2
EPS = 1e-6
D_FF = 4 * D_MODEL

HG_HEADS = 4
HG_K = 128
HG_V = (D_MODEL // 2) // HG_HEADS
HG_KW = HG_HEADS * HG_K
HG_VW = HG_HEADS * HG_V

MLA_HEADS = 4
NOPE = 128
ROPE = 64
V_DIM = (D_MODEL // 2) // MLA_HEADS
QK_DIM = NOPE + ROPE
Q_RANK = 256
KV_RANK = 256
ROPE_THETA = 10000.0
Q_BLOCK = 128

IN_SPLITS = (HG_KW, 2 * HG_KW, 2 * HG_KW + HG_VW, 2 * HG_KW + 2 * HG_VW,
             2 * HG_KW + 2 * HG_VW + Q_RANK, 2 * HG_KW + 2 * HG_VW + Q_RANK + KV_RANK)
IN_COLS = 2 * HG_KW + 2 * HG_VW + Q_RANK + KV_RANK + ROPE

POOL_WINDOWS = (2, 4, 8, 16)
POOL_GROUPS = len(POOL_WINDOWS)
POOL_G = D_MODEL // POOL_GROUPS

kernel_name = "hybrid_hgrn2_mla_pool_trunk"


def rmsnorm(x, g):
    xf = x.astype(jnp.float32)
    y = xf * lax.rsqrt(jnp.mean(xf * xf, axis=-1, keepdims=True) + EPS)
    return (y * g.astype(jnp.float32)).astype(x.dtype)


def chunk_ids(L):
    p = jnp.arange(L)
    return jnp.where(p < N_META, 0, 1 + (p - N_META) // CHUNK)


def rope_tables(L):
    half = ROPE // 2
    inv = ROPE_THETA ** (-jnp.arange(half, dtype=jnp.float32) / half)
    ang = jnp.arange(L, dtype=jnp.float32)[:, None] * inv[None, :]
    return jnp.cos(ang), jnp.sin(ang)


def rope_tail(x, cos, sin):
    xn, xr = x[..., :NOPE], x[..., NOPE:]
    x1, x2 = xr[..., :ROPE // 2], xr[..., ROPE // 2:]
    c = cos[None, :, None, :].astype(x.dtype)
    s = sin[None, :, None, :].astype(x.dtype)
    return jnp.concatenate([xn, x1 * c - x2 * s, x2 * c + x1 * s], axis=-1)


def hgrn2_chunk_scan(q, k, v, log_f):
    B, L, H, K = q.shape
    lead = CHUNK - N_META
    tail = (-(L + lead)) % CHUNK
    nc = (L + lead + tail) // CHUNK

    def to_chunks(a):
        a = jnp.pad(a.astype(jnp.float32), ((0, 0), (lead, tail), (0, 0), (0, 0)))
        return a.reshape(B, nc, CHUNK, H, a.shape[-1]).transpose(1, 0, 3, 2, 4)

    qc, kc, vc, gc = to_chunks(q), to_chunks(k), to_chunks(v), to_chunks(log_f)
    tri = jnp.tril(jnp.ones((CHUNK, CHUNK), dtype=bool))[:, :, None]

    def step(S, inp):
        qi, ki, vi, gi = inp
        b = jnp.cumsum(gi, axis=2)
        o_inter = jnp.einsum('bhtk,bhkv->bhtv', qi * jnp.exp(b), S)
        diff = b[:, :, :, None, :] - b[:, :, None, :, :]
        dec = jnp.exp(jnp.where(tri, diff, -jnp.inf))
        att = jnp.einsum('bhtk,bhtsk,bhsk->bhts', qi, dec, ki)
        o_intra = jnp.einsum('bhts,bhsv->bhtv', att, vi)
        b_last = b[:, :, -1:, :]
        S_new = jnp.exp(b_last[:, :, 0, :])[..., None] * S + jnp.einsum(
            'bhsk,bhsv->bhkv', ki * jnp.exp(b_last - b), vi)
        return S_new, o_inter + o_intra

    S0 = jnp.zeros((B, H, K, v.shape[-1]), jnp.float32)
    _, o = lax.scan(step, S0, (qc, kc, vc, gc))
    o = o.transpose(1, 0, 3, 2, 4).reshape(B, nc * CHUNK, H, -1)
    return o[:, lead:lead + L]


def hgrn2_mixer(hq, hf, hi, hg, lb, out_gain):
    B, L, _ = hq.shape
    lb = lb.astype(jnp.float32)
    log_f = jnp.logaddexp(jnp.log(lb), jnp.log1p(-lb) + jax.nn.log_sigmoid(hf.astype(jnp.float32)))
    k = -jnp.expm1(log_f)
    q = jax.nn.silu(hq.astype(jnp.float32))
    o = hgrn2_chunk_scan(q.reshape(B, L, HG_HEADS, HG_K), k.reshape(B, L, HG_HEADS, HG_K),
                         hi.reshape(B, L, HG_HEADS, HG_V), log_f.reshape(B, L, HG_HEADS, HG_K))
    o = rmsnorm(o, out_gain) * jax.nn.silu(hg.reshape(B, L, HG_HEADS, HG_V).astype(jnp.float32))
    return o.reshape(B, L, HG_VW).astype(hq.dtype)


def block_causal_attention(q, k, v, cid):
    B, L, H, Dq = q.shape
    nq = -(-L // Q_BLOCK)
    pad = nq * Q_BLOCK - L
    qp = jnp.pad(q, ((0, 0), (0, pad), (0, 0), (0, 0)))
    qb = qp.reshape(B, nq, Q_BLOCK, H, Dq).transpose(1, 0, 2, 3, 4)
    qcid = jnp.pad(cid, (0, pad), constant_values=L).reshape(nq, Q_BLOCK)
    scale = Dq ** -0.5

    def one_block(args):
        qblk, qc = args
        s = jnp.einsum('bqhd,bkhd->bhqk', qblk, k).astype(jnp.float32) * scale
        mask = cid[None, :] <= qc[:, None]
        p = jax.nn.softmax(jnp.where(mask, s, -jnp.inf), axis=-1).astype(v.dtype)
        return jnp.einsum('bhqk,bkhd->bqhd', p, v)

    out = lax.map(one_block, (qb, qcid))
    out = out.transpose(1, 0, 2, 3, 4).reshape(B, nq * Q_BLOCK, H, v.shape[-1])
    return out[:, :L]


def mla_mixer(cq, ckv, kr, qa_g, kva_g, w_q_up, w_kv_up, qn_g, kn_g, cos, sin, cid):
    B, L, _ = cq.shape
    q = jnp.einsum('blr,re->ble', rmsnorm(cq, qa_g), w_q_up).reshape(B, L, MLA_HEADS, QK_DIM)
    kv = jnp.einsum('blr,re->ble', rmsnorm(ckv, kva_g), w_kv_up).reshape(B, L, MLA_HEADS, NOPE + V_DIM)
    k_nope, v = kv[..., :NOPE], kv[..., NOPE:]
    k = jnp.concatenate([k_nope, jnp.broadcast_to(kr[:, :, None, :], (B, L, MLA_HEADS, ROPE))], axis=-1)
    q = rope_tail(rmsnorm(q, qn_g), cos, sin)
    k = rope_tail(rmsnorm(k, kn_g), cos, sin)
    o = block_causal_attention(q, k, v, cid)
    return o.reshape(B, L, MLA_HEADS * V_DIM)


def multiscale_pool(h, w_groups, scale):
    B, L, D = h.shape
    hf = h.astype(jnp.float32).reshape(B, L, POOL_GROUPS, POOL_G)
    t = jnp.arange(L, dtype=jnp.float32)
    diffs = []
    for gi, w in enumerate(POOL_WINDOWS):
        c = jnp.pad(jnp.cumsum(hf[:, :, gi], axis=1), ((0, 0), (w, 0), (0, 0)))
        win = c[:, w:] - c[:, :L]
        cnt = jnp.minimum(t + 1.0, float(w))[None, :, None]
        diffs.append(win / cnt - hf[:, :, gi])
    d = jnp.stack(diffs, axis=2).astype(h.dtype)
    y = jnp.einsum('blgc,gce->blge', d, w_groups).reshape(B, L, D)
    return y * scale


def squared_relu_mlp(h, w_up, w_down):
    a = jax.nn.relu(jnp.einsum('bld,df->blf', h, w_up))
    return jnp.einsum('blf,fd->bld', a * a, w_down)


def setup_inputs(seed: int = 0) -> dict:
    key = jax.random.key(seed)
    ks = jax.random.split(key, 20)
    n = jax.random.normal
    f32 = jnp.float32
    return {
        'x': n(ks[0], (BATCH, SEQ, D_MODEL), f32),
        'meta_tokens': n(ks[1], (N_META, D_MODEL), f32),
        'mix_norm': 1.0 + 0.02 * n(ks[2], (DEPTH, D_MODEL), f32),
        'mlp_norm': 1.0 + 0.02 * n(ks[3], (DEPTH, D_MODEL), f32),
        'w_mlp_up': n(ks[4], (DEPTH, D_MODEL, D_FF), f32) * D_MODEL ** -0.5,
        'w_mlp_down': n(ks[5], (DEPTH, D_FF, D_MODEL), f32) * (0.5 * D_FF ** -0.5),
        'w_in': n(ks[6], (N_EVEN, D_MODEL, IN_COLS), f32) * D_MODEL ** -0.5,
        'hgrn_lb': 0.5 * n(ks[7], (N_EVEN, HG_KW), f32),
        'hgrn_out_norm': 1.0 + 0.02 * n(ks[8], (N_EVEN, HG_V), f32),
        'mla_q_a_norm': 1.0 + 0.02 * n(ks[9], (N_EVEN, Q_RANK), f32),
        'mla_kv_a_norm': 1.0 + 0.02 * n(ks[10], (N_EVEN, KV_RANK), f32),
        'w_q_up': n(ks[11], (N_EVEN, Q_RANK, MLA_HEADS * QK_DIM), f32) * Q_RANK ** -0.5,
        'w_kv_up': n(ks[12], (N_EVEN, KV_RANK, MLA_HEADS * (NOPE + V_DIM)), f32) * KV_RANK ** -0.5,
        'q_norm': 1.0 + 0.02 * n(ks[13], (N_EVEN, QK_DIM), f32),
        'k_norm': 1.0 + 0.02 * n(ks[14], (N_EVEN, QK_DIM), f32),
        'w_out': n(ks[15], (N_EVEN, HG_VW + MLA_HEADS * V_DIM, D_MODEL), f32) * D_MODEL ** -0.5,
        'pool_w': n(ks[16], (N_ODD, POOL_GROUPS, POOL_G, POOL_G), f32) * POOL_G ** -0.5,
        'pool_scale': 1.0 + 0.1 * n(ks[17], (N_ODD, D_MODEL), f32),
    }


def reference(x, meta_tokens, mix_norm, mlp_norm, w_mlp_up, w_mlp_down, w_in, hgrn_lb,
              hgrn_out_norm, mla_q_a_norm, mla_kv_a_norm, w_q_up, w_kv_up, q_norm, k_norm,
              w_out, pool_w, pool_scale):
    B = x.shape[0]
    meta = jnp.broadcast_to(meta_tokens[None].astype(x.dtype), (B, N_META, D_MODEL))
    h = jnp.concatenate([meta, x], axis=1)
    L = h.shape[1]
    cid = chunk_ids(L)
    cos, sin = rope_tables(L)
    lb_cum = jnp.cumsum(jax.nn.softmax(hgrn_lb.astype(jnp.float32), axis=0), axis=0)
    lower_bounds = lb_cum - lb_cum[0:1]

    for layer in range(DEPTH):
        u = rmsnorm(h, mix_norm[layer])
        if layer % 2 == 0:
            e = layer // 2
            z = jnp.einsum('bld,de->ble', u, w_in[e])
            hq, hf, hi, hg, cq, ckv, kr = jnp.split(z, IN_SPLITS, axis=-1)
            o_a = hgrn2_mixer(hq, hf, hi, hg, lower_bounds[e], hgrn_out_norm[e])
            o_b = mla_mixer(cq, ckv, kr, mla_q_a_norm[e], mla_kv_a_norm[e], w_q_up[e], w_kv_up[e],
                            q_norm[e], k_norm[e], cos, sin, cid)
            mix = jnp.einsum('ble,ed->bld', jnp.concatenate([o_a, o_b], axis=-1), w_out[e])
        else:
            o = layer // 2
            mix = multiscale_pool(u, pool_w[o], pool_scale[o])
        h = h + mix
        h = h + squared_relu_mlp(rmsnorm(h, mlp_norm[layer]), w_mlp_up[layer], w_mlp_down[layer])

    return h[:, N_META:]
```

```python
import numpy as np
import ml_dtypes
import concourse.bass as bass
import concourse.mybir as mybir
from concourse.bass_utils import run_bass_kernel_spmd

F32 = mybir.dt.float32
BF16 = mybir.dt.bfloat16
ALU = mybir.AluOpType
AF = mybir.ActivationFunctionType

ENGINES = ("pe", "act", "dve", "pool", "sp")


import threading


class Stepper:
    def __init__(self, fn):
        self.fn = fn
        self.go = threading.Semaphore(0)
        self.back = threading.Semaphore(0)
        self.done = False
        self.started = False
        self.budget = 0
        self.exc = None
        self.th = threading.Thread(target=self._run, daemon=True)

    def _run(self):
        self.go.acquire()
        try:
            self.fn()
        except BaseException as e:
            self.exc = e
        self.done = True
        self.back.release()

    def pause_point(self):
        self.budget -= 1
        if self.budget <= 0:
            self.back.release()
            self.go.acquire()

    def step(self, n):
        if self.done:
            return
        self.budget = n
        if not self.started:
            self.started = True
            self.th.start()
        self.go.release()
        self.back.acquire()
        if self.exc is not None:
            raise self.exc

    def finish(self):
        while not self.done:
            self.step(10 ** 9)


class Sched:
    NDMA = 8

    def __init__(self, nc):
        self.nc = nc
        self.ops = {e: [] for e in ENGINES}
        self.cnt = {e: 0 for e in ENGINES}
        self.sem = {}
        self.dsem = {}
        self.dcnt = {"sp": 0, "pool": 0, "act": 0}
        self.last_w = {}
        self.readers = {}
        self.waited = {e: {} for e in ENGINES}
        self._stack = []
        self.stepper = None

    def _pause(self):
        st = self.stepper
        if st is not None and threading.current_thread() is st.th:
            st.pause_point()

    def open(self):
        import contextlib
        self._es = contextlib.ExitStack()
        for e in ("pe", "act", "dve", "pool"):
            self.sem[e] = self._es.enter_context(self.nc.semaphore("s_" + e))
        for q in ("sp", "pool"):
            self.dsem[q] = [self._es.enter_context(self.nc.semaphore("d_%s%d" % (q, i)))
                            for i in range(self.NDMA)]
        return self

    def close(self):
        self._es.close()

    def barrier(self):
        evs = []
        for e in ("pe", "act", "dve", "pool"):
            if self.cnt[e]:
                evs.append((self.sem[e], self.cnt[e]))
        n = self.cnt.get("cc", 0)
        for k in range(min(n, self.NCC)):
            last_i = n - 1 - ((n - 1 - k) % self.NCC)
            evs.append((self.ccsem[k], last_i // self.NCC + 1))
        for q in ("sp", "pool"):
            n = self.dcnt[q]
            for k in range(min(n, self.NDMA)):
                last_i = n - 1 - ((n - 1 - k) % self.NDMA)
                evs.append((self.dsem[q][k], 16 * (last_i // self.NDMA + 1)))
        for e in ENGINES:
            waits = self._waits(e, evs)
            if waits:
                self.ops[e].append((waits, None, None, 0))
        self.last_w = {}
        self.readers = {}

    def _deps(self, reads, writes):
        evs = []
        for r in reads:
            if r in self.last_w:
                evs.append(self.last_w[r])
        for w in writes:
            if w in self.last_w:
                evs.append(self.last_w[w])
            evs.extend(self.readers.get(w, ()))
        return evs

    def _commit(self, ev, reads, writes):
        for r in reads:
            self.readers.setdefault(r, []).append(ev)
        for w in writes:
            self.last_w[w] = ev
            self.readers[w] = []

    def _waits(self, eng, evs):
        out = []
        wd = self.waited[eng]
        best = {}
        for (s, v) in evs:
            key = id(s)
            if wd.get(key, 0) >= v:
                continue
            if key not in best or best[key][1] < v:
                best[key] = (s, v)
        for key, (s, v) in best.items():
            wd[key] = v
            out.append((s, v))
        return out

    def op(self, eng, fn, reads=(), writes=()):
        evs = self._deps(reads, writes)
        if eng == "pe":
            evs = [ev for ev in evs if ev[0] is not self.sem["pe"]]
        waits = self._waits(eng, evs)
        self.cnt[eng] += 1
        ev = (self.sem[eng], self.cnt[eng])
        self.ops[eng].append((waits, fn, ev[0], 1))
        self._commit(ev, reads, writes)
        self._pause()
        return ev

    def dma(self, q, out, in_, reads=(), writes=(), **kw):
        i = self.dcnt[q]
        self.dcnt[q] += 1
        s = self.dsem[q][i % self.NDMA]
        prev = 16 * (i // self.NDMA)
        evs = self._deps(reads, writes)
        if prev:
            evs.append((s, prev))
        waits = self._waits(q, evs)
        ev = (s, prev + 16)

        def fn(e, out=out, in_=in_, kw=kw):
            return e.dma_start(out=out, in_=in_, **kw)
        self.ops[q].append((waits, fn, s, 16))
        self._commit(ev, reads, writes)
        self._pause()
        return ev

    NCC = 4

    def collective(self, kind, ins, outs, groups, reads=(), writes=()):
        if "cc" not in self.cnt:
            self.ccsem = [self._es.enter_context(self.nc.semaphore("s_cc%d" % i)) for i in range(self.NCC)]
            self.cnt["cc"] = 0
        i = self.cnt["cc"]
        self.cnt["cc"] += 1
        s_ = self.ccsem[i % self.NCC]
        prev = i // self.NCC
        evs = self._deps(reads, writes)
        if prev:
            evs.append((s_, prev))
        waits = self._waits("pool", evs)
        ev = (s_, prev + 1)

        def fn(e):
            return e.collective_compute(kind, ALU.bypass, replica_groups=groups, ins=ins, outs=outs)
        self.ops["pool"].append((waits, fn, s_, None))
        self._commit(ev, reads, writes)
        return ev

    def finish(self, eng, evs):
        waits = self._waits(eng, evs)
        self.ops[eng].append((waits, None, None, 0))

    def emit(self, block):
        nc = self.nc

        def run(eng_name):
            def body(e):
                for (waits, fn, s, inc) in self.ops[eng_name]:
                    for (ws, wv) in waits:
                        e.wait_ge(ws, wv)
                    if fn is not None:
                        ins = fn(e)
                        if inc is None:
                            ins.then_inc(s)
                        else:
                            ins.then_inc(s, inc)
                self.ops[eng_name] = []
            return body
        block.tensor(run("pe"))
        block.scalar(run("act"))
        block.vector(run("dve"))
        block.gpsimd(run("pool"))
        block.sync(run("sp"))


D = 1024
DC = 8
DFF = 4096
EPS = 1e-6
PRE = 64
SEQ_FULL = 16384
ATT_SCALE = 192.0 ** -0.5
INTERLEAVE = False
STEP_EVERY = 1
HOLD_MLA = True
NEG = -30000.0


_UID = [0]


def _uid():
    _UID[0] += 1
    return "u%d_" % _UID[0]


class Rot:
    def __init__(self, names):
        self.names = list(names)
        self.i = 0

    def next(self):
        n = self.names[self.i % len(self.names)]
        self.i += 1
        return n


class WStream:
    def __init__(self, S, name, slots, items, look=1):
        self.S, self.name, self.slots, self.items, self.look = S, name, slots, items, look
        self.emitted = 0

    def _emit_upto(self, i):
        while self.emitted <= min(i, len(self.items) - 1):
            k = self.emitted
            slot = self.slots[k % len(self.slots)]
            self.S.dma("sp", slot[:], self.items[k][0], reads=[self.items[k][1]],
                       writes=["%s%d" % (self.name, k % len(self.slots))])
            self.emitted += 1

    def get(self, i):
        self._emit_upto(i + self.look)
        k = i % len(self.slots)
        return self.slots[k], "%s%d" % (self.name, k)


def _mm_group(mms):
    def fn(e):
        ins = None
        n = len(mms)
        for i, (o, l, r) in enumerate(mms):
            ins = e.matmul(o, l, r, start=(i == 0), stop=(i == n - 1))
        return ins
    return fn


class Ctx:
    pass


def t_tiles(nown):
    return [(0, PRE)] + [(PRE + 512 * i, 512) for i in range(nown // 512)]


def emit_t_conversions(S, T):
    for li in range(2):
        for g in range(8):
            S.dma("pool", T["wup_bf"][li, g], T["wup_f32"][li, g], writes=["wupbf%d_%d" % (li, g)])
            S.dma("pool", T["wdn_bf"][li, g], T["wdn_f32"][li, g], writes=["wdnbf%d_%d" % (li, g)])


def build_tphase(nc, S, es, nown, T, layers, tail, convert=True):
    ncol = PRE + nown
    uid = _uid()
    sb = lambda name, shape, dt: es.enter_context(nc.sbuf_tensor(uid + name, shape, dt))
    pbanks = [es.enter_context(nc.psum_tensor(uid + "pb%d" % i, [128, 512], F32)) for i in range(8)]
    prot = Rot(range(8))

    ones = sb("ones", [128, 128], BF16)
    S.op("pool", lambda e: e.memset(ones[:], 1.0), writes=["ones"])
    gains = sb("gains", [128, 5, DC], F32)
    S.dma("sp", gains[:], T["gains"], writes=["gains"])

    h_t = [sb("h_t%d" % i, [128, DC, 512], F32) for i in range(2)]
    u_bf = sb("u_bf", [128, DC, 512], BF16)
    sq_bf = sb("sq_bf", [128, DC, 512], BF16)
    rt = sb("rt", [128, 512], F32)
    rstd = sb("rstd", [128, 512], F32)

    if layers is not None:
        o_t = [sb("o_t%d" % i, [128, DC, 512], BF16) for i in range(2)]
        wout = sb("wout", [128, 8 * 8 * 128], BF16)
        S.dma("pool", wout[:], T["wout_f32"], writes=["wout"])
        poolw = sb("poolw", [128, 4 * 2 * 2 * 128], BF16)
        S.dma("pool", poolw[:], T["poolw_f32"], writes=["poolw"])
        if convert:
            emit_t_conversions(S, T)
        if T.get("after_loads") is not None:
            T["after_loads"]()
        if T.get("sel") is not None:
            selv = sb("selv", [128, 4], F32)
            S.dma("sp", selv[:], T["sel"], writes=["selv"])
            cand = [sb("cand%d" % i, [128, DC, 512], BF16) for i in range(2)]
            candrot = Rot(range(2))
        invcnt = sb("invcnt", [128, 4, PRE], F32)
        S.dma("sp", invcnt[:], T["invcnt"], writes=["invcnt"])
        a_bf = sb("a_bf", [128, 32, 512], BF16)
        rl = [sb("rl%d" % i, [128, 512], F32) for i in range(2)]
        rlrot = Rot(range(2))
        u32 = sb("u32", [128, DC, 16 + 512], F32)
        wt = [sb("wt%d" % i, [128, 2, 16 + 512], F32) for i in range(2)]
        d_bf = sq_bf
        S.op("pool", lambda e: e.memset(u32[:, :, 0:16], 0.0), writes=["u32halo"])
        wup_slots = [sb("wup%d" % i, [128, 8 * 512], BF16) for i in range(2)]
        wdn_slots = [sb("wdn%d" % i, [128, 32 * 128], BF16) for i in range(2)]
        tiles = t_tiles(nown)
        up_items, dn_items = [], []
        for _ in tiles:
            for li in range(2):
                for g in range(8):
                    up_items.append((T["wup_bf"][li, g], "wupbf%d_%d" % (li, g)))
                for k in range(8):
                    dn_items.append((T["wdn_bf"][li, k], "wdnbf%d_%d" % (li, k)))
        wup = WStream(S, "wup", wup_slots, up_items, look=1)
        wdn = WStream(S, "wdn", wdn_slots, dn_items, look=1)
        cnt = {"up": 0, "dn": 0}

    def rms(h, hname, N, gain_idx, out_bf=None, out32=None):
        S.op("act", lambda e: e.activation(out=sq_bf[:, :, :N], in_=h[:, :, :N], func=AF.Square),
             reads=[hname], writes=["sq_bf"])
        pb = prot.next()
        ps = pbanks[pb]
        S.op("pe", _mm_group([(ps[:, :N], ones[:], sq_bf[:, c, :N]) for c in range(DC)]),
             reads=["sq_bf", "ones"], writes=["pb%d" % pb])
        S.op("act", lambda e: e.activation(out=rt[:, :N], in_=ps[:, :N], func=AF.Ln, bias=EPS, scale=1.0 / D),
             reads=["pb%d" % pb], writes=["rt"])
        S.op("act", lambda e: e.activation(out=rstd[:, :N], in_=rt[:, :N], func=AF.Exp, scale=-0.5), reads=["rt"], writes=["rstd"])
        if out_bf is not None:
            for c in range(DC):
                S.op("dve", lambda e, c=c: e.scalar_tensor_tensor(
                    out=out_bf[:, c, :N], in0=h[:, c, :N], scalar=gains[:, gain_idx, c:c + 1], in1=rstd[:, :N],
                    op0=ALU.mult, op1=ALU.mult), reads=[hname, "gains", "rstd"], writes=["u_bf%d" % c])
        if out32 is not None:
            for c in range(DC):
                S.op("dve", lambda e, c=c: e.scalar_tensor_tensor(
                    out=out32[:, c, 16:16 + N], in0=h[:, c, :N], scalar=gains[:, gain_idx, c:c + 1], in1=rstd[:, :N],
                    op0=ALU.mult, op1=ALU.mult), reads=[hname, "gains", "rstd"], writes=["u32_%d" % c])

    def mlp(h, hname, N, li, hook_up=None, hook_dn=None):
        rms(h, hname, N, li, out_bf=u_bf)
        if hook_up is not None:
            hook_up()
        for g in range(8):
            slot, sname = wup.get(cnt["up"]); cnt["up"] += 1
            for j in range(4):
                pb = prot.next(); ps = pbanks[pb]
                S.op("pe", _mm_group([(ps[:, :N], slot[:, c * 512 + j * 128:c * 512 + (j + 1) * 128], u_bf[:, c, :N])
                                      for c in range(DC)]), reads=["u_bf%d" % c for c in range(DC)] + [sname], writes=["pb%d" % pb])
                r = rlrot.next()
                S.op("act", lambda e, ps=ps, r=r: e.activation(out=rl[r][:, :N], in_=ps[:, :N], func=AF.Relu),
                     reads=["pb%d" % pb], writes=["rl%d" % r])
                f = g * 4 + j
                S.op("pool", lambda e, r=r, f=f: e.tensor_tensor(out=a_bf[:, f, :N], in0=rl[r][:, :N], in1=rl[r][:, :N], op=ALU.mult),
                     reads=["rl%d" % r], writes=["a_bf%d" % f])
        if hook_dn is not None:
            hook_dn()
        for k in range(8):
            slot, sname = wdn.get(cnt["dn"]); cnt["dn"] += 1
            pb = prot.next(); ps = pbanks[pb]
            S.op("pe", _mm_group([(ps[:, :N], slot[:, f * 128:(f + 1) * 128], a_bf[:, f, :N]) for f in range(32)]),
                 reads=["a_bf%d" % f for f in range(32)] + [sname], writes=["pb%d" % pb])
            S.op("dve", lambda e, ps=ps, k=k: e.tensor_tensor(out=h[:, k, :N], in0=h[:, k, :N], in1=ps[:, :N], op=ALU.add),
                 reads=["pb%d" % pb, hname], writes=[hname])

    tiles = t_tiles(nown)
    out_evs = []

    src_ap = T["src_h"].rearrange("(c p) n -> p c n", p=128)

    def load_tile(ti, part):
        if ti >= len(tiles):
            return
        col0, N = tiles[ti]
        hb = ti % 2
        if part == 0:
            S.dma("sp", h_t[hb][:, :, :N], src_ap[:, :, col0:col0 + N], writes=["h_t%d" % hb])
        if layers is None:
            return
        o = o_t[hb]; oname = "o_t%d" % hb
        for cc in (0, 1) if part == 0 else (2, 3):
            ci = candrot.next(); cd = cand[ci]
            for (c0, c1, oap) in T["of_tiles"](col0, N, cc):
                S.dma("sp", cd[:, c0:c1, :N], oap, writes=["cand%d" % ci])
            if cc == 0:
                S.op("dve", lambda e, cd=cd, cc=cc: e.tensor_scalar(
                    out=o[:, :, :N], in0=cd[:, :, :N], scalar1=selv[:, cc:cc + 1], scalar2=None, op0=ALU.mult),
                    reads=["cand%d" % ci, "selv"], writes=[oname])
            else:
                S.op("dve", lambda e, cd=cd, cc=cc: e.scalar_tensor_tensor(
                    out=o[:, :, :N], in0=cd[:, :, :N], scalar=selv[:, cc:cc + 1], in1=o[:, :, :N], op0=ALU.mult, op1=ALU.add),
                    reads=["cand%d" % ci, "selv", oname], writes=[oname])

    def do_tile(ti, col0, N):
        hb = ti % 2
        h = h_t[hb]; hname = "h_t%d" % hb
        if ti == 0:
            load_tile(0, 0)
            load_tile(0, 1)
        if layers is None:
            load_tile(ti + 1, 0)
        if layers is not None:
            o = o_t[hb]; oname = "o_t%d" % hb
            for k in range(8):
                pb = prot.next(); ps = pbanks[pb]
                S.op("pe", _mm_group([(ps[:, :N], wout[:, (k * 8 + c) * 128:(k * 8 + c + 1) * 128], o[:, c, :N]) for c in range(8)]),
                     reads=[oname, "wout"], writes=["pb%d" % pb])
                S.op("dve", lambda e, ps=ps, k=k: e.tensor_tensor(out=h[:, k, :N], in0=h[:, k, :N], in1=ps[:, :N], op=ALU.add),
                     reads=["pb%d" % pb, hname], writes=[hname])
            mlp(h, hname, N, 0, hook_up=lambda: load_tile(ti + 1, 0), hook_dn=lambda: load_tile(ti + 1, 1))
            rms(h, hname, N, 2, out32=u32)
            W = 16 + N
            for g in range(4):
                srcw = u32[:, 2 * g:2 * g + 2, :]
                sname = "u32g%d" % g
                sh = 1
                lo = 0
                for lvl in range(g + 1):
                    dst = wt[lvl % 2]
                    S.op("dve", lambda e, dst=dst, srcw=srcw, sh=sh, lo=lo: e.tensor_tensor(
                        out=dst[:, :, lo + sh:W], in0=srcw[:, :, lo + sh:W], in1=srcw[:, :, lo:W - sh], op=ALU.add),
                        reads=([sname] if sname.startswith("wt") else ["u32_%d" % (2 * g), "u32_%d" % (2 * g + 1)]) + ["u32halo"], writes=["wt%d" % (lvl % 2)])
                    srcw = dst
                    sname = "wt%d" % (lvl % 2)
                    lo += sh
                    sh *= 2
                win = srcw
                if ti == 0:
                    for cc in range(2):
                        S.op("dve", lambda e, win=win, g=g, cc=cc: e.tensor_tensor(
                            out=win[:, cc, 16:16 + N], in0=win[:, cc, 16:16 + N], in1=invcnt[:, g, :N], op=ALU.mult),
                            reads=[sname, "invcnt"], writes=[sname])
                    S.op("dve", lambda e, win=win, g=g: e.tensor_tensor(
                        out=d_bf[:, 2 * g:2 * g + 2, :N], in0=win[:, :, 16:16 + N], in1=u32[:, 2 * g:2 * g + 2, 16:16 + N],
                        op=ALU.subtract), reads=[sname] + ["u32_%d" % c for c in range(DC)], writes=["sq_bf"])
                else:
                    S.op("dve", lambda e, win=win, g=g: e.scalar_tensor_tensor(
                        out=d_bf[:, 2 * g:2 * g + 2, :N], in0=win[:, :, 16:16 + N], scalar=1.0 / (2 ** (g + 1)),
                        in1=u32[:, 2 * g:2 * g + 2, 16:16 + N], op0=ALU.mult, op1=ALU.subtract),
                        reads=[sname] + ["u32_%d" % c for c in range(DC)], writes=["sq_bf"])
            for ke in range(8):
                g = ke // 2
                pb = prot.next(); ps = pbanks[pb]
                S.op("pe", _mm_group([(ps[:, :N], poolw[:, ((g * 2 + ke % 2) * 2 + cc) * 128:((g * 2 + ke % 2) * 2 + cc + 1) * 128],
                                       d_bf[:, 2 * g + cc, :N]) for cc in range(2)]),
                     reads=["sq_bf", "poolw"], writes=["pb%d" % pb])
                S.op("dve", lambda e, ps=ps, ke=ke: e.scalar_tensor_tensor(
                    out=h[:, ke, :N], in0=ps[:, :N], scalar=gains[:, 3, ke:ke + 1], in1=h[:, ke, :N], op0=ALU.mult, op1=ALU.add),
                    reads=["pb%d" % pb, hname, "gains"], writes=[hname])
            S.op("pool", lambda e, N=N: e.tensor_copy(out=u32[:, :, 0:16], in_=u32[:, :, N:N + 16]),
                 reads=["u32_%d" % c for c in range(DC)], writes=["u32halo"])
            mlp(h, hname, N, 1)
        if tail == "out":
            dst = T["out_h"].rearrange("(c p) n -> p c n", p=128)
            out_evs.append(S.dma("pool", dst[:, :, col0:col0 + N], h[:, :, :N], reads=[hname], writes=["out_h"]))
        else:
            if layers is not None:
                dst = T["dst_h"].rearrange("(c p) n -> p c n", p=128)
                out_evs.append(S.dma("pool", dst[:, :, col0:col0 + N], h[:, :, :N], reads=[hname], writes=["dst_h"]))
            rms(h, hname, N, 4, out_bf=u_bf)
            out_evs.append(S.dma("pool", T["xu_dst"](ti, N), u_bf[:, :, :N], reads=["u_bf%d" % c for c in range(DC)], writes=["xu%d" % ti]))
            T["after_xu"](ti)

    for ti, (col0, N) in enumerate(tiles):
        do_tile(ti, col0, N)
    return out_evs


def _mm_list(mms):
    def fn(e):
        ins = None
        for (o, l, r, st, sp) in mms:
            ins = e.matmul(o, l, r, start=st, stop=sp)
        return ins
    return fn


def m_tiles(seq):
    return [(0, PRE)] + [(PRE + 512 * i, 512) for i in range(seq // 512)]


def build_mixer(nc, S, es, seq, T, e_idx):
    LP = PRE + seq
    NKT = 1 + seq // 128
    uid = _uid()
    sb = lambda name, shape, dt: es.enter_context(nc.sbuf_tensor(uid + name, shape, dt))
    pbanks = [es.enter_context(nc.psum_tensor(uid + "mb%d" % i, [128, 512], F32)) for i in range(2)]
    prot = Rot(range(2))
    sbanks = [es.enter_context(nc.psum_tensor(uid + "sb%d" % i, [128, 512], F32)) for i in range(2)]
    srot = Rot(range(2))
    hO_t = es.enter_context(nc.psum_tensor(uid + "hO", [128, 512], F32))
    pO = es.enter_context(nc.psum_tensor(uid + "pO", [128, 512], F32))
    pZ = es.enter_context(nc.psum_tensor(uid + "pZ", [128, 512], F32))
    pT = es.enter_context(nc.psum_tensor(uid + "pT", [128, 1024], BF16))

    ones = sb("ones", [128, 128], BF16)
    S.op("pool", lambda e: e.memset(ones[:], 1.0), writes=["ones"])
    ident = sb("ident", [128, 128], BF16)
    S.dma("pool", ident[:], T["ident"], writes=["ident"])
    wqueue = "sp" if T.get("weights_bf16") else "pool"
    win_fm = sb("win_fm", [128, DC, 1024], BF16)
    S.dma(wqueue, win_fm[:], T["win_fm"], writes=["win_fm"])
    win_hi = sb("win_hi", [128, DC, 128], BF16)
    S.dma(wqueue, win_hi[:], T["win_hi"], writes=["win_hi"])
    wq = sb("wq", [128, 2, 256], BF16)
    S.dma(wqueue, wq[:], T["wq"], writes=["wq"])
    wkv = sb("wkv", [128, 2, 256], BF16)
    S.dma(wqueue, wkv[:], T["wkv"], writes=["wkv"])
    mvec = sb("mvec", [128, 16], F32)
    S.dma("sp", mvec[:], T["mvec"], writes=["mvec"])
    masktab = sb("masktab", [128, 896], F32)
    S.dma("sp", masktab[:], T["masktab"], writes=["masktab"])
    triu = sb("triu", [64, 64], F32)
    S.dma("sp", triu[:], T["triu"], writes=["triu"])
    cmask = sb("cmask", [128, 512], F32)
    S.dma("sp", cmask[:], T["cmask"], writes=["cmask"])
    if T.get("after_loads") is not None:
        T["after_loads"]()
    S.op("dve", lambda e: e.tensor_scalar(out=win_fm[:, :, 960:992], in0=win_fm[:, :, 960:992], scalar1=-1.0, scalar2=None, op0=ALU.mult),
         reads=["win_fm"], writes=["win_fm"])
    S.op("dve", lambda e: e.tensor_scalar(out=wq[:, :, 192:224], in0=wq[:, :, 192:224], scalar1=-1.0, scalar2=None, op0=ALU.mult),
         reads=["wq"], writes=["wq"])
    lbv = sb("lbv", [128, 4], F32)
    if e_idx == 0:
        S.op("dve", lambda e: e.memset(lbv[:, 0:1], 0.0), writes=["lbv"])
        S.op("dve", lambda e: e.memset(lbv[:, 1:2], 1.0), reads=["lbv"], writes=["lbv"])
    else:
        ex = sb("lbex", [128, 2], F32)
        S.op("act", lambda e: e.activation(out=ex[:], in_=mvec[:, 11:13], func=AF.Exp), reads=["mvec"], writes=["lbex"])
        S.op("dve", lambda e: e.tensor_tensor(out=lbv[:, 2:3], in0=ex[:, 0:1], in1=ex[:, 1:2], op=ALU.add), reads=["lbex"], writes=["lbv2"])
        S.op("dve", lambda e: e.reciprocal(out=lbv[:, 3:4], in_=lbv[:, 2:3]), reads=["lbv2"], writes=["lbv3"])
        S.op("dve", lambda e: e.tensor_tensor(out=lbv[:, 0:1], in0=ex[:, 1:2], in1=lbv[:, 3:4], op=ALU.mult), reads=["lbex", "lbv3"], writes=["lbv"])
        S.op("dve", lambda e: e.tensor_tensor(out=lbv[:, 1:2], in0=ex[:, 0:1], in1=lbv[:, 3:4], op=ALU.mult), reads=["lbex", "lbv3", "lbv"], writes=["lbv"])

    Kn = sb("Kn", [128, LP], BF16)
    Kr = sb("Kr", [64, LP], BF16)
    Vs = sb("Vs", [128, NKT * 128], BF16)

    ut = sb("ut", [128, DC, 512], BF16)
    cs = sb("cs", [64, 2, 512], F32)
    wk = {}

    def W(name, shape=(128, 512), dt=F32):
        if name not in wk:
            wk[name] = sb("w_" + name, list(shape), dt)
        return wk[name]

    S32 = sb("S32", [128, 128], F32)
    S_bf2 = [sb("S_bf%d" % i, [128, 128], BF16) for i in range(2)]
    gch = [0]
    S.op("pool", lambda e: e.memset(S32[:], 0.0), writes=["S32"])
    S.op("pool", lambda e: e.memset(S_bf2[0][:], 0.0), writes=["S_bf0"])
    Pt = [sb("Pt%d" % i, [128, 512], BF16) for i in range(3)]
    prot_p = Rot(range(3))
    xo = [sb("xo%d" % i, [128, 2, 512], BF16) for i in range(2)]

    def proj(col0, M, N):
        pb = prot.next(); ps = pbanks[pb]
        S.op("pe", _mm_group([(ps[:M, :N], win_fm[:, c, col0:col0 + M], ut[:, c, :N]) for c in range(DC)]),
             reads=["ut", "win_fm"], writes=["mb%d" % pb])
        return pb, ps

    def rstd_from(ps, pbname, N, dim, outname):
        rt = W("rt")
        rs = W(outname)
        S.op("act", lambda e: e.activation(out=rt[:, :N], in_=ps[:, :N], func=AF.Ln, bias=EPS, scale=1.0 / dim),
             reads=[pbname], writes=["rt"])
        S.op("act", lambda e: e.activation(out=rs[:, :N], in_=rt[:, :N], func=AF.Exp, scale=-0.5), reads=["rt"], writes=[outname])
        return rs

    tiles = m_tiles(seq)
    out_evs = []

    handoff = {}

    def prologue(ti, pos0, N):
        nch = N // 64
        xob = xo[ti % 2]; xoname = "xo%d" % (ti % 2)
        S.dma("sp", ut[:, :, :N], T["uf_tile"](ti, pos0, N), writes=["ut"])
        S.dma("sp", cs[:, 0, :N], T["cosT"][:, pos0:pos0 + N], writes=["cs0"])
        S.dma("sp", cs[:, 1, :N], T["sinT"][:, pos0:pos0 + N], writes=["cs1"])
        q32, sg, gate, logf, kk, bb, eb, enb = (W(n) for n in ("q32", "sg", "gate", "logf", "kk", "bb", "eb", "enb"))
        qe, ke, kd = (W(n, dt=BF16) for n in ("qe", "ke", "kd"))
        pb, ps = proj(0, 128, N)
        S.op("act", lambda e, ps=ps: e.activation(out=q32[:, :N], in_=ps[:, :N], func=AF.Silu), reads=["mb%d" % pb], writes=["q32"])
        pb, ps = proj(256, 128, N)
        S.op("act", lambda e, ps=ps: e.activation(out=gate[:, :N], in_=ps[:, :N], func=AF.Silu), reads=["mb%d" % pb], writes=["gate"])
        pb, ps = proj(128, 128, N)
        S.op("act", lambda e, ps=ps: e.activation(out=sg[:, :N], in_=ps[:, :N], func=AF.Sigmoid), reads=["mb%d" % pb], writes=["sg"])
        S.op("dve", lambda e: e.tensor_scalar(out=sg[:, :N], in0=sg[:, :N], scalar1=lbv[:, 1:2], scalar2=lbv[:, 0:1], op0=ALU.mult, op1=ALU.add),
             reads=["sg", "lbv"], writes=["sg"])
        S.op("act", lambda e: e.activation(out=logf[:, :N], in_=sg[:, :N], func=AF.Ln), reads=["sg"], writes=["logf"])
        S.op("dve", lambda e: e.tensor_scalar(out=kk[:, :N], in0=sg[:, :N], scalar1=-1.0, scalar2=1.0, op0=ALU.mult, op1=ALU.add),
             reads=["sg"], writes=["kk"])
        if ti == 0:
            S.op("dve", lambda e: e.memset(logf[:, 0:48], 0.0), reads=["logf"], writes=["logf"])
        S.op("dve", lambda e: e.tensor_tensor_scan(out=bb[:, :N], data0=cmask[:, :N], data1=logf[:, :N], initial=0.0, op0=ALU.mult, op1=ALU.add),
             reads=["cmask", "logf"], writes=["bb"])
        S.op("act", lambda e: e.activation(out=eb[:, :N], in_=bb[:, :N], func=AF.Exp), reads=["bb"], writes=["eb"])
        S.op("act", lambda e: e.activation(out=enb[:, :N], in_=bb[:, :N], func=AF.Exp, scale=-1.0), reads=["bb"], writes=["enb"])
        S.op("dve", lambda e: e.tensor_tensor(out=qe[:, :N], in0=q32[:, :N], in1=eb[:, :N], op=ALU.mult), reads=["q32", "eb"], writes=["qe"])
        S.op("dve", lambda e: e.tensor_tensor(out=ke[:, :N], in0=kk[:, :N], in1=enb[:, :N], op=ALU.mult), reads=["kk", "enb"], writes=["ke"])
        for c in range(nch):
            S.op("dve", lambda e, c=c: e.scalar_tensor_tensor(
                out=kd[:, c * 64:(c + 1) * 64], in0=kk[:, c * 64:(c + 1) * 64], scalar=eb[:, c * 64 + 63:c * 64 + 64],
                in1=enb[:, c * 64:(c + 1) * 64], op0=ALU.mult, op1=ALU.mult), reads=["kk", "eb", "enb"], writes=["kd"])

        def tr_fn(e, nch=nch):
            ins = None
            for c in range(nch):
                ins = e.transpose(pT[:64, c * 128:(c + 1) * 128], kd[:, c * 64:(c + 1) * 64], ident[:])
            return ins
        S.op("pe", tr_fn, reads=["kd", "ident"], writes=["pT"])
        kdT = W("kdT", (64, 1024), BF16)
        S.op("act", lambda e, nch=nch: e.activation(out=kdT[:, :nch * 128], in_=pT[:64, :nch * 128], func=AF.Copy), reads=["pT"], writes=["kdT"])
        Vc = W("Vc", (64, 1024), BF16)
        for j0 in range(0, nch, 4):
            pb = prot.next(); ps = pbanks[pb]
            n4 = min(4, nch - j0)

            def vc_fn(e, ps=ps, j0=j0, n4=n4):
                ins = None
                for jj in range(n4):
                    cj = j0 + jj
                    for c in range(DC):
                        ins = e.matmul(ps[:64, jj * 128:(jj + 1) * 128], ut[:, c, cj * 64:(cj + 1) * 64], win_hi[:, c, :],
                                       start=(c == 0), stop=(c == DC - 1))
                return ins
            S.op("pe", vc_fn, reads=["ut", "win_hi"], writes=["mb%d" % pb])
            S.op("act", lambda e, ps=ps, j0=j0, n4=n4: e.activation(out=Vc[:, j0 * 128:(j0 + n4) * 128], in_=ps[:64, :n4 * 128], func=AF.Copy),
                 reads=["mb%d" % pb], writes=["Vc"])
        hO = hO_t
        ATm = W("ATm", (64, 512), BF16)
        U32 = W("U32", (128, 1024))
        for c in range(nch):
            cs_ = slice(c * 64, (c + 1) * 64)
            pb = prot.next(); ps = pbanks[pb]
            S.op("pe", _mm_group([(ps[:64, :64], ke[:, cs_], qe[:, cs_])]), reads=["ke", "qe"], writes=["mb%d" % pb])
            S.op("dve", lambda e, ps=ps, cs_=cs_: e.tensor_tensor(out=ATm[:, cs_], in0=ps[:64, :64], in1=triu[:], op=ALU.mult),
                 reads=["mb%d" % pb, "triu"], writes=["ATm%d" % c])
        for c in range(nch):
            pb2 = prot.next(); ps2 = pbanks[pb2]
            S.op("pe", _mm_group([(ps2[:, :128], kdT[:, c * 128:(c + 1) * 128], Vc[:, c * 128:(c + 1) * 128])]),
                 reads=["kdT", "Vc"], writes=["mb%d" % pb2])
            S.op("act", lambda e, ps2=ps2, c=c: e.activation(out=U32[:, c * 128:(c + 1) * 128], in_=ps2[:, :128], func=AF.Copy),
                 reads=["mb%d" % pb2], writes=["U32_%d" % c])
        for c in range(nch):
            cs_ = slice(c * 64, (c + 1) * 64)
            g = gch[0]; gch[0] += 1
            Sin = S_bf2[g % 2]; Sout = S_bf2[(g + 1) % 2]
            S.op("pe", _mm_group([(hO[:, cs_], Vc[:, c * 128:(c + 1) * 128], ATm[:, cs_]),
                                  (hO[:, cs_], Sin[:], qe[:, cs_])]),
                 reads=["Vc", "ATm%d" % c, "S_bf%d" % (g % 2), "qe"], writes=["hO"])
            S.op("dve", lambda e, c=c, Sout=Sout: e.scalar_tensor_tensor(
                out=Sout[:], in0=S32[:], scalar=eb[:, c * 64 + 63:c * 64 + 64], in1=U32[:, c * 128:(c + 1) * 128], op0=ALU.mult, op1=ALU.add),
                reads=["S32", "eb", "U32_%d" % c], writes=["S_bf%d" % ((g + 1) % 2)])
            S.op("dve", lambda e, c=c: e.scalar_tensor_tensor(
                out=S32[:], in0=S32[:], scalar=eb[:, c * 64 + 63:c * 64 + 64], in1=U32[:, c * 128:(c + 1) * 128], op0=ALU.mult, op1=ALU.add),
                reads=["S32", "eb", "U32_%d" % c], writes=["S32"])
        hO_res = ["hO"]
        osq = W("osq", dt=BF16)
        S.op("act", lambda e: e.activation(out=osq[:, :N], in_=hO[:, :N], func=AF.Square), reads=hO_res, writes=["osq"])
        pb, ps = prot.next(), None
        ps = pbanks[pb]
        S.op("pe", _mm_group([(ps[:, :N], ones[:], osq[:, :N])]), reads=["osq", "ones"], writes=["mb%d" % pb])
        rs = rstd_from(ps, "mb%d" % pb, N, 128.0, "rs_o")
        t1 = W("t1")
        S.op("dve", lambda e, rs=rs: e.scalar_tensor_tensor(out=t1[:, :N], in0=hO[:, :N], scalar=mvec[:, 10:11], in1=rs[:, :N], op0=ALU.mult, op1=ALU.mult),
             reads=hO_res + ["mvec", "rs_o"], writes=["t1"])
        S.op("pool", lambda e: e.tensor_tensor(out=xob[:, 0, :N], in0=t1[:, :N], in1=gate[:, :N], op=ALU.mult),
             reads=["t1", "gate"], writes=[xoname + "a"])
        if HOLD_MLA and S.stepper is not None and threading.current_thread() is S.stepper.th:
            S.stepper.budget = 10 ** 9
        cq32 = W("cq32", (128, 2, 512)); csq = W("csq", (128, 2, 512), BF16)
        cqn = W("cqn", (128, 2, 512), BF16); ckvn = W("ckvn", (128, 2, 512), BF16)
        for (col0, gcol, dst, dname) in ((384, 0, cqn, "cqn"), (640, 2, ckvn, "ckvn")):
            for r in range(2):
                pb, ps = proj(col0 + 128 * r, 128, N)
                S.op("act", lambda e, ps=ps, r=r: e.activation(out=cq32[:, r, :N], in_=ps[:, :N], func=AF.Copy), reads=["mb%d" % pb], writes=["cq32_%d" % r])
                S.op("act", lambda e, ps=ps, r=r: e.activation(out=csq[:, r, :N], in_=ps[:, :N], func=AF.Square), reads=["mb%d" % pb], writes=["csq_%d" % r])
            pb = prot.next(); ps = pbanks[pb]
            S.op("pe", _mm_group([(ps[:, :N], ones[:], csq[:, r, :N]) for r in range(2)]), reads=["csq_0", "csq_1", "ones"], writes=["mb%d" % pb])
            rs = rstd_from(ps, "mb%d" % pb, N, 256.0, "rs_c")
            for r in range(2):
                S.op("dve", lambda e, r=r, dst=dst, gcol=gcol, rs=rs: e.scalar_tensor_tensor(
                    out=dst[:, r, :N], in0=cq32[:, r, :N], scalar=mvec[:, gcol + r:gcol + r + 1], in1=rs[:, :N], op0=ALU.mult, op1=ALU.mult),
                    reads=["cq32_%d" % r, "mvec", "rs_c"], writes=[dname])

        def qk_finish(jn, jr, jt, gbase, out_n, out_n_name, out_r, out_r_name):
            sqn = W("sqn", dt=BF16); sqr = W("sqr", (64, 512), BF16)
            n32 = W("q32"); r32 = W("r32", (64, 512)); t32 = W("t32", (64, 512))
            ps_n, pbn = jn()
            S.op("dve", lambda e: e.tensor_copy(out=n32[:, :N], in_=ps_n[:, :N]), reads=[pbn], writes=["q32"])
            S.op("act", lambda e: e.activation(out=sqn[:, :N], in_=ps_n[:, :N], func=AF.Square), reads=[pbn], writes=["sqn"])
            ps_r, pbr = jr()
            S.op("dve", lambda e: e.tensor_copy(out=r32[:, :N], in_=ps_r[:64, :N]), reads=[pbr], writes=["r32"])
            S.op("act", lambda e: e.activation(out=sqr[:, :N], in_=ps_r[:64, :N], func=AF.Square), reads=[pbr], writes=["sqr"])
            ps_t, pbt = jt()
            S.op("dve", lambda e: e.tensor_copy(out=t32[:, :N], in_=ps_t[:64, :N]), reads=[pbt], writes=["t32"])
            pb = prot.next(); ps = pbanks[pb]
            S.op("pe", _mm_group([(ps[:, :N], ones[:], sqn[:, :N]), (ps[:, :N], ones[:64, :], sqr[:, :N])]),
                 reads=["sqn", "sqr", "ones"], writes=["mb%d" % pb])
            rs = rstd_from(ps, "mb%d" % pb, N, 192.0, "rs_qk")
            S.op("dve", lambda e: e.scalar_tensor_tensor(out=out_n, in0=n32[:, :N], scalar=mvec[:, gbase:gbase + 1], in1=rs[:, :N],
                                                         op0=ALU.mult, op1=ALU.mult), reads=["q32", "mvec", "rs_qk"], writes=[out_n_name])
            ra = r32; rb = t32
            S.op("dve", lambda e: e.scalar_tensor_tensor(out=ra[:, :N], in0=r32[:, :N], scalar=mvec[:64, gbase + 1:gbase + 2], in1=cs[:, 0, :N],
                                                         op0=ALU.mult, op1=ALU.mult), reads=["r32", "mvec", "cs0"], writes=["r32"])
            S.op("dve", lambda e: e.scalar_tensor_tensor(out=rb[:, :N], in0=t32[:, :N], scalar=mvec[:64, gbase + 2:gbase + 3], in1=cs[:, 1, :N],
                                                         op0=ALU.mult, op1=ALU.mult), reads=["t32", "mvec", "cs1"], writes=["t32"])
            S.op("pool", lambda e: e.tensor_tensor(out=ra[:, :N], in0=ra[:, :N], in1=rb[:, :N], op=ALU.add), reads=["r32", "t32"], writes=["r32"])
            S.op("dve", lambda e: e.tensor_tensor(out=out_r, in0=ra[:, :N], in1=rs[:64, :N], op=ALU.mult), reads=["r32", "rs_qk"], writes=[out_r_name])

        def job_w(wt, rdname, src, c0, c1, M):
            def f():
                pb = prot.next(); ps = pbanks[pb]
                S.op("pe", _mm_group([(ps[:M, :N], wt[:, r, c0:c1], src[:, r, :N]) for r in range(2)]), reads=[rdname, "wq", "wkv"], writes=["mb%d" % pb])
                return ps, "mb%d" % pb
            return f

        def job_p(c0, M):
            def f():
                pb, ps = proj(c0, M, N)
                return ps, "mb%d" % pb
            return f

        qnn = "qn%d" % (ti % 2); qrn = "qr%d" % (ti % 2)
        qn = W(qnn, dt=BF16); qr = W(qrn, (64, 512), BF16)
        handoff[ti] = (qn, qr, qnn, qrn)
        qk_finish(job_w(wq, "cqn", cqn, 0, 128, 128), job_w(wq, "cqn", cqn, 128, 192, 64), job_w(wq, "cqn", cqn, 192, 256, 64),
                  4, qn[:, :N], qnn, qr[:, :N], qrn)
        kvres = "KV_t%d" % ti
        qk_finish(job_w(wkv, "ckvn", ckvn, 0, 128, 128), job_p(896, 64), job_p(960, 64),
                  7, Kn[:, pos0:pos0 + N], kvres + "n", Kr[:, pos0:pos0 + N], kvres + "r")
        pb = prot.next(); ps = pbanks[pb]
        if ti == 0:
            subs = [(0, 64)]
            kt0 = 0
        else:
            subs = [(jj * 128, 128) for jj in range(4)]
            kt0 = 1 + 4 * (ti - 1)

        def v_fn(e, ps=ps, subs=subs):
            ins = None
            for jj, (c0, M) in enumerate(subs):
                for r in range(2):
                    ins = e.matmul(ps[:M, jj * 128:(jj + 1) * 128], ckvn[:, r, c0:c0 + M], wkv[:, r, 128:256], start=(r == 0), stop=(r == 1))
            return ins
        S.op("pe", v_fn, reads=["ckvn", "wkv"], writes=["mb%d" % pb])
        Mv = subs[0][1]
        S.op("act", lambda e, ps=ps, kt0=kt0, Mv=Mv, ns=len(subs): e.activation(
            out=Vs[:Mv, kt0 * 128:(kt0 + ns) * 128], in_=ps[:Mv, :ns * 128], func=AF.Copy), reads=["mb%d" % pb], writes=[kvres + "v"])
    def attention(ti, pos0, N, stepper):
        xob = xo[ti % 2]; xoname = "xo%d" % (ti % 2)
        qn, qr, qnn, qrn = handoff.pop(ti)
        if ti == 0:
            kts = [(0, 64, "pad", 0)]
        else:
            kts = [(0, 64, "pad", 0)] + [(j, 128, None, 0) for j in range(1, 4 * (ti - 1) + 1)] + \
                  [(4 * (ti - 1) + 1 + jj, 128, "diag", jj) for jj in range(4)]
        nk = len(kts)
        pend = []

        def emit_S(idx):
            j, M, kind, jj = kts[idx]
            kc0 = 0 if j == 0 else PRE + 128 * (j - 1)
            tk = 0 if j == 0 else 1 + (j - 1) // 4
            kres = "KV_t%d" % tk
            pb = srot.next(); ps = sbanks[pb]
            S.op("pe", _mm_group([(ps[:M, :N], Kn[:, kc0:kc0 + M], qn[:, :N]), (ps[:M, :N], Kr[:, kc0:kc0 + M], qr[:, :N])]),
                 reads=[kres + "n", kres + "r", qnn, qrn], writes=["sb%d" % pb])
            p = prot_p.next(); P = Pt[p]
            if kind == "pad":
                S.op("act", lambda e: e.activation(out=P[:M, :N], in_=ps[:M, :N], func=AF.Exp, bias=mvec[:M, 13:14], scale=ATT_SCALE),
                     reads=["sb%d" % pb, "mvec"], writes=["Pt%d" % p])
            elif kind == "diag":
                tmp = W("dtmp")
                off = (3 - jj) * 128
                S.op("dve", lambda e: e.tensor_tensor(out=tmp[:, :N], in0=ps[:, :N], in1=masktab[:, off:off + N], op=ALU.add),
                     reads=["sb%d" % pb, "masktab"], writes=["dtmp"])
                S.op("act", lambda e: e.activation(out=P[:, :N], in_=tmp[:, :N], func=AF.Exp, scale=ATT_SCALE), reads=["dtmp"], writes=["Pt%d" % p])
            else:
                S.op("act", lambda e: e.activation(out=P[:, :N], in_=ps[:, :N], func=AF.Exp, scale=ATT_SCALE), reads=["sb%d" % pb], writes=["Pt%d" % p])
            return (idx, j, M, p, kres)

        def emit_OZ(rec):
            idx, j, M, p, kres = rec
            P = Pt[p]
            S.op("pe", _mm_list([(pO[:, :N], Vs[:M, j * 128:(j + 1) * 128], P[:M, :N], idx == 0, idx == nk - 1),
                                 (pZ[:, :N], ones[:M, :], P[:M, :N], idx == 0, idx == nk - 1)]),
                 reads=[kres + "v", "Pt%d" % p, "ones"], writes=["pO", "pZ"])

        LA = 1
        per_iter = max(1, -(-200 // max(1, nk - 2)))
        for idx in range(nk):
            pend.append(emit_S(idx))
            if len(pend) > LA:
                emit_OZ(pend.pop(0))
            if stepper is not None and INTERLEAVE and idx % STEP_EVERY == STEP_EVERY - 1:
                stepper.step(per_iter * STEP_EVERY)
        while pend:
            emit_OZ(pend.pop(0))
        if stepper is not None:
            stepper.finish()
        rz = W("rz")
        S.op("dve", lambda e: e.reciprocal(out=rz[:, :N], in_=pZ[:, :N]), reads=["pZ"], writes=["rz"])
        S.op("dve", lambda e: e.tensor_tensor(out=xob[:, 1, :N], in0=pO[:, :N], in1=rz[:, :N], op=ALU.mult), reads=["pO", "rz"], writes=[xoname + "b"])
        if ti == 0:
            S.op("pool", lambda e: e.memset(xob[:, :, 0:48], 0.0), reads=[xoname + "a", xoname + "b"], writes=[xoname + "a", xoname + "b"])
        out_evs.append(S.dma("sp", T["xo_dst"](ti, N), xob[:, :, :N], reads=[xoname + "a", xoname + "b"], writes=["xo%d" % ti]))
        T["after_xo"](ti)

    for par in range(2):
        W("qn%d" % par, dt=BF16)
        W("qr%d" % par, (64, 512), BF16)
    prologue(0, *tiles[0])
    for ti, (pos0, N) in enumerate(tiles):
        st = None
        if ti + 1 < len(tiles):
            st = Stepper(lambda ti=ti: prologue(ti + 1, *tiles[ti + 1]))
        S.stepper = st
        attention(ti, pos0, N, st)
        S.stepper = None
    return out_evs


IN_HQ, IN_HF, IN_HI, IN_HG, IN_CQ, IN_CKV, IN_KR = 0, 512, 1024, 1536, 2048, 2304, 2560
BF = ml_dtypes.bfloat16


def _pc(v):
    v = np.asarray(v, np.float32)
    return np.ascontiguousarray(v.reshape(-1, 128).T)


def host_tphase_weights(inp, le, lo, lnext):
    e, o = le // 2, lo // 2
    w = {}
    w["wout_f32"] = np.ascontiguousarray(
        inp["w_out"][e].reshape(8, 128, 8, 128).transpose(1, 2, 0, 3).reshape(128, 8192))
    w["wup_f32"] = np.ascontiguousarray(np.stack([
        inp["w_mlp_up"][l].reshape(8, 128, 8, 512).transpose(2, 1, 0, 3).reshape(8, 128, 4096) for l in (le, lo)]))
    w["wdn_f32"] = np.ascontiguousarray(np.stack([
        inp["w_mlp_down"][l].reshape(32, 128, 8, 128).transpose(2, 1, 0, 3).reshape(8, 128, 4096) for l in (le, lo)]))
    w["poolw_f32"] = np.ascontiguousarray(
        inp["pool_w"][o].reshape(4, 2, 128, 2, 128).transpose(2, 0, 3, 1, 4).reshape(128, 2048))
    g = np.zeros((128, 5, 8), np.float32)
    g[:, 0] = _pc(inp["mlp_norm"][le]); g[:, 1] = _pc(inp["mlp_norm"][lo])
    g[:, 2] = _pc(inp["mix_norm"][lo]); g[:, 3] = _pc(inp["pool_scale"][o])
    if lnext is not None:
        g[:, 4] = _pc(inp["mix_norm"][lnext])
    w["gains"] = g
    return w


def host_invcnt(c):
    t = np.zeros((128, 4, PRE), np.float32)
    for g in range(4):
        wdw = 2 ** (g + 1)
        t[:, g, :] = 1.0 / wdw
        if c == 0:
            for col in range(48, PRE):
                pos = col - 48
                t[:, g, col] = 1.0 / min(pos + 1, wdw)
    return t


def host_mixer_weights(inp, e, hd):
    w_in = inp["w_in"][e]
    cols = np.concatenate([
        np.arange(IN_HQ + hd * 128, IN_HQ + (hd + 1) * 128), np.arange(IN_HF + hd * 128, IN_HF + (hd + 1) * 128),
        np.arange(IN_HG + hd * 128, IN_HG + (hd + 1) * 128), np.arange(IN_CQ, IN_CQ + 256), np.arange(IN_CKV, IN_CKV + 256),
        np.arange(IN_KR, IN_KR + 64), np.arange(IN_KR + 32, IN_KR + 64), np.arange(IN_KR, IN_KR + 32)])
    w = {}
    w["win_fm"] = np.ascontiguousarray(w_in[:, cols].reshape(8, 128, 1024).transpose(1, 0, 2))
    w["win_hi"] = np.ascontiguousarray(w_in[:, IN_HI + hd * 128:IN_HI + (hd + 1) * 128].reshape(8, 128, 128).transpose(1, 0, 2))
    qb = hd * 192
    qcols = np.concatenate([np.arange(qb, qb + 192), np.arange(qb + 160, qb + 192), np.arange(qb + 128, qb + 160)])
    w["wq"] = np.ascontiguousarray(inp["w_q_up"][e][:, qcols].reshape(2, 128, 256).transpose(1, 0, 2))
    w["wkv"] = np.ascontiguousarray(inp["w_kv_up"][e][:, hd * 256:(hd + 1) * 256].reshape(2, 128, 256).transpose(1, 0, 2))
    mv = np.zeros((128, 16), np.float32)
    mv[:, 0:2] = _pc(inp["mla_q_a_norm"][e]); mv[:, 2:4] = _pc(inp["mla_kv_a_norm"][e])
    for base, g in ((4, inp["q_norm"][e]), (7, inp["k_norm"][e])):
        g = np.asarray(g, np.float32)
        mv[:, base] = g[0:128]
        mv[:64, base + 1] = g[128:192]
        mv[:32, base + 2] = g[160:192]; mv[32:64, base + 2] = g[128:160]
    mv[:, 10] = inp["hgrn_out_norm"][e]
    mv[:, 11] = inp["hgrn_lb"][0][hd * 128:(hd + 1) * 128]
    mv[:, 12] = inp["hgrn_lb"][1][hd * 128:(hd + 1) * 128]
    mv[:48, 13] = NEG
    w["mvec"] = mv
    return w


def host_tables(seq):
    LP = PRE + seq
    pos = np.maximum(np.arange(LP) - 48, 0).astype(np.float32)
    inv = (np.float32(10000.0) ** (-(np.arange(32, dtype=np.float32)) / np.float32(32))).astype(np.float32)
    ang = (pos[:, None] * inv[None, :]).astype(np.float32)
    t = {}
    t["cosT"] = np.ascontiguousarray(np.tile(np.cos(ang).astype(np.float32).T, (2, 1)))
    t["sinT"] = np.ascontiguousarray(np.tile(np.sin(ang).astype(np.float32).T, (2, 1)))
    kc = (np.arange(128) // 64)[:, None]
    x = np.arange(896)[None, :]
    t["masktab"] = np.where(kc <= x // 64 - 6, 0.0, NEG).astype(np.float32)
    si = np.arange(64)
    t["triu"] = (si[:, None] <= si[None, :]).astype(np.float32)
    cm = np.ones((128, 512), np.float32); cm[:, ::64] = 0.0
    t["cmask"] = cm
    t["ident"] = np.eye(128, dtype=np.float32)
    return t


import contextlib


def _dram(nc, name, shape, dt, kind):
    return nc.dram_tensor(name, list(shape), dt, kind=kind).ap()


GROUPS = [[0, 1, 2, 3], [4, 5, 6, 7]]
M_W = (("win_fm", [128, 8, 1024]), ("win_hi", [128, 8, 128]), ("wq", [128, 2, 256]), ("wkv", [128, 2, 256]), ("mvec", [128, 16]))
M_TAB = (("cosT", None), ("sinT", None), ("masktab", [128, 896]), ("triu", [64, 64]), ("cmask", [128, 512]), ("ident", [128, 128]))


XB = 2048


def build_fused(seq, stop=None):
    nown = seq // 4
    ncol = PRE + nown
    LP = PRE + seq
    NTt = nown // 512
    NB = seq // XB
    nc = bass.Bass("TRN2", target_bir_lowering=False)
    xT = _dram(nc, "xT", [D, ncol], F32, "ExternalInput")
    sel = _dram(nc, "sel", [128, 4], F32, "ExternalInput")
    invcnt = _dram(nc, "invcnt", [128, 4, PRE], F32, "ExternalInput")
    g0 = _dram(nc, "gains_p0", [128, 5, 8], F32, "ExternalInput")
    tabs = {}
    for name, shape in M_TAB:
        tabs[name] = _dram(nc, name, shape or [64, LP], F32, "ExternalInput")
    mw = []
    for e in range(2):
        mw.append({name: _dram(nc, "%s_e%d" % (name, e), shape, F32, "ExternalInput") for name, shape in M_W})
    tw = []
    for t in range(2):
        d = {}
        d["wout_f32"] = _dram(nc, "wout_f32_t%d" % t, [128, 8192], F32, "ExternalInput")
        d["poolw_f32"] = _dram(nc, "poolw_f32_t%d" % t, [128, 2048], F32, "ExternalInput")
        d["wup_f32"] = _dram(nc, "wup_f32_t%d" % t, [2, 8, 128, 4096], F32, "ExternalInput")
        d["wdn_f32"] = _dram(nc, "wdn_f32_t%d" % t, [2, 8, 128, 4096], F32, "ExternalInput")
        d["gains"] = _dram(nc, "gains_t%d" % t, [128, 5, 8], F32, "ExternalInput")
        d["wup_bf"] = _dram(nc, "wup_bf_t%d" % t, [2, 8, 128, 4096], BF16, "Internal")
        d["wdn_bf"] = _dram(nc, "wdn_bf_t%d" % t, [2, 8, 128, 4096], BF16, "Internal")
        d["invcnt"] = invcnt
        d["sel"] = sel
        tw.append(d)
    out_h = _dram(nc, "out_h", [D, ncol], F32, "ExternalOutput")
    hT = _dram(nc, "hT", [D, ncol], F32, "Internal")
    xu_pre = _dram(nc, "xu_pre", [D, PRE], BF16, "Internal")
    uf_pre = _dram(nc, "uf_pre", [4 * D, PRE], BF16, "Internal")
    xu_blk = [_dram(nc, "xu_blk%d" % t, [D, 512], BF16, "Internal") for t in range(NTt)]
    uf_blk = [_dram(nc, "uf_blk%d" % t, [4 * D, 512], BF16, "Internal") for t in range(NTt)]
    xo_pre = _dram(nc, "xo_pre", [256, PRE], BF16, "Internal")
    of_pre = _dram(nc, "of_pre", [4 * 256, PRE], BF16, "Internal")
    xo_blk = [_dram(nc, "xo_blk%d" % k, [256, XB], BF16, "Internal") for k in range(NB)]
    of_blk = [_dram(nc, "of_blk%d" % k, [4 * 256, XB], BF16, "Internal") for k in range(NB)]

    S = Sched(nc).open()
    MWN = ("win_fm", "win_hi", "wq", "wkv")
    mw_bf = [{name: _dram(nc, "%s_bf_e%d" % (name, e), dict(M_W)[name], BF16, "Internal") for name in MWN} for e in range(2)]

    def emit_m_conversions(e):
        for name in MWN:
            S.dma("pool", mw_bf[e][name], mw[e][name], writes=["mwbf"])

    def xu_dst(ti, N):
        t = xu_pre if ti == 0 else xu_blk[ti - 1]
        return t.rearrange("(c p) n -> p c n", p=128)

    def after_xu(ti):
        if ti == 0:
            S.collective("AllGather", [xu_pre], [uf_pre], GROUPS, reads=["xu0"], writes=["ufpre"])
        else:
            S.collective("AllGather", [xu_blk[ti - 1]], [uf_blk[ti - 1]], GROUPS, reads=["xu%d" % ti], writes=["ufblk%d" % (ti - 1)])

    def uf_tile(ti, pos0, N):
        if ti == 0:
            return uf_pre.rearrange("(g c p) n -> g p c n", g=4, p=128)[0]
        i = ti - 1
        return uf_blk[i % NTt].rearrange("(g c p) n -> g p c n", g=4, p=128)[i // NTt]

    def xo_dst(ti, N):
        if ti == 0:
            return xo_pre.rearrange("(j p) n -> p j n", p=128)
        i = ti - 1
        off = (i % 4) * 512
        return xo_blk[i // 4].rearrange("(j p) n -> p j n", p=128)[:, :, off:off + 512]

    def after_xo(ti):
        if ti == 0:
            S.collective("AllGather", [xo_pre], [of_pre], GROUPS, reads=["xo0"], writes=["ofpre"])
        elif (ti - 1) % 4 == 3:
            k = (ti - 1) // 4
            S.collective("AllGather", [xo_blk[k]], [of_blk[k]], GROUPS,
                         reads=["xo%d" % (4 * k + 1 + j) for j in range(4)], writes=["ofblk%d" % k])

    def of_tiles(col0, N, cc):
        if col0 == 0 and cc == 0:
            src, off = of_pre, 0
        else:
            q = cc * nown + col0 - PRE
            src, off = of_blk[q // XB], q % XB
        v = src.rearrange("(h j p) n -> j p h n", h=4, p=128)
        return [(j * 4, j * 4 + 4, v[j][:, :, off:off + N]) for j in range(2)]

    def run_phase(fn):
        with contextlib.ExitStack() as es:
            evs = fn(es)
            with nc.Block() as block:
                S.emit(block)
        return evs

    def dump(src, shape, dt):
        dbg = _dram(nc, "dbg_out", shape, dt, "ExternalOutput")
        ev = S.dma("sp", dbg, src, writes=["dbg"])
        with nc.Block() as block:
            S.finish("sp", [ev])
            S.emit(block)
        S.close()
        return nc

    emit_m_conversions(0)
    run_phase(lambda es: build_tphase(nc, S, es, nown, dict(src_h=xT, gains=g0, xu_dst=xu_dst, after_xu=after_xu), None, "xu"))
    S.barrier()
    if stop == "ag1":
        return dump(uf_blk[0], [4 * D, 512], BF16)
    final_evs = None
    for e in range(2):
        Tm = dict(mw[e]); Tm.update(mw_bf[e]); Tm["weights_bf16"] = True
        Tm.update(tabs); Tm["uf_tile"] = uf_tile; Tm["xo_dst"] = xo_dst; Tm["after_xo"] = after_xo
        Tm["after_loads"] = lambda e=e: emit_t_conversions(S, tw[e])
        run_phase(lambda es, e=e, Tm=Tm: build_mixer(nc, S, es, seq, Tm, e))
        S.barrier()
        if stop == "ag2":
            return dump(of_blk[0], [4 * 256, XB], BF16)
        Tt = dict(tw[e]); Tt["of_tiles"] = of_tiles
        Tt["src_h"] = xT if e == 0 else hT
        if e == 0:
            Tt["dst_h"] = hT; Tt["xu_dst"] = xu_dst; Tt["after_xu"] = after_xu
            Tt["after_loads"] = lambda: emit_m_conversions(1)
        else:
            Tt["out_h"] = out_h
        final_evs = run_phase(lambda es, e=e, Tt=Tt: build_tphase(nc, S, es, nown, Tt, (2 * e, 2 * e + 1), "xu" if e == 0 else "out", convert=False))
        S.barrier()
    with nc.Block() as block:
        S.finish("sp", final_evs)
        S.emit(block)
    S.close()
    return nc


def kernel(_stop=None, **inputs):
    inp = {k: np.asarray(v) for k, v in inputs.items()}
    x = inp["x"].astype(np.float32)
    B, seq, _ = x.shape
    nown = seq // 4
    ncol = PRE + nown
    LP = PRE + seq
    tabs = host_tables(seq)
    g0 = np.zeros((128, 5, 8), np.float32); g0[:, 4] = _pc(inp["mix_norm"][0])
    tws = [host_tphase_weights(inp, 0, 1, 2), host_tphase_weights(inp, 2, 3, None)]
    in_maps = []
    for r in range(8):
        b, c = divmod(r, 4)
        P = np.zeros((LP, D), np.float32)
        P[48:64] = inp["meta_tokens"]
        P[64:] = x[b]
        m = dict(xT=np.ascontiguousarray(P[c * nown:c * nown + ncol].T), gains_p0=g0, invcnt=host_invcnt(c))
        selv = np.zeros((128, 4), np.float32); selv[:, c] = 1.0
        m["sel"] = selv
        m.update(tabs)
        for e in range(2):
            for k, v in host_mixer_weights(inp, e, c).items():
                m["%s_e%d" % (k, e)] = v
        for t in range(2):
            for k, v in tws[t].items():
                m["%s_t%d" % (k, t)] = v
        in_maps.append(m)
    if _stop is not None:
        ncd = build_fused(seq, _stop)
        used = set(t.name for t in ncd.m.functions[0].allocations if hasattr(t, "name")) if False else None
        return run_bass_kernel_spmd(ncd, in_maps, core_ids=list(range(8))).results
    res = run_bass_kernel_spmd(build_fused(seq), in_maps, core_ids=list(range(8))).results
    out = np.zeros((B, seq, D), np.float32)
    for r in range(8):
        b, c = divmod(r, 4)
        out[b, c * nown:(c + 1) * nown] = res[r]["out_h"][:, PRE:].T
    return out
```

```python
import numpy as np
import ml_dtypes
import concourse.bass as bass
import concourse.mybir as mybir
from concourse.bass_utils import run_bass_kernel_spmd

F32 = mybir.dt.float32
BF16 = mybir.dt.bfloat16
ALU = mybir.AluOpType
AF = mybir.ActivationFunctionType

ENGINES = ("pe", "act", "dve", "pool", "sp")


import threading


class Stepper:
    def __init__(self, fn):
        self.fn = fn
        self.go = threading.Semaphore(0)
        self.back = threading.Semaphore(0)
        self.done = False
        self.started = False
        self.budget = 0
        self.exc = None
        self.th = threading.Thread(target=self._run, daemon=True)

    def _run(self):
        self.go.acquire()
        try:
            self.fn()
        except BaseException as e:
            self.exc = e
        self.done = True
        self.back.release()

    def pause_point(self):
        self.budget -= 1
        if self.budget <= 0:
            self.back.release()
            self.go.acquire()

    def step(self, n):
        if self.done:
            return
        self.budget = n
        if not self.started:
            self.started = True
            self.th.start()
        self.go.release()
        self.back.acquire()
        if self.exc is not None:
            raise self.exc

    def finish(self):
        while not self.done:
            self.step(10 ** 9)


class Sched:
    NDMA = 8

    def __init__(self, nc):
        self.nc = nc
        self.ops = {e: [] for e in ENGINES}
        self.cnt = {e: 0 for e in ENGINES}
        self.sem = {}
        self.dsem = {}
        self.dcnt = {"sp": 0, "pool": 0, "act": 0}
        self.last_w = {}
        self.readers = {}
        self.waited = {e: {} for e in ENGINES}
        self._stack = []
        self.stepper = None

    def _pause(self):
        st = self.stepper
        if st is not None and threading.current_thread() is st.th:
            st.pause_point()

    def open(self):
        import contextlib
        self._es = contextlib.ExitStack()
        for e in ("pe", "act", "dve", "pool"):
            self.sem[e] = self._es.enter_context(self.nc.semaphore("s_" + e))
        for q in ("sp", "pool"):
            self.dsem[q] = [self._es.enter_context(self.nc.semaphore("d_%s%d" % (q, i)))
                            for i in range(self.NDMA)]
        return self

    def close(self):
        self._es.close()

    def barrier(self):
        evs = []
        for e in ("pe", "act", "dve", "pool"):
            if self.cnt[e]:
                evs.append((self.sem[e], self.cnt[e]))
        n = self.cnt.get("cc", 0)
        for k in range(min(n, self.NCC)):
            last_i = n - 1 - ((n - 1 - k) % self.NCC)
            evs.append((self.ccsem[k], last_i // self.NCC + 1))
        for q in ("sp", "pool"):
            n = self.dcnt[q]
            for k in range(min(n, self.NDMA)):
                last_i = n - 1 - ((n - 1 - k) % self.NDMA)
                evs.append((self.dsem[q][k], 16 * (last_i // self.NDMA + 1)))
        for e in ENGINES:
            waits = self._waits(e, evs)
            if waits:
                self.ops[e].append((waits, None, None, 0))
        self.last_w = {}
        self.readers = {}

    def _deps(self, reads, writes):
        evs = []
        for r in reads:
            if r in self.last_w:
                evs.append(self.last_w[r])
        for w in writes:
            if w in self.last_w:
                evs.append(self.last_w[w])
            evs.extend(self.readers.get(w, ()))
        return evs

    def _commit(self, ev, reads, writes):
        for r in reads:
            self.readers.setdefault(r, []).append(ev)
        for w in writes:
            self.last_w[w] = ev
            self.readers[w] = []

    def _waits(self, eng, evs):
        out = []
        wd = self.waited[eng]
        best = {}
        for (s, v) in evs:
            key = id(s)
            if wd.get(key, 0) >= v:
                continue
            if key not in best or best[key][1] < v:
                best[key] = (s, v)
        for key, (s, v) in best.items():
            wd[key] = v
            out.append((s, v))
        return out

    def op(self, eng, fn, reads=(), writes=()):
        evs = self._deps(reads, writes)
        if eng == "pe":
            evs = [ev for ev in evs if ev[0] is not self.sem["pe"]]
        waits = self._waits(eng, evs)
        self.cnt[eng] += 1
        ev = (self.sem[eng], self.cnt[eng])
        self.ops[eng].append((waits, fn, ev[0], 1))
        self._commit(ev, reads, writes)
        self._pause()
        return ev

    def dma(self, q, out, in_, reads=(), writes=(), **kw):
        i = self.dcnt[q]
        self.dcnt[q] += 1
        s = self.dsem[q][i % self.NDMA]
        prev = 16 * (i // self.NDMA)
        evs = self._deps(reads, writes)
        if prev:
            evs.append((s, prev))
        waits = self._waits(q, evs)
        ev = (s, prev + 16)

        def fn(e, out=out, in_=in_, kw=kw):
            return e.dma_start(out=out, in_=in_, **kw)
        self.ops[q].append((waits, fn, s, 16))
        self._commit(ev, reads, writes)
        self._pause()
        return ev

    NCC = 4

    def collective(self, kind, ins, outs, groups, reads=(), writes=()):
        if "cc" not in self.cnt:
            self.ccsem = [self._es.enter_context(self.nc.semaphore("s_cc%d" % i)) for i in range(self.NCC)]
            self.cnt["cc"] = 0
        i = self.cnt["cc"]
        self.cnt["cc"] += 1
        s_ = self.ccsem[i % self.NCC]
        prev = i // self.NCC
        evs = self._deps(reads, writes)
        if prev:
            evs.append((s_, prev))
        waits = self._waits("pool", evs)
        ev = (s_, prev + 1)

        def fn(e):
            return e.collective_compute(kind, ALU.bypass, replica_groups=groups, ins=ins, outs=outs)
        self.ops["pool"].append((waits, fn, s_, None))
        self._commit(ev, reads, writes)
        return ev

    def finish(self, eng, evs):
        waits = self._waits(eng, evs)
        self.ops[eng].append((waits, None, None, 0))

    def emit(self, block):
        nc = self.nc

        def run(eng_name):
            def body(e):
                for (waits, fn, s, inc) in self.ops[eng_name]:
                    for (ws, wv) in waits:
                        e.wait_ge(ws, wv)
                    if fn is not None:
                        ins = fn(e)
                        if inc is None:
                            ins.then_inc(s)
                        else:
                            ins.then_inc(s, inc)
                self.ops[eng_name] = []
            return body
        block.tensor(run("pe"))
        block.scalar(run("act"))
        block.vector(run("dve"))
        block.gpsimd(run("pool"))
        block.sync(run("sp"))


D = 1024
DC = 8
DFF = 4096
EPS = 1e-6
PRE = 64
SEQ_FULL = 16384
ATT_SCALE = 192.0 ** -0.5
INTERLEAVE = False
STEP_EVERY = 1
HOLD_MLA = True
NEG = -30000.0


_UID = [0]


def _uid():
    _UID[0] += 1
    return "u%d_" % _UID[0]


class Rot:
    def __init__(self, names):
        self.names = list(names)
        self.i = 0

    def next(self):
        n = self.names[self.i % len(self.names)]
        self.i += 1
        return n


class WStream:
    def __init__(self, S, name, slots, items, look=1):
        self.S, self.name, self.slots, self.items, self.look = S, name, slots, items, look
        self.emitted = 0

    def _emit_upto(self, i):
        while self.emitted <= min(i, len(self.items) - 1):
            k = self.emitted
            slot = self.slots[k % len(self.slots)]
            self.S.dma("sp", slot[:], self.items[k][0], reads=[self.items[k][1]],
                       writes=["%s%d" % (self.name, k % len(self.slots))])
            self.emitted += 1

    def get(self, i):
        self._emit_upto(i + self.look)
        k = i % len(self.slots)
        return self.slots[k], "%s%d" % (self.name, k)


def _mm_group(mms):
    def fn(e):
        ins = None
        n = len(mms)
        for i, (o, l, r) in enumerate(mms):
            ins = e.matmul(o, l, r, start=(i == 0), stop=(i == n - 1))
        return ins
    return fn


class Ctx:
    pass


def t_tiles(nown):
    return [(0, PRE)] + [(PRE + 512 * i, 512) for i in range(nown // 512)]


def t_conversion_items(T):
    items = []
    if T.get("wout_bf") is not None:
        items.append((T["wout_bf"], T["wout_f32"], "woutbf"))
        items.append((T["poolw_bf"], T["poolw_f32"], "poolwbf"))
    for li in range(2):
        for g in range(8):
            items.append((T["wup_bf"][li, g], T["wup_f32"][li, g], "wupbf%d_%d" % (li, g)))
            items.append((T["wdn_bf"][li, g], T["wdn_f32"][li, g], "wdnbf%d_%d" % (li, g)))
    return items


def emit_t_conversions(S, T):
    for li in range(2):
        for g in range(8):
            S.dma("pool", T["wup_bf"][li, g], T["wup_f32"][li, g], writes=["wupbf%d_%d" % (li, g)])
            S.dma("pool", T["wdn_bf"][li, g], T["wdn_f32"][li, g], writes=["wdnbf%d_%d" % (li, g)])


def build_tphase(nc, S, es, nown, T, layers, tail, convert=True):
    ncol = PRE + nown
    uid = _uid()
    sb = lambda name, shape, dt: es.enter_context(nc.sbuf_tensor(uid + name, shape, dt))
    pbanks = [es.enter_context(nc.psum_tensor(uid + "pb%d" % i, [128, 512], F32)) for i in range(8)]
    prot = Rot(range(8))

    ones = sb("ones", [128, 128], BF16)
    S.op("pool", lambda e: e.memset(ones[:], 1.0), writes=["ones"])
    gains = sb("gains", [128, 5, DC], F32)
    S.dma("sp", gains[:], T["gains"], writes=["gains"])

    h_t = [sb("h_t%d" % i, [128, DC, 512], F32) for i in range(2)]
    u_bf = sb("u_bf", [128, DC, 512], BF16)
    sq_bf = sb("sq_bf", [128, DC, 512], BF16)
    rt = sb("rt", [128, 512], F32)
    rstd = sb("rstd", [128, 512], F32)

    if layers is not None:
        o_t = [sb("o_t%d" % i, [128, DC, 512], BF16) for i in range(2)]
        wout = sb("wout", [128, 8 * 8 * 128], BF16)
        poolw = sb("poolw", [128, 4 * 2 * 2 * 128], BF16)
        if T.get("wout_bf") is not None:
            S.dma("sp", wout[:], T["wout_bf"], writes=["wout"])
            S.dma("sp", poolw[:], T["poolw_bf"], writes=["poolw"])
        else:
            S.dma("pool", wout[:], T["wout_f32"], writes=["wout"])
            S.dma("pool", poolw[:], T["poolw_f32"], writes=["poolw"])
        if convert:
            emit_t_conversions(S, T)
        if T.get("after_loads") is not None:
            T["after_loads"]()
        if T.get("sel") is not None:
            selv = sb("selv", [128, 4], F32)
            S.dma("sp", selv[:], T["sel"], writes=["selv"])
            cand = [sb("cand%d" % i, [128, DC, 512], BF16) for i in range(2)]
            candrot = Rot(range(2))
        invcnt = sb("invcnt", [128, 4, PRE], F32)
        S.dma("sp", invcnt[:], T["invcnt"], writes=["invcnt"])
        a_bf = sb("a_bf", [128, 32, 512], BF16)
        rl = [sb("rl%d" % i, [128, 512], F32) for i in range(2)]
        rlrot = Rot(range(2))
        u32 = sb("u32", [128, DC, 16 + 512], F32)
        wt = [sb("wt%d" % i, [128, 2, 16 + 512], F32) for i in range(2)]
        d_bf = sq_bf
        S.op("pool", lambda e: e.memset(u32[:, :, 0:16], 0.0), writes=["u32halo"])
        wup_slots = [sb("wup%d" % i, [128, 8 * 512], BF16) for i in range(2)]
        wdn_slots = [sb("wdn%d" % i, [128, 32 * 128], BF16) for i in range(2)]
        tiles = t_tiles(nown)
        up_items, dn_items = [], []
        for _ in tiles:
            for li in range(2):
                for g in range(8):
                    up_items.append((T["wup_bf"][li, g], "wupbf%d_%d" % (li, g)))
                for k in range(8):
                    dn_items.append((T["wdn_bf"][li, k], "wdnbf%d_%d" % (li, k)))
        wup = WStream(S, "wup", wup_slots, up_items, look=1)
        wdn = WStream(S, "wdn", wdn_slots, dn_items, look=1)
        cnt = {"up": 0, "dn": 0}

    def rms(h, hname, N, gain_idx, out_bf=None, out32=None):
        S.op("act", lambda e: e.activation(out=sq_bf[:, :, :N], in_=h[:, :, :N], func=AF.Square),
             reads=[hname], writes=["sq_bf"])
        pb = prot.next()
        ps = pbanks[pb]
        S.op("pe", _mm_group([(ps[:, :N], ones[:], sq_bf[:, c, :N]) for c in range(DC)]),
             reads=["sq_bf", "ones"], writes=["pb%d" % pb])
        S.op("act", lambda e: e.activation(out=rt[:, :N], in_=ps[:, :N], func=AF.Ln, bias=EPS, scale=1.0 / D),
             reads=["pb%d" % pb], writes=["rt"])
        S.op("act", lambda e: e.activation(out=rstd[:, :N], in_=rt[:, :N], func=AF.Exp, scale=-0.5), reads=["rt"], writes=["rstd"])
        if out_bf is not None:
            for c in range(DC):
                S.op("dve", lambda e, c=c: e.scalar_tensor_tensor(
                    out=out_bf[:, c, :N], in0=h[:, c, :N], scalar=gains[:, gain_idx, c:c + 1], in1=rstd[:, :N],
                    op0=ALU.mult, op1=ALU.mult), reads=[hname, "gains", "rstd"], writes=["u_bf%d" % c])
        if out32 is not None:
            for c in range(DC):
                S.op("dve", lambda e, c=c: e.scalar_tensor_tensor(
                    out=out32[:, c, 16:16 + N], in0=h[:, c, :N], scalar=gains[:, gain_idx, c:c + 1], in1=rstd[:, :N],
                    op0=ALU.mult, op1=ALU.mult), reads=[hname, "gains", "rstd"], writes=["u32_%d" % c])

    def mlp(h, hname, N, li, hook_up=None, hook_dn=None):
        rms(h, hname, N, li, out_bf=u_bf)
        if hook_up is not None:
            hook_up()
        for g in range(8):
            slot, sname = wup.get(cnt["up"]); cnt["up"] += 1
            for j in range(4):
                pb = prot.next(); ps = pbanks[pb]
                S.op("pe", _mm_group([(ps[:, :N], slot[:, c * 512 + j * 128:c * 512 + (j + 1) * 128], u_bf[:, c, :N])
                                      for c in range(DC)]), reads=["u_bf%d" % c for c in range(DC)] + [sname], writes=["pb%d" % pb])
                r = rlrot.next()
                S.op("act", lambda e, ps=ps, r=r: e.activation(out=rl[r][:, :N], in_=ps[:, :N], func=AF.Relu),
                     reads=["pb%d" % pb], writes=["rl%d" % r])
                f = g * 4 + j
                S.op("pool", lambda e, r=r, f=f: e.tensor_tensor(out=a_bf[:, f, :N], in0=rl[r][:, :N], in1=rl[r][:, :N], op=ALU.mult),
                     reads=["rl%d" % r], writes=["a_bf%d" % f])
        if hook_dn is not None:
            hook_dn()
        for k in range(8):
            slot, sname = wdn.get(cnt["dn"]); cnt["dn"] += 1
            pb = prot.next(); ps = pbanks[pb]
            S.op("pe", _mm_group([(ps[:, :N], slot[:, f * 128:(f + 1) * 128], a_bf[:, f, :N]) for f in range(32)]),
                 reads=["a_bf%d" % f for f in range(32)] + [sname], writes=["pb%d" % pb])
            S.op("dve", lambda e, ps=ps, k=k: e.tensor_tensor(out=h[:, k, :N], in0=h[:, k, :N], in1=ps[:, :N], op=ALU.add),
                 reads=["pb%d" % pb, hname], writes=[hname])

    tiles = t_tiles(nown)
    out_evs = []

    src_ap = T["src_h"].rearrange("(c p) n -> p c n", p=128)

    def load_tile(ti, part):
        if ti >= len(tiles):
            return
        col0, N = tiles[ti]
        hb = ti % 2
        if part == 0:
            S.dma("sp", h_t[hb][:, :, :N], src_ap[:, :, col0:col0 + N], writes=["h_t%d" % hb])
        if layers is None:
            return
        o = o_t[hb]; oname = "o_t%d" % hb
        for cc in (0, 1) if part == 0 else (2, 3):
            ci = candrot.next(); cd = cand[ci]
            for (c0, c1, oap) in T["of_tiles"](col0, N, cc):
                S.dma("sp", cd[:, c0:c1, :N], oap, writes=["cand%d" % ci])
            if cc == 0:
                S.op("dve", lambda e, cd=cd, cc=cc: e.tensor_scalar(
                    out=o[:, :, :N], in0=cd[:, :, :N], scalar1=selv[:, cc:cc + 1], scalar2=None, op0=ALU.mult),
                    reads=["cand%d" % ci, "selv"], writes=[oname])
            else:
                S.op("dve", lambda e, cd=cd, cc=cc: e.scalar_tensor_tensor(
                    out=o[:, :, :N], in0=cd[:, :, :N], scalar=selv[:, cc:cc + 1], in1=o[:, :, :N], op0=ALU.mult, op1=ALU.add),
                    reads=["cand%d" % ci, "selv", oname], writes=[oname])

    def do_tile(ti, col0, N):
        hb = ti % 2
        h = h_t[hb]; hname = "h_t%d" % hb
        if ti == 0:
            load_tile(0, 0)
            load_tile(0, 1)
        if layers is None:
            load_tile(ti + 1, 0)
        if layers is not None:
            o = o_t[hb]; oname = "o_t%d" % hb
            for k in range(8):
                pb = prot.next(); ps = pbanks[pb]
                S.op("pe", _mm_group([(ps[:, :N], wout[:, (k * 8 + c) * 128:(k * 8 + c + 1) * 128], o[:, c, :N]) for c in range(8)]),
                     reads=[oname, "wout"], writes=["pb%d" % pb])
                S.op("dve", lambda e, ps=ps, k=k: e.tensor_tensor(out=h[:, k, :N], in0=h[:, k, :N], in1=ps[:, :N], op=ALU.add),
                     reads=["pb%d" % pb, hname], writes=[hname])
            mlp(h, hname, N, 0, hook_up=lambda: load_tile(ti + 1, 0), hook_dn=lambda: load_tile(ti + 1, 1))
            rms(h, hname, N, 2, out32=u32)
            W = 16 + N
            for g in range(4):
                srcw = u32[:, 2 * g:2 * g + 2, :]
                sname = "u32g%d" % g
                sh = 1
                lo = 0
                for lvl in range(g + 1):
                    dst = wt[lvl % 2]
                    S.op("dve", lambda e, dst=dst, srcw=srcw, sh=sh, lo=lo: e.tensor_tensor(
                        out=dst[:, :, lo + sh:W], in0=srcw[:, :, lo + sh:W], in1=srcw[:, :, lo:W - sh], op=ALU.add),
                        reads=([sname] if sname.startswith("wt") else ["u32_%d" % (2 * g), "u32_%d" % (2 * g + 1)]) + ["u32halo"], writes=["wt%d" % (lvl % 2)])
                    srcw = dst
                    sname = "wt%d" % (lvl % 2)
                    lo += sh
                    sh *= 2
                win = srcw
                if ti == 0:
                    for cc in range(2):
                        S.op("dve", lambda e, win=win, g=g, cc=cc: e.tensor_tensor(
                            out=win[:, cc, 16:16 + N], in0=win[:, cc, 16:16 + N], in1=invcnt[:, g, :N], op=ALU.mult),
                            reads=[sname, "invcnt"], writes=[sname])
                    S.op("dve", lambda e, win=win, g=g: e.tensor_tensor(
                        out=d_bf[:, 2 * g:2 * g + 2, :N], in0=win[:, :, 16:16 + N], in1=u32[:, 2 * g:2 * g + 2, 16:16 + N],
                        op=ALU.subtract), reads=[sname] + ["u32_%d" % c for c in range(DC)], writes=["sq_bf"])
                else:
                    S.op("dve", lambda e, win=win, g=g: e.scalar_tensor_tensor(
                        out=d_bf[:, 2 * g:2 * g + 2, :N], in0=win[:, :, 16:16 + N], scalar=1.0 / (2 ** (g + 1)),
                        in1=u32[:, 2 * g:2 * g + 2, 16:16 + N], op0=ALU.mult, op1=ALU.subtract),
                        reads=[sname] + ["u32_%d" % c for c in range(DC)], writes=["sq_bf"])
            for ke in range(8):
                g = ke // 2
                pb = prot.next(); ps = pbanks[pb]
                S.op("pe", _mm_group([(ps[:, :N], poolw[:, ((g * 2 + ke % 2) * 2 + cc) * 128:((g * 2 + ke % 2) * 2 + cc + 1) * 128],
                                       d_bf[:, 2 * g + cc, :N]) for cc in range(2)]),
                     reads=["sq_bf", "poolw"], writes=["pb%d" % pb])
                S.op("dve", lambda e, ps=ps, ke=ke: e.scalar_tensor_tensor(
                    out=h[:, ke, :N], in0=ps[:, :N], scalar=gains[:, 3, ke:ke + 1], in1=h[:, ke, :N], op0=ALU.mult, op1=ALU.add),
                    reads=["pb%d" % pb, hname, "gains"], writes=[hname])
            S.op("pool", lambda e, N=N: e.tensor_copy(out=u32[:, :, 0:16], in_=u32[:, :, N:N + 16]),
                 reads=["u32_%d" % c for c in range(DC)], writes=["u32halo"])
            mlp(h, hname, N, 1)
        if tail == "out":
            dst = T["out_h"].rearrange("(c p) n -> p c n", p=128)
            out_evs.append(S.dma("pool", dst[:, :, col0:col0 + N], h[:, :, :N], reads=[hname], writes=["out_h"]))
        else:
            if layers is not None:
                dst = T["dst_h"].rearrange("(c p) n -> p c n", p=128)
                out_evs.append(S.dma("pool", dst[:, :, col0:col0 + N], h[:, :, :N], reads=[hname], writes=["dst_h"]))
            rms(h, hname, N, 4, out_bf=u_bf)
            out_evs.append(S.dma("pool", T["xu_dst"](ti, N), u_bf[:, :, :N], reads=["u_bf%d" % c for c in range(DC)], writes=["xu%d" % ti]))
            T["after_xu"](ti)

    for ti, (col0, N) in enumerate(tiles):
        do_tile(ti, col0, N)
    return out_evs


def _mm_list(mms):
    def fn(e):
        ins = None
        for (o, l, r, st, sp) in mms:
            ins = e.matmul(o, l, r, start=st, stop=sp)
        return ins
    return fn


def m_tiles(seq):
    return [(0, PRE)] + [(PRE + 512 * i, 512) for i in range(seq // 512)]


def build_mixer(nc, S, es, seq, T, e_idx):
    LP = PRE + seq
    NKT = 1 + seq // 128
    uid = _uid()
    sb = lambda name, shape, dt: es.enter_context(nc.sbuf_tensor(uid + name, shape, dt))
    pbanks = [es.enter_context(nc.psum_tensor(uid + "mb%d" % i, [128, 512], F32)) for i in range(2)]
    prot = Rot(range(2))
    sbanks = [es.enter_context(nc.psum_tensor(uid + "sb%d" % i, [128, 512], F32)) for i in range(2)]
    srot = Rot(range(2))
    hO_t = es.enter_context(nc.psum_tensor(uid + "hO", [128, 512], F32))
    pO = es.enter_context(nc.psum_tensor(uid + "pO", [128, 512], F32))
    pZ = es.enter_context(nc.psum_tensor(uid + "pZ", [128, 512], F32))
    pT = es.enter_context(nc.psum_tensor(uid + "pT", [128, 1024], BF16))

    ones = sb("ones", [128, 128], BF16)
    S.op("pool", lambda e: e.memset(ones[:], 1.0), writes=["ones"])
    ident = sb("ident", [128, 128], BF16)
    S.dma("pool", ident[:], T["ident"], writes=["ident"])
    wqueue = "sp" if T.get("weights_bf16") else "pool"
    win_fm = sb("win_fm", [128, DC, 1024], BF16)
    S.dma(wqueue, win_fm[:], T["win_fm"], writes=["win_fm"])
    win_hi = sb("win_hi", [128, DC, 128], BF16)
    S.dma(wqueue, win_hi[:], T["win_hi"], writes=["win_hi"])
    wq = sb("wq", [128, 2, 256], BF16)
    S.dma(wqueue, wq[:], T["wq"], writes=["wq"])
    wkv = sb("wkv", [128, 2, 256], BF16)
    S.dma(wqueue, wkv[:], T["wkv"], writes=["wkv"])
    mvec = sb("mvec", [128, 16], F32)
    S.dma("sp", mvec[:], T["mvec"], writes=["mvec"])
    masktab = sb("masktab", [128, 896], F32)
    S.dma("sp", masktab[:], T["masktab"], writes=["masktab"])
    triu = sb("triu", [64, 64], F32)
    S.dma("sp", triu[:], T["triu"], writes=["triu"])
    cmask = sb("cmask", [128, 512], F32)
    S.dma("sp", cmask[:], T["cmask"], writes=["cmask"])
    if T.get("after_loads") is not None:
        T["after_loads"]()
    S.op("dve", lambda e: e.tensor_scalar(out=win_fm[:, :, 960:992], in0=win_fm[:, :, 960:992], scalar1=-1.0, scalar2=None, op0=ALU.mult),
         reads=["win_fm"], writes=["win_fm"])
    S.op("dve", lambda e: e.tensor_scalar(out=wq[:, :, 192:224], in0=wq[:, :, 192:224], scalar1=-1.0, scalar2=None, op0=ALU.mult),
         reads=["wq"], writes=["wq"])
    lbv = sb("lbv", [128, 4], F32)
    if e_idx == 0:
        S.op("dve", lambda e: e.memset(lbv[:, 0:1], 0.0), writes=["lbv"])
        S.op("dve", lambda e: e.memset(lbv[:, 1:2], 1.0), reads=["lbv"], writes=["lbv"])
    else:
        ex = sb("lbex", [128, 2], F32)
        S.op("act", lambda e: e.activation(out=ex[:], in_=mvec[:, 11:13], func=AF.Exp), reads=["mvec"], writes=["lbex"])
        S.op("dve", lambda e: e.tensor_tensor(out=lbv[:, 2:3], in0=ex[:, 0:1], in1=ex[:, 1:2], op=ALU.add), reads=["lbex"], writes=["lbv2"])
        S.op("dve", lambda e: e.reciprocal(out=lbv[:, 3:4], in_=lbv[:, 2:3]), reads=["lbv2"], writes=["lbv3"])
        S.op("dve", lambda e: e.tensor_tensor(out=lbv[:, 0:1], in0=ex[:, 1:2], in1=lbv[:, 3:4], op=ALU.mult), reads=["lbex", "lbv3"], writes=["lbv"])
        S.op("dve", lambda e: e.tensor_tensor(out=lbv[:, 1:2], in0=ex[:, 0:1], in1=lbv[:, 3:4], op=ALU.mult), reads=["lbex", "lbv3", "lbv"], writes=["lbv"])

    Kn = sb("Kn", [128, LP], BF16)
    Kr = sb("Kr", [64, LP], BF16)
    Vs = sb("Vs", [128, NKT * 128], BF16)

    ut = sb("ut", [128, DC, 512], BF16)
    cs = sb("cs", [64, 2, 512], F32)
    wk = {}

    def W(name, shape=(128, 512), dt=F32):
        if name not in wk:
            wk[name] = sb("w_" + name, list(shape), dt)
        return wk[name]

    S32 = sb("S32", [128, 128], F32)
    S_bf2 = [sb("S_bf%d" % i, [128, 128], BF16) for i in range(2)]
    gch = [0]
    S.op("pool", lambda e: e.memset(S32[:], 0.0), writes=["S32"])
    S.op("pool", lambda e: e.memset(S_bf2[0][:], 0.0), writes=["S_bf0"])
    Pt = [sb("Pt%d" % i, [128, 512], BF16) for i in range(3)]
    prot_p = Rot(range(3))
    xo = [sb("xo%d" % i, [128, 2, 512], BF16) for i in range(2)]

    def proj(col0, M, N):
        pb = prot.next(); ps = pbanks[pb]
        S.op("pe", _mm_group([(ps[:M, :N], win_fm[:, c, col0:col0 + M], ut[:, c, :N]) for c in range(DC)]),
             reads=["ut", "win_fm"], writes=["mb%d" % pb])
        return pb, ps

    def rstd_from(ps, pbname, N, dim, outname):
        rt = W("rt")
        rs = W(outname)
        S.op("act", lambda e: e.activation(out=rt[:, :N], in_=ps[:, :N], func=AF.Ln, bias=EPS, scale=1.0 / dim),
             reads=[pbname], writes=["rt"])
        S.op("act", lambda e: e.activation(out=rs[:, :N], in_=rt[:, :N], func=AF.Exp, scale=-0.5), reads=["rt"], writes=[outname])
        return rs

    tiles = m_tiles(seq)
    out_evs = []

    handoff = {}

    def prologue(ti, pos0, N):
        nch = N // 64
        xob = xo[ti % 2]; xoname = "xo%d" % (ti % 2)
        S.dma("sp", ut[:, :, :N], T["uf_tile"](ti, pos0, N), writes=["ut"])
        S.dma("sp", cs[:, 0, :N], T["cosT"][:, pos0:pos0 + N], writes=["cs0"])
        S.dma("sp", cs[:, 1, :N], T["sinT"][:, pos0:pos0 + N], writes=["cs1"])
        q32, sg, gate, logf, kk, bb, eb, enb = (W(n) for n in ("q32", "sg", "gate", "logf", "kk", "bb", "eb", "enb"))
        qe, ke, kd = (W(n, dt=BF16) for n in ("qe", "ke", "kd"))
        pb, ps = proj(0, 128, N)
        S.op("act", lambda e, ps=ps: e.activation(out=q32[:, :N], in_=ps[:, :N], func=AF.Silu), reads=["mb%d" % pb], writes=["q32"])
        pb, ps = proj(256, 128, N)
        S.op("act", lambda e, ps=ps: e.activation(out=gate[:, :N], in_=ps[:, :N], func=AF.Silu), reads=["mb%d" % pb], writes=["gate"])
        pb, ps = proj(128, 128, N)
        S.op("act", lambda e, ps=ps: e.activation(out=sg[:, :N], in_=ps[:, :N], func=AF.Sigmoid), reads=["mb%d" % pb], writes=["sg"])
        S.op("dve", lambda e: e.tensor_scalar(out=sg[:, :N], in0=sg[:, :N], scalar1=lbv[:, 1:2], scalar2=lbv[:, 0:1], op0=ALU.mult, op1=ALU.add),
             reads=["sg", "lbv"], writes=["sg"])
        S.op("act", lambda e: e.activation(out=logf[:, :N], in_=sg[:, :N], func=AF.Ln), reads=["sg"], writes=["logf"])
        S.op("dve", lambda e: e.tensor_scalar(out=kk[:, :N], in0=sg[:, :N], scalar1=-1.0, scalar2=1.0, op0=ALU.mult, op1=ALU.add),
             reads=["sg"], writes=["kk"])
        if ti == 0:
            S.op("dve", lambda e: e.memset(logf[:, 0:48], 0.0), reads=["logf"], writes=["logf"])
        S.op("dve", lambda e: e.tensor_tensor_scan(out=bb[:, :N], data0=cmask[:, :N], data1=logf[:, :N], initial=0.0, op0=ALU.mult, op1=ALU.add),
             reads=["cmask", "logf"], writes=["bb"])
        S.op("act", lambda e: e.activation(out=eb[:, :N], in_=bb[:, :N], func=AF.Exp), reads=["bb"], writes=["eb"])
        S.op("act", lambda e: e.activation(out=enb[:, :N], in_=bb[:, :N], func=AF.Exp, scale=-1.0), reads=["bb"], writes=["enb"])
        S.op("dve", lambda e: e.tensor_tensor(out=qe[:, :N], in0=q32[:, :N], in1=eb[:, :N], op=ALU.mult), reads=["q32", "eb"], writes=["qe"])
        S.op("dve", lambda e: e.tensor_tensor(out=ke[:, :N], in0=kk[:, :N], in1=enb[:, :N], op=ALU.mult), reads=["kk", "enb"], writes=["ke"])
        for c in range(nch):
            S.op("dve", lambda e, c=c: e.scalar_tensor_tensor(
                out=kd[:, c * 64:(c + 1) * 64], in0=kk[:, c * 64:(c + 1) * 64], scalar=eb[:, c * 64 + 63:c * 64 + 64],
                in1=enb[:, c * 64:(c + 1) * 64], op0=ALU.mult, op1=ALU.mult), reads=["kk", "eb", "enb"], writes=["kd"])

        def tr_fn(e, nch=nch):
            ins = None
            for c in range(nch):
                ins = e.transpose(pT[:64, c * 128:(c + 1) * 128], kd[:, c * 64:(c + 1) * 64], ident[:])
            return ins
        S.op("pe", tr_fn, reads=["kd", "ident"], writes=["pT"])
        kdT = W("kdT", (64, 1024), BF16)
        S.op("act", lambda e, nch=nch: e.activation(out=kdT[:, :nch * 128], in_=pT[:64, :nch * 128], func=AF.Copy), reads=["pT"], writes=["kdT"])
        Vc = W("Vc", (64, 1024), BF16)
        for j0 in range(0, nch, 4):
            pb = prot.next(); ps = pbanks[pb]
            n4 = min(4, nch - j0)

            def vc_fn(e, ps=ps, j0=j0, n4=n4):
                ins = None
                for jj in range(n4):
                    cj = j0 + jj
                    for c in range(DC):
                        ins = e.matmul(ps[:64, jj * 128:(jj + 1) * 128], ut[:, c, cj * 64:(cj + 1) * 64], win_hi[:, c, :],
                                       start=(c == 0), stop=(c == DC - 1))
                return ins
            S.op("pe", vc_fn, reads=["ut", "win_hi"], writes=["mb%d" % pb])
            S.op("act", lambda e, ps=ps, j0=j0, n4=n4: e.activation(out=Vc[:, j0 * 128:(j0 + n4) * 128], in_=ps[:64, :n4 * 128], func=AF.Copy),
                 reads=["mb%d" % pb], writes=["Vc"])
        hO = hO_t
        ATm = W("ATm", (64, 512), BF16)
        U32 = W("U32", (128, 1024))
        for c in range(nch):
            cs_ = slice(c * 64, (c + 1) * 64)
            pb = prot.next(); ps = pbanks[pb]
            S.op("pe", _mm_group([(ps[:64, :64], ke[:, cs_], qe[:, cs_])]), reads=["ke", "qe"], writes=["mb%d" % pb])
            S.op("dve", lambda e, ps=ps, cs_=cs_: e.tensor_tensor(out=ATm[:, cs_], in0=ps[:64, :64], in1=triu[:], op=ALU.mult),
                 reads=["mb%d" % pb, "triu"], writes=["ATm%d" % c])
        for c in range(nch):
            pb2 = prot.next(); ps2 = pbanks[pb2]
            S.op("pe", _mm_group([(ps2[:, :128], kdT[:, c * 128:(c + 1) * 128], Vc[:, c * 128:(c + 1) * 128])]),
                 reads=["kdT", "Vc"], writes=["mb%d" % pb2])
            S.op("act", lambda e, ps2=ps2, c=c: e.activation(out=U32[:, c * 128:(c + 1) * 128], in_=ps2[:, :128], func=AF.Copy),
                 reads=["mb%d" % pb2], writes=["U32_%d" % c])
        for c in range(nch):
            cs_ = slice(c * 64, (c + 1) * 64)
            g = gch[0]; gch[0] += 1
            Sin = S_bf2[g % 2]; Sout = S_bf2[(g + 1) % 2]
            S.op("pe", _mm_group([(hO[:, cs_], Vc[:, c * 128:(c + 1) * 128], ATm[:, cs_]),
                                  (hO[:, cs_], Sin[:], qe[:, cs_])]),
                 reads=["Vc", "ATm%d" % c, "S_bf%d" % (g % 2), "qe"], writes=["hO"])
            S.op("dve", lambda e, c=c, Sout=Sout: e.scalar_tensor_tensor(
                out=Sout[:], in0=S32[:], scalar=eb[:, c * 64 + 63:c * 64 + 64], in1=U32[:, c * 128:(c + 1) * 128], op0=ALU.mult, op1=ALU.add),
                reads=["S32", "eb", "U32_%d" % c], writes=["S_bf%d" % ((g + 1) % 2)])
            S.op("dve", lambda e, c=c: e.scalar_tensor_tensor(
                out=S32[:], in0=S32[:], scalar=eb[:, c * 64 + 63:c * 64 + 64], in1=U32[:, c * 128:(c + 1) * 128], op0=ALU.mult, op1=ALU.add),
                reads=["S32", "eb", "U32_%d" % c], writes=["S32"])
        hO_res = ["hO"]
        osq = W("osq", dt=BF16)
        S.op("act", lambda e: e.activation(out=osq[:, :N], in_=hO[:, :N], func=AF.Square), reads=hO_res, writes=["osq"])
        pb, ps = prot.next(), None
        ps = pbanks[pb]
        S.op("pe", _mm_group([(ps[:, :N], ones[:], osq[:, :N])]), reads=["osq", "ones"], writes=["mb%d" % pb])
        rs = rstd_from(ps, "mb%d" % pb, N, 128.0, "rs_o")
        t1 = W("t1")
        S.op("dve", lambda e, rs=rs: e.scalar_tensor_tensor(out=t1[:, :N], in0=hO[:, :N], scalar=mvec[:, 10:11], in1=rs[:, :N], op0=ALU.mult, op1=ALU.mult),
             reads=hO_res + ["mvec", "rs_o"], writes=["t1"])
        S.op("pool", lambda e: e.tensor_tensor(out=xob[:, 0, :N], in0=t1[:, :N], in1=gate[:, :N], op=ALU.mult),
             reads=["t1", "gate"], writes=[xoname + "a"])
        if HOLD_MLA and S.stepper is not None and threading.current_thread() is S.stepper.th:
            S.stepper.budget = 10 ** 9
        cq32 = W("cq32", (128, 2, 512)); csq = W("csq", (128, 2, 512), BF16)
        cqn = W("cqn", (128, 2, 512), BF16); ckvn = W("ckvn", (128, 2, 512), BF16)
        for (col0, gcol, dst, dname) in ((384, 0, cqn, "cqn"), (640, 2, ckvn, "ckvn")):
            for r in range(2):
                pb, ps = proj(col0 + 128 * r, 128, N)
                S.op("act", lambda e, ps=ps, r=r: e.activation(out=cq32[:, r, :N], in_=ps[:, :N], func=AF.Copy), reads=["mb%d" % pb], writes=["cq32_%d" % r])
                S.op("act", lambda e, ps=ps, r=r: e.activation(out=csq[:, r, :N], in_=ps[:, :N], func=AF.Square), reads=["mb%d" % pb], writes=["csq_%d" % r])
            pb = prot.next(); ps = pbanks[pb]
            S.op("pe", _mm_group([(ps[:, :N], ones[:], csq[:, r, :N]) for r in range(2)]), reads=["csq_0", "csq_1", "ones"], writes=["mb%d" % pb])
            rs = rstd_from(ps, "mb%d" % pb, N, 256.0, "rs_c")
            for r in range(2):
                S.op("dve", lambda e, r=r, dst=dst, gcol=gcol, rs=rs: e.scalar_tensor_tensor(
                    out=dst[:, r, :N], in0=cq32[:, r, :N], scalar=mvec[:, gcol + r:gcol + r + 1], in1=rs[:, :N], op0=ALU.mult, op1=ALU.mult),
                    reads=["cq32_%d" % r, "mvec", "rs_c"], writes=[dname])

        def qk_finish(jn, jr, jt, gbase, out_n, out_n_name, out_r, out_r_name):
            sqn = W("sqn", dt=BF16); sqr = W("sqr", (64, 512), BF16)
            n32 = W("q32"); r32 = W("r32", (64, 512)); t32 = W("t32", (64, 512))
            ps_n, pbn = jn()
            S.op("dve", lambda e: e.tensor_copy(out=n32[:, :N], in_=ps_n[:, :N]), reads=[pbn], writes=["q32"])
            S.op("act", lambda e: e.activation(out=sqn[:, :N], in_=ps_n[:, :N], func=AF.Square), reads=[pbn], writes=["sqn"])
            ps_r, pbr = jr()
            S.op("dve", lambda e: e.tensor_copy(out=r32[:, :N], in_=ps_r[:64, :N]), reads=[pbr], writes=["r32"])
            S.op("act", lambda e: e.activation(out=sqr[:, :N], in_=ps_r[:64, :N], func=AF.Square), reads=[pbr], writes=["sqr"])
            ps_t, pbt = jt()
            S.op("dve", lambda e: e.tensor_copy(out=t32[:, :N], in_=ps_t[:64, :N]), reads=[pbt], writes=["t32"])
            pb = prot.next(); ps = pbanks[pb]
            S.op("pe", _mm_group([(ps[:, :N], ones[:], sqn[:, :N]), (ps[:, :N], ones[:64, :], sqr[:, :N])]),
                 reads=["sqn", "sqr", "ones"], writes=["mb%d" % pb])
            rs = rstd_from(ps, "mb%d" % pb, N, 192.0, "rs_qk")
            S.op("dve", lambda e: e.scalar_tensor_tensor(out=out_n, in0=n32[:, :N], scalar=mvec[:, gbase:gbase + 1], in1=rs[:, :N],
                                                         op0=ALU.mult, op1=ALU.mult), reads=["q32", "mvec", "rs_qk"], writes=[out_n_name])
            ra = r32; rb = t32
            S.op("dve", lambda e: e.scalar_tensor_tensor(out=ra[:, :N], in0=r32[:, :N], scalar=mvec[:64, gbase + 1:gbase + 2], in1=cs[:, 0, :N],
                                                         op0=ALU.mult, op1=ALU.mult), reads=["r32", "mvec", "cs0"], writes=["r32"])
            S.op("dve", lambda e: e.scalar_tensor_tensor(out=rb[:, :N], in0=t32[:, :N], scalar=mvec[:64, gbase + 2:gbase + 3], in1=cs[:, 1, :N],
                                                         op0=ALU.mult, op1=ALU.mult), reads=["t32", "mvec", "cs1"], writes=["t32"])
            S.op("pool", lambda e: e.tensor_tensor(out=ra[:, :N], in0=ra[:, :N], in1=rb[:, :N], op=ALU.add), reads=["r32", "t32"], writes=["r32"])
            S.op("dve", lambda e: e.tensor_tensor(out=out_r, in0=ra[:, :N], in1=rs[:64, :N], op=ALU.mult), reads=["r32", "rs_qk"], writes=[out_r_name])

        def job_w(wt, rdname, src, c0, c1, M):
            def f():
                pb = prot.next(); ps = pbanks[pb]
                S.op("pe", _mm_group([(ps[:M, :N], wt[:, r, c0:c1], src[:, r, :N]) for r in range(2)]), reads=[rdname, "wq", "wkv"], writes=["mb%d" % pb])
                return ps, "mb%d" % pb
            return f

        def job_p(c0, M):
            def f():
                pb, ps = proj(c0, M, N)
                return ps, "mb%d" % pb
            return f

        qnn = "qn%d" % (ti % 2); qrn = "qr%d" % (ti % 2)
        qn = W(qnn, dt=BF16); qr = W(qrn, (64, 512), BF16)
        handoff[ti] = (qn, qr, qnn, qrn)
        qk_finish(job_w(wq, "cqn", cqn, 0, 128, 128), job_w(wq, "cqn", cqn, 128, 192, 64), job_w(wq, "cqn", cqn, 192, 256, 64),
                  4, qn[:, :N], qnn, qr[:, :N], qrn)
        kvres = "KV_t%d" % ti
        qk_finish(job_w(wkv, "ckvn", ckvn, 0, 128, 128), job_p(896, 64), job_p(960, 64),
                  7, Kn[:, pos0:pos0 + N], kvres + "n", Kr[:, pos0:pos0 + N], kvres + "r")
        pb = prot.next(); ps = pbanks[pb]
        if ti == 0:
            subs = [(0, 64)]
            kt0 = 0
        else:
            subs = [(jj * 128, 128) for jj in range(4)]
            kt0 = 1 + 4 * (ti - 1)

        def v_fn(e, ps=ps, subs=subs):
            ins = None
            for jj, (c0, M) in enumerate(subs):
                for r in range(2):
                    ins = e.matmul(ps[:M, jj * 128:(jj + 1) * 128], ckvn[:, r, c0:c0 + M], wkv[:, r, 128:256], start=(r == 0), stop=(r == 1))
            return ins
        S.op("pe", v_fn, reads=["ckvn", "wkv"], writes=["mb%d" % pb])
        Mv = subs[0][1]
        S.op("act", lambda e, ps=ps, kt0=kt0, Mv=Mv, ns=len(subs): e.activation(
            out=Vs[:Mv, kt0 * 128:(kt0 + ns) * 128], in_=ps[:Mv, :ns * 128], func=AF.Copy), reads=["mb%d" % pb], writes=[kvres + "v"])
        if T.get("tile_hook") is not None:
            T["tile_hook"](ti, len(tiles))
    def attention(ti, pos0, N, stepper):
        xob = xo[ti % 2]; xoname = "xo%d" % (ti % 2)
        qn, qr, qnn, qrn = handoff.pop(ti)
        if ti == 0:
            kts = [(0, 64, "pad", 0)]
        else:
            kts = [(0, 64, "pad", 0)] + [(j, 128, None, 0) for j in range(1, 4 * (ti - 1) + 1)] + \
                  [(4 * (ti - 1) + 1 + jj, 128, "diag", jj) for jj in range(4)]
        nk = len(kts)
        pend = []

        def emit_S(idx):
            j, M, kind, jj = kts[idx]
            kc0 = 0 if j == 0 else PRE + 128 * (j - 1)
            tk = 0 if j == 0 else 1 + (j - 1) // 4
            kres = "KV_t%d" % tk
            pb = srot.next(); ps = sbanks[pb]
            S.op("pe", _mm_group([(ps[:M, :N], Kn[:, kc0:kc0 + M], qn[:, :N]), (ps[:M, :N], Kr[:, kc0:kc0 + M], qr[:, :N])]),
                 reads=[kres + "n", kres + "r", qnn, qrn], writes=["sb%d" % pb])
            p = prot_p.next(); P = Pt[p]
            if kind == "pad":
                S.op("act", lambda e: e.activation(out=P[:M, :N], in_=ps[:M, :N], func=AF.Exp, bias=mvec[:M, 13:14], scale=ATT_SCALE),
                     reads=["sb%d" % pb, "mvec"], writes=["Pt%d" % p])
            elif kind == "diag":
                tmp = W("dtmp")
                off = (3 - jj) * 128
                S.op("dve", lambda e: e.tensor_tensor(out=tmp[:, :N], in0=ps[:, :N], in1=masktab[:, off:off + N], op=ALU.add),
                     reads=["sb%d" % pb, "masktab"], writes=["dtmp"])
                S.op("act", lambda e: e.activation(out=P[:, :N], in_=tmp[:, :N], func=AF.Exp, scale=ATT_SCALE), reads=["dtmp"], writes=["Pt%d" % p])
            else:
                S.op("act", lambda e: e.activation(out=P[:, :N], in_=ps[:, :N], func=AF.Exp, scale=ATT_SCALE), reads=["sb%d" % pb], writes=["Pt%d" % p])
            return (idx, j, M, p, kres)

        def emit_OZ(rec):
            idx, j, M, p, kres = rec
            P = Pt[p]
            S.op("pe", _mm_list([(pO[:, :N], Vs[:M, j * 128:(j + 1) * 128], P[:M, :N], idx == 0, idx == nk - 1),
                                 (pZ[:, :N], ones[:M, :], P[:M, :N], idx == 0, idx == nk - 1)]),
                 reads=[kres + "v", "Pt%d" % p, "ones"], writes=["pO", "pZ"])

        LA = 1
        per_iter = max(1, -(-200 // max(1, nk - 2)))
        for idx in range(nk):
            pend.append(emit_S(idx))
            if len(pend) > LA:
                emit_OZ(pend.pop(0))
            if stepper is not None and INTERLEAVE and idx % STEP_EVERY == STEP_EVERY - 1:
                stepper.step(per_iter * STEP_EVERY)
        while pend:
            emit_OZ(pend.pop(0))
        if stepper is not None:
            stepper.finish()
        rz = W("rz")
        S.op("dve", lambda e: e.reciprocal(out=rz[:, :N], in_=pZ[:, :N]), reads=["pZ"], writes=["rz"])
        S.op("dve", lambda e: e.tensor_tensor(out=xob[:, 1, :N], in0=pO[:, :N], in1=rz[:, :N], op=ALU.mult), reads=["pO", "rz"], writes=[xoname + "b"])
        if ti == 0:
            S.op("pool", lambda e: e.memset(xob[:, :, 0:48], 0.0), reads=[xoname + "a", xoname + "b"], writes=[xoname + "a", xoname + "b"])
        out_evs.append(S.dma("sp", T["xo_dst"](ti, N), xob[:, :, :N], reads=[xoname + "a", xoname + "b"], writes=["xo%d" % ti]))
        T["after_xo"](ti)

    for par in range(2):
        W("qn%d" % par, dt=BF16)
        W("qr%d" % par, (64, 512), BF16)
    prologue(0, *tiles[0])
    for ti, (pos0, N) in enumerate(tiles):
        st = None
        if ti + 1 < len(tiles):
            st = Stepper(lambda ti=ti: prologue(ti + 1, *tiles[ti + 1]))
        S.stepper = st
        attention(ti, pos0, N, st)
        S.stepper = None
    return out_evs


IN_HQ, IN_HF, IN_HI, IN_HG, IN_CQ, IN_CKV, IN_KR = 0, 512, 1024, 1536, 2048, 2304, 2560
BF = ml_dtypes.bfloat16


def _pc(v):
    v = np.asarray(v, np.float32)
    return np.ascontiguousarray(v.reshape(-1, 128).T)


def host_tphase_weights(inp, le, lo, lnext):
    e, o = le // 2, lo // 2
    w = {}
    w["wout_f32"] = np.ascontiguousarray(
        inp["w_out"][e].reshape(8, 128, 8, 128).transpose(1, 2, 0, 3).reshape(128, 8192))
    w["wup_f32"] = np.ascontiguousarray(np.stack([
        inp["w_mlp_up"][l].reshape(8, 128, 8, 512).transpose(2, 1, 0, 3).reshape(8, 128, 4096) for l in (le, lo)]))
    w["wdn_f32"] = np.ascontiguousarray(np.stack([
        inp["w_mlp_down"][l].reshape(32, 128, 8, 128).transpose(2, 1, 0, 3).reshape(8, 128, 4096) for l in (le, lo)]))
    w["poolw_f32"] = np.ascontiguousarray(
        inp["pool_w"][o].reshape(4, 2, 128, 2, 128).transpose(2, 0, 3, 1, 4).reshape(128, 2048))
    g = np.zeros((128, 5, 8), np.float32)
    g[:, 0] = _pc(inp["mlp_norm"][le]); g[:, 1] = _pc(inp["mlp_norm"][lo])
    g[:, 2] = _pc(inp["mix_norm"][lo]); g[:, 3] = _pc(inp["pool_scale"][o])
    if lnext is not None:
        g[:, 4] = _pc(inp["mix_norm"][lnext])
    w["gains"] = g
    return w


def host_invcnt(c):
    t = np.zeros((128, 4, PRE), np.float32)
    for g in range(4):
        wdw = 2 ** (g + 1)
        t[:, g, :] = 1.0 / wdw
        if c == 0:
            for col in range(48, PRE):
                pos = col - 48
                t[:, g, col] = 1.0 / min(pos + 1, wdw)
    return t


def host_mixer_weights(inp, e, hd):
    w_in = inp["w_in"][e]
    cols = np.concatenate([
        np.arange(IN_HQ + hd * 128, IN_HQ + (hd + 1) * 128), np.arange(IN_HF + hd * 128, IN_HF + (hd + 1) * 128),
        np.arange(IN_HG + hd * 128, IN_HG + (hd + 1) * 128), np.arange(IN_CQ, IN_CQ + 256), np.arange(IN_CKV, IN_CKV + 256),
        np.arange(IN_KR, IN_KR + 64), np.arange(IN_KR + 32, IN_KR + 64), np.arange(IN_KR, IN_KR + 32)])
    w = {}
    w["win_fm"] = np.ascontiguousarray(w_in[:, cols].reshape(8, 128, 1024).transpose(1, 0, 2))
    w["win_hi"] = np.ascontiguousarray(w_in[:, IN_HI + hd * 128:IN_HI + (hd + 1) * 128].reshape(8, 128, 128).transpose(1, 0, 2))
    qb = hd * 192
    qcols = np.concatenate([np.arange(qb, qb + 192), np.arange(qb + 160, qb + 192), np.arange(qb + 128, qb + 160)])
    w["wq"] = np.ascontiguousarray(inp["w_q_up"][e][:, qcols].reshape(2, 128, 256).transpose(1, 0, 2))
    w["wkv"] = np.ascontiguousarray(inp["w_kv_up"][e][:, hd * 256:(hd + 1) * 256].reshape(2, 128, 256).transpose(1, 0, 2))
    mv = np.zeros((128, 16), np.float32)
    mv[:, 0:2] = _pc(inp["mla_q_a_norm"][e]); mv[:, 2:4] = _pc(inp["mla_kv_a_norm"][e])
    for base, g in ((4, inp["q_norm"][e]), (7, inp["k_norm"][e])):
        g = np.asarray(g, np.float32)
        mv[:, base] = g[0:128]
        mv[:64, base + 1] = g[128:192]
        mv[:32, base + 2] = g[160:192]; mv[32:64, base + 2] = g[128:160]
    mv[:, 10] = inp["hgrn_out_norm"][e]
    mv[:, 11] = inp["hgrn_lb"][0][hd * 128:(hd + 1) * 128]
    mv[:, 12] = inp["hgrn_lb"][1][hd * 128:(hd + 1) * 128]
    mv[:48, 13] = NEG
    w["mvec"] = mv
    return w


def host_tables(seq):
    LP = PRE + seq
    pos = np.maximum(np.arange(LP) - 48, 0).astype(np.float32)
    inv = (np.float32(10000.0) ** (-(np.arange(32, dtype=np.float32)) / np.float32(32))).astype(np.float32)
    ang = (pos[:, None] * inv[None, :]).astype(np.float32)
    t = {}
    t["cosT"] = np.ascontiguousarray(np.tile(np.cos(ang).astype(np.float32).T, (2, 1)))
    t["sinT"] = np.ascontiguousarray(np.tile(np.sin(ang).astype(np.float32).T, (2, 1)))
    kc = (np.arange(128) // 64)[:, None]
    x = np.arange(896)[None, :]
    t["masktab"] = np.where(kc <= x // 64 - 6, 0.0, NEG).astype(np.float32)
    si = np.arange(64)
    t["triu"] = (si[:, None] <= si[None, :]).astype(np.float32)
    cm = np.ones((128, 512), np.float32); cm[:, ::64] = 0.0
    t["cmask"] = cm
    t["ident"] = np.eye(128, dtype=np.float32)
    return t


import contextlib


def _dram(nc, name, shape, dt, kind):
    return nc.dram_tensor(name, list(shape), dt, kind=kind).ap()


GROUPS = [[0, 1, 2, 3], [4, 5, 6, 7]]
M_W = (("win_fm", [128, 8, 1024]), ("win_hi", [128, 8, 128]), ("wq", [128, 2, 256]), ("wkv", [128, 2, 256]), ("mvec", [128, 16]))
M_TAB = (("cosT", None), ("sinT", None), ("masktab", [128, 896]), ("triu", [64, 64]), ("cmask", [128, 512]), ("ident", [128, 128]))


XB = 2048


def build_fused(seq, stop=None):
    nown = seq // 4
    ncol = PRE + nown
    LP = PRE + seq
    NTt = nown // 512
    NB = seq // XB
    nc = bass.Bass("TRN2", target_bir_lowering=False)
    xT = _dram(nc, "xT", [D, ncol], F32, "ExternalInput")
    sel = _dram(nc, "sel", [128, 4], F32, "ExternalInput")
    invcnt = _dram(nc, "invcnt", [128, 4, PRE], F32, "ExternalInput")
    g0 = _dram(nc, "gains_p0", [128, 5, 8], F32, "ExternalInput")
    tabs = {}
    for name, shape in M_TAB:
        tabs[name] = _dram(nc, name, shape or [64, LP], F32, "ExternalInput")
    mw = []
    for e in range(2):
        mw.append({name: _dram(nc, "%s_e%d" % (name, e), shape, F32, "ExternalInput") for name, shape in M_W})
    tw = []
    for t in range(2):
        d = {}
        d["wout_f32"] = _dram(nc, "wout_f32_t%d" % t, [128, 8192], F32, "ExternalInput")
        d["poolw_f32"] = _dram(nc, "poolw_f32_t%d" % t, [128, 2048], F32, "ExternalInput")
        d["wup_f32"] = _dram(nc, "wup_f32_t%d" % t, [2, 8, 128, 4096], F32, "ExternalInput")
        d["wdn_f32"] = _dram(nc, "wdn_f32_t%d" % t, [2, 8, 128, 4096], F32, "ExternalInput")
        d["gains"] = _dram(nc, "gains_t%d" % t, [128, 5, 8], F32, "ExternalInput")
        d["wup_bf"] = _dram(nc, "wup_bf_t%d" % t, [2, 8, 128, 4096], BF16, "Internal")
        d["wdn_bf"] = _dram(nc, "wdn_bf_t%d" % t, [2, 8, 128, 4096], BF16, "Internal")
        d["wout_bf"] = _dram(nc, "wout_bf_t%d" % t, [128, 8192], BF16, "Internal")
        d["poolw_bf"] = _dram(nc, "poolw_bf_t%d" % t, [128, 2048], BF16, "Internal")
        d["invcnt"] = invcnt
        d["sel"] = sel
        tw.append(d)
    out_h = _dram(nc, "out_h", [D, ncol], F32, "ExternalOutput")
    hT = _dram(nc, "hT", [D, ncol], F32, "Internal")
    xu_pre = _dram(nc, "xu_pre", [D, PRE], BF16, "Internal")
    uf_pre = _dram(nc, "uf_pre", [4 * D, PRE], BF16, "Internal")
    xu_blk = [_dram(nc, "xu_blk%d" % t, [D, 512], BF16, "Internal") for t in range(NTt)]
    uf_blk = [_dram(nc, "uf_blk%d" % t, [4 * D, 512], BF16, "Internal") for t in range(NTt)]
    xo_pre = _dram(nc, "xo_pre", [256, PRE], BF16, "Internal")
    of_pre = _dram(nc, "of_pre", [4 * 256, PRE], BF16, "Internal")
    xo_blk = [_dram(nc, "xo_blk%d" % k, [256, XB], BF16, "Internal") for k in range(NB)]
    of_blk = [_dram(nc, "of_blk%d" % k, [4 * 256, XB], BF16, "Internal") for k in range(NB)]

    S = Sched(nc).open()
    MWN = ("win_fm", "win_hi", "wq", "wkv")
    mw_bf = [{name: _dram(nc, "%s_bf_e%d" % (name, e), dict(M_W)[name], BF16, "Internal") for name in MWN} for e in range(2)]

    def emit_m_conversions(e):
        for name in MWN:
            S.dma("pool", mw_bf[e][name], mw[e][name], writes=["mwbf"])

    def xu_dst(ti, N):
        t = xu_pre if ti == 0 else xu_blk[ti - 1]
        return t.rearrange("(c p) n -> p c n", p=128)

    def after_xu(ti):
        if ti == 0:
            S.collective("AllGather", [xu_pre], [uf_pre], GROUPS, reads=["xu0"], writes=["ufpre"])
        else:
            S.collective("AllGather", [xu_blk[ti - 1]], [uf_blk[ti - 1]], GROUPS, reads=["xu%d" % ti], writes=["ufblk%d" % (ti - 1)])

    def uf_tile(ti, pos0, N):
        if ti == 0:
            return uf_pre.rearrange("(g c p) n -> g p c n", g=4, p=128)[0]
        i = ti - 1
        return uf_blk[i % NTt].rearrange("(g c p) n -> g p c n", g=4, p=128)[i // NTt]

    def xo_dst(ti, N):
        if ti == 0:
            return xo_pre.rearrange("(j p) n -> p j n", p=128)
        i = ti - 1
        off = (i % 4) * 512
        return xo_blk[i // 4].rearrange("(j p) n -> p j n", p=128)[:, :, off:off + 512]

    def after_xo(ti):
        if ti == 0:
            S.collective("AllGather", [xo_pre], [of_pre], GROUPS, reads=["xo0"], writes=["ofpre"])
        elif (ti - 1) % 4 == 3:
            k = (ti - 1) // 4
            S.collective("AllGather", [xo_blk[k]], [of_blk[k]], GROUPS,
                         reads=["xo%d" % (4 * k + 1 + j) for j in range(4)], writes=["ofblk%d" % k])

    def of_tiles(col0, N, cc):
        if col0 == 0 and cc == 0:
            src, off = of_pre, 0
        else:
            q = cc * nown + col0 - PRE
            src, off = of_blk[q // XB], q % XB
        v = src.rearrange("(h j p) n -> j p h n", h=4, p=128)
        return [(j * 4, j * 4 + 4, v[j][:, :, off:off + N]) for j in range(2)]

    def run_phase(fn):
        with contextlib.ExitStack() as es:
            evs = fn(es)
            with nc.Block() as block:
                S.emit(block)
        return evs

    def dump(src, shape, dt):
        dbg = _dram(nc, "dbg_out", shape, dt, "ExternalOutput")
        ev = S.dma("sp", dbg, src, writes=["dbg"])
        with nc.Block() as block:
            S.finish("sp", [ev])
            S.emit(block)
        S.close()
        return nc

    emit_m_conversions(0)
    run_phase(lambda es: build_tphase(nc, S, es, nown, dict(src_h=xT, gains=g0, xu_dst=xu_dst, after_xu=after_xu), None, "xu"))
    S.barrier()
    if stop == "ag1":
        return dump(uf_blk[0], [4 * D, 512], BF16)
    final_evs = None
    for e in range(2):
        Tm = dict(mw[e]); Tm.update(mw_bf[e]); Tm["weights_bf16"] = True
        Tm.update(tabs); Tm["uf_tile"] = uf_tile; Tm["xo_dst"] = xo_dst; Tm["after_xo"] = after_xo
        conv_items = t_conversion_items(tw[e])

        def tile_hook(ti, ntiles, conv_items=conv_items):
            n = len(conv_items) if ti == ntiles - 1 else min(len(conv_items), -(-34 // ntiles))
            for _ in range(n):
                dst, src, res = conv_items.pop(0)
                S.dma("pool", dst, src, writes=[res])
        Tm["tile_hook"] = tile_hook
        run_phase(lambda es, e=e, Tm=Tm: build_mixer(nc, S, es, seq, Tm, e))
        S.barrier()
        if stop == "ag2":
            return dump(of_blk[0], [4 * 256, XB], BF16)
        Tt = dict(tw[e]); Tt["of_tiles"] = of_tiles
        Tt["src_h"] = xT if e == 0 else hT
        if e == 0:
            Tt["dst_h"] = hT; Tt["xu_dst"] = xu_dst; Tt["after_xu"] = after_xu
            Tt["after_loads"] = lambda: emit_m_conversions(1)
        else:
            Tt["out_h"] = out_h
        final_evs = run_phase(lambda es, e=e, Tt=Tt: build_tphase(nc, S, es, nown, Tt, (2 * e, 2 * e + 1), "xu" if e == 0 else "out", convert=False))
        S.barrier()
    with nc.Block() as block:
        S.finish("sp", final_evs)
        S.emit(block)
    S.close()
    return nc


def kernel(_stop=None, **inputs):
    inp = {k: np.asarray(v) for k, v in inputs.items()}
    x = inp["x"].astype(np.float32)
    B, seq, _ = x.shape
    nown = seq // 4
    ncol = PRE + nown
    LP = PRE + seq
    tabs = host_tables(seq)
    g0 = np.zeros((128, 5, 8), np.float32); g0[:, 4] = _pc(inp["mix_norm"][0])
    tws = [host_tphase_weights(inp, 0, 1, 2), host_tphase_weights(inp, 2, 3, None)]
    in_maps = []
    for r in range(8):
        b, c = divmod(r, 4)
        P = np.zeros((LP, D), np.float32)
        P[48:64] = inp["meta_tokens"]
        P[64:] = x[b]
        m = dict(xT=np.ascontiguousarray(P[c * nown:c * nown + ncol].T), gains_p0=g0, invcnt=host_invcnt(c))
        selv = np.zeros((128, 4), np.float32); selv[:, c] = 1.0
        m["sel"] = selv
        m.update(tabs)
        for e in range(2):
            for k, v in host_mixer_weights(inp, e, c).items():
                m["%s_e%d" % (k, e)] = v
        for t in range(2):
            for k, v in tws[t].items():
                m["%s_t%d" % (k, t)] = v
        in_maps.append(m)
    if _stop is not None:
        ncd = build_fused(seq, _stop)
        used = set(t.name for t in ncd.m.functions[0].allocations if hasattr(t, "name")) if False else None
        return run_bass_kernel_spmd(ncd, in_maps, core_ids=list(range(8))).results
    res = run_bass_kernel_spmd(build_fused(seq), in_maps, core_ids=list(range(8))).results
    out = np.zeros((B, seq, D), np.float32)
    for r in range(8):
        b, c = divmod(r, 4)
        out[b, c * nown:(c + 1) * nown] = res[r]["out_h"][:, PRE:].T
    return out
```

```python
import numpy as np
import ml_dtypes
import concourse.bass as bass
import concourse.mybir as mybir
from concourse.bass_utils import run_bass_kernel_spmd

F32 = mybir.dt.float32
BF16 = mybir.dt.bfloat16
ALU = mybir.AluOpType
AF = mybir.ActivationFunctionType

ENGINES = ("pe", "act", "dve", "pool", "sp")


import threading


class Stepper:
    def __init__(self, fn):
        self.fn = fn
        self.go = threading.Semaphore(0)
        self.back = threading.Semaphore(0)
        self.done = False
        self.started = False
        self.budget = 0
        self.exc = None
        self.th = threading.Thread(target=self._run, daemon=True)

    def _run(self):
        self.go.acquire()
        try:
            self.fn()
        except BaseException as e:
            self.exc = e
        self.done = True
        self.back.release()

    def pause_point(self):
        self.budget -= 1
        if self.budget <= 0:
            self.back.release()
            self.go.acquire()

    def step(self, n):
        if self.done:
            return
        self.budget = n
        if not self.started:
            self.started = True
            self.th.start()
        self.go.release()
        self.back.acquire()
        if self.exc is not None:
            raise self.exc

    def finish(self):
        while not self.done:
            self.step(10 ** 9)


class Sched:
    NDMA = 8

    def __init__(self, nc):
        self.nc = nc
        self.ops = {e: [] for e in ENGINES}
        self.cnt = {e: 0 for e in ENGINES}
        self.sem = {}
        self.dsem = {}
        self.dcnt = {"sp": 0, "pool": 0, "act": 0}
        self.last_w = {}
        self.readers = {}
        self.waited = {e: {} for e in ENGINES}
        self._stack = []
        self.stepper = None

    def _pause(self):
        st = self.stepper
        if st is not None and threading.current_thread() is st.th:
            st.pause_point()

    def open(self):
        import contextlib
        self._es = contextlib.ExitStack()
        for e in ("pe", "act", "dve", "pool"):
            self.sem[e] = self._es.enter_context(self.nc.semaphore("s_" + e))
        for q in ("sp", "pool", "act"):
            self.dsem[q] = [self._es.enter_context(self.nc.semaphore("d_%s%d" % (q, i)))
                            for i in range(self.NDMA)]
        return self

    def close(self):
        self._es.close()

    def barrier(self):
        evs = []
        for e in ("pe", "act", "dve", "pool"):
            if self.cnt[e]:
                evs.append((self.sem[e], self.cnt[e]))
        n = self.cnt.get("cc", 0)
        for k in range(min(n, self.NCC)):
            last_i = n - 1 - ((n - 1 - k) % self.NCC)
            evs.append((self.ccsem[k], last_i // self.NCC + 1))
        for q in ("sp", "pool", "act"):
            n = self.dcnt[q]
            for k in range(min(n, self.NDMA)):
                last_i = n - 1 - ((n - 1 - k) % self.NDMA)
                evs.append((self.dsem[q][k], 16 * (last_i // self.NDMA + 1)))
        for e in ENGINES:
            waits = self._waits(e, evs)
            if waits:
                self.ops[e].append((waits, None, None, 0))
        self.last_w = {}
        self.readers = {}

    def _deps(self, reads, writes):
        evs = []
        for r in reads:
            if r in self.last_w:
                evs.append(self.last_w[r])
        for w in writes:
            if w in self.last_w:
                evs.append(self.last_w[w])
            evs.extend(self.readers.get(w, ()))
        return evs

    def _commit(self, ev, reads, writes):
        for r in reads:
            self.readers.setdefault(r, []).append(ev)
        for w in writes:
            self.last_w[w] = ev
            self.readers[w] = []

    def _waits(self, eng, evs):
        out = []
        wd = self.waited[eng]
        best = {}
        for (s, v) in evs:
            key = id(s)
            if wd.get(key, 0) >= v:
                continue
            if key not in best or best[key][1] < v:
                best[key] = (s, v)
        for key, (s, v) in best.items():
            wd[key] = v
            out.append((s, v))
        return out

    def op(self, eng, fn, reads=(), writes=()):
        evs = self._deps(reads, writes)
        if eng == "pe":
            evs = [ev for ev in evs if ev[0] is not self.sem["pe"]]
        waits = self._waits(eng, evs)
        self.cnt[eng] += 1
        ev = (self.sem[eng], self.cnt[eng])
        self.ops[eng].append((waits, fn, ev[0], 1))
        self._commit(ev, reads, writes)
        self._pause()
        return ev

    def dma(self, q, out, in_, reads=(), writes=(), **kw):
        i = self.dcnt[q]
        self.dcnt[q] += 1
        s = self.dsem[q][i % self.NDMA]
        prev = 16 * (i // self.NDMA)
        evs = self._deps(reads, writes)
        if prev:
            evs.append((s, prev))
        waits = self._waits(q, evs)
        ev = (s, prev + 16)

        def fn(e, out=out, in_=in_, kw=kw):
            return e.dma_start(out=out, in_=in_, **kw)
        self.ops[q].append((waits, fn, s, 16))
        self._commit(ev, reads, writes)
        self._pause()
        return ev

    NCC = 4

    def collective(self, kind, ins, outs, groups, reads=(), writes=()):
        if "cc" not in self.cnt:
            self.ccsem = [self._es.enter_context(self.nc.semaphore("s_cc%d" % i)) for i in range(self.NCC)]
            self.cnt["cc"] = 0
        i = self.cnt["cc"]
        self.cnt["cc"] += 1
        s_ = self.ccsem[i % self.NCC]
        prev = i // self.NCC
        evs = self._deps(reads, writes)
        if prev:
            evs.append((s_, prev))
        waits = self._waits("pool", evs)
        ev = (s_, prev + 1)

        def fn(e):
            return e.collective_compute(kind, ALU.bypass, replica_groups=groups, ins=ins, outs=outs)
        self.ops["pool"].append((waits, fn, s_, None))
        self._commit(ev, reads, writes)
        return ev

    def finish(self, eng, evs):
        waits = self._waits(eng, evs)
        self.ops[eng].append((waits, None, None, 0))

    def emit(self, block):
        nc = self.nc

        def run(eng_name):
            def body(e):
                for (waits, fn, s, inc) in self.ops[eng_name]:
                    for (ws, wv) in waits:
                        e.wait_ge(ws, wv)
                    if fn is not None:
                        ins = fn(e)
                        if inc is None:
                            ins.then_inc(s)
                        else:
                            ins.then_inc(s, inc)
                self.ops[eng_name] = []
            return body
        block.tensor(run("pe"))
        block.scalar(run("act"))
        block.vector(run("dve"))
        block.gpsimd(run("pool"))
        block.sync(run("sp"))


D = 1024
DC = 8
DFF = 4096
EPS = 1e-6
PRE = 64
SEQ_FULL = 16384
ATT_SCALE = 192.0 ** -0.5
INTERLEAVE = False
PFQ = "act"
STEP_EVERY = 1
HOLD_MLA = True
NEG = -30000.0


_UID = [0]


def _uid():
    _UID[0] += 1
    return "u%d_" % _UID[0]


class Rot:
    def __init__(self, names):
        self.names = list(names)
        self.i = 0

    def next(self):
        n = self.names[self.i % len(self.names)]
        self.i += 1
        return n


class WStream:
    def __init__(self, S, name, slots, items, look=1):
        self.S, self.name, self.slots, self.items, self.look = S, name, slots, items, look
        self.emitted = 0

    def _emit_upto(self, i):
        while self.emitted <= min(i, len(self.items) - 1):
            k = self.emitted
            slot = self.slots[k % len(self.slots)]
            self.S.dma("sp", slot[:], self.items[k][0], reads=[self.items[k][1]],
                       writes=["%s%d" % (self.name, k % len(self.slots))])
            self.emitted += 1

    def get(self, i):
        self._emit_upto(i + self.look)
        k = i % len(self.slots)
        return self.slots[k], "%s%d" % (self.name, k)


def _mm_group(mms):
    def fn(e):
        ins = None
        n = len(mms)
        for i, (o, l, r) in enumerate(mms):
            ins = e.matmul(o, l, r, start=(i == 0), stop=(i == n - 1))
        return ins
    return fn


class Ctx:
    pass


def t_tiles(nown):
    return [(0, PRE)] + [(PRE + 512 * i, 512) for i in range(nown // 512)]


def t_conversion_items(T):
    items = []
    if T.get("wout_bf") is not None:
        items.append((T["wout_bf"], T["wout_f32"], "woutbf"))
        items.append((T["poolw_bf"], T["poolw_f32"], "poolwbf"))
    for li in range(2):
        for g in range(8):
            items.append((T["wup_bf"][li, g], T["wup_f32"][li, g], "wupbf%d_%d" % (li, g)))
            items.append((T["wdn_bf"][li, g], T["wdn_f32"][li, g], "wdnbf%d_%d" % (li, g)))
    return items


def emit_t_conversions(S, T):
    for li in range(2):
        for g in range(8):
            S.dma("pool", T["wup_bf"][li, g], T["wup_f32"][li, g], writes=["wupbf%d_%d" % (li, g)])
            S.dma("pool", T["wdn_bf"][li, g], T["wdn_f32"][li, g], writes=["wdnbf%d_%d" % (li, g)])


def build_tphase(nc, S, es, nown, T, layers, tail, convert=True):
    ncol = PRE + nown
    uid = _uid()
    sb = lambda name, shape, dt: es.enter_context(nc.sbuf_tensor(uid + name, shape, dt))
    pbanks = [es.enter_context(nc.psum_tensor(uid + "pb%d" % i, [128, 512], F32)) for i in range(8)]
    prot = Rot(range(8))

    ones = sb("ones", [128, 128], BF16)
    S.op("pool", lambda e: e.memset(ones[:], 1.0), writes=["ones"])
    gains = sb("gains", [128, 5, DC], F32)
    S.dma("sp", gains[:], T["gains"], writes=["gains"])

    h_t = [sb("h_t%d" % i, [128, DC, 512], F32) for i in range(2)]
    u_bf = sb("u_bf", [128, DC, 512], BF16)
    sq_bf = sb("sq_bf", [128, DC, 512], BF16)
    rt = sb("rt", [128, 512], F32)
    rstd = sb("rstd", [128, 512], F32)

    if layers is not None:
        o_t = [sb("o_t%d" % i, [128, DC, 512], BF16) for i in range(2)]
        wout = sb("wout", [128, 8 * 8 * 128], BF16)
        poolw = sb("poolw", [128, 4 * 2 * 2 * 128], BF16)
        if T.get("wout_bf") is not None:
            S.dma("sp", wout[:], T["wout_bf"], writes=["wout"])
            S.dma("sp", poolw[:], T["poolw_bf"], writes=["poolw"])
        else:
            S.dma("pool", wout[:], T["wout_f32"], writes=["wout"])
            S.dma("pool", poolw[:], T["poolw_f32"], writes=["poolw"])
        if convert:
            emit_t_conversions(S, T)
        if T.get("after_loads") is not None:
            T["after_loads"]()
        if T.get("sel") is not None:
            selv = sb("selv", [128, 4], F32)
            S.dma("sp", selv[:], T["sel"], writes=["selv"])
            cand = [sb("cand%d" % i, [128, DC, 512], BF16) for i in range(2)]
            candrot = Rot(range(2))
        invcnt = sb("invcnt", [128, 4, PRE], F32)
        S.dma("sp", invcnt[:], T["invcnt"], writes=["invcnt"])
        a_bf = sb("a_bf", [128, 32, 512], BF16)
        rl = [sb("rl%d" % i, [128, 512], F32) for i in range(2)]
        rlrot = Rot(range(2))
        u32 = sb("u32", [128, DC, 16 + 512], F32)
        wt = [sb("wt%d" % i, [128, 2, 16 + 512], F32) for i in range(2)]
        d_bf = sq_bf
        S.op("pool", lambda e: e.memset(u32[:, :, 0:16], 0.0), writes=["u32halo"])
        wup_slots = [sb("wup%d" % i, [128, 8 * 512], BF16) for i in range(2)]
        wdn_slots = [sb("wdn%d" % i, [128, 32 * 128], BF16) for i in range(2)]
        tiles = t_tiles(nown)
        up_items, dn_items = [], []
        for _ in tiles:
            for li in range(2):
                for g in range(8):
                    up_items.append((T["wup_bf"][li, g], "wupbf%d_%d" % (li, g)))
                for k in range(8):
                    dn_items.append((T["wdn_bf"][li, k], "wdnbf%d_%d" % (li, k)))
        wup = WStream(S, "wup", wup_slots, up_items, look=1)
        wdn = WStream(S, "wdn", wdn_slots, dn_items, look=1)
        cnt = {"up": 0, "dn": 0}

    def rms(h, hname, N, gain_idx, out_bf=None, out32=None):
        S.op("act", lambda e: e.activation(out=sq_bf[:, :, :N], in_=h[:, :, :N], func=AF.Square),
             reads=[hname], writes=["sq_bf"])
        pb = prot.next()
        ps = pbanks[pb]
        S.op("pe", _mm_group([(ps[:, :N], ones[:], sq_bf[:, c, :N]) for c in range(DC)]),
             reads=["sq_bf", "ones"], writes=["pb%d" % pb])
        S.op("act", lambda e: e.activation(out=rt[:, :N], in_=ps[:, :N], func=AF.Ln, bias=EPS, scale=1.0 / D),
             reads=["pb%d" % pb], writes=["rt"])
        S.op("act", lambda e: e.activation(out=rstd[:, :N], in_=rt[:, :N], func=AF.Exp, scale=-0.5), reads=["rt"], writes=["rstd"])
        if out_bf is not None:
            for c in range(DC):
                S.op("dve", lambda e, c=c: e.scalar_tensor_tensor(
                    out=out_bf[:, c, :N], in0=h[:, c, :N], scalar=gains[:, gain_idx, c:c + 1], in1=rstd[:, :N],
                    op0=ALU.mult, op1=ALU.mult), reads=[hname, "gains", "rstd"], writes=["u_bf%d" % c])
        if out32 is not None:
            for c in range(DC):
                S.op("dve", lambda e, c=c: e.scalar_tensor_tensor(
                    out=out32[:, c, 16:16 + N], in0=h[:, c, :N], scalar=gains[:, gain_idx, c:c + 1], in1=rstd[:, :N],
                    op0=ALU.mult, op1=ALU.mult), reads=[hname, "gains", "rstd"], writes=["u32_%d" % c])

    def mlp(h, hname, N, li, hook_up=None, hook_dn=None):
        rms(h, hname, N, li, out_bf=u_bf)
        if hook_up is not None:
            hook_up()
        for g in range(8):
            slot, sname = wup.get(cnt["up"]); cnt["up"] += 1
            for j in range(4):
                pb = prot.next(); ps = pbanks[pb]
                S.op("pe", _mm_group([(ps[:, :N], slot[:, c * 512 + j * 128:c * 512 + (j + 1) * 128], u_bf[:, c, :N])
                                      for c in range(DC)]), reads=["u_bf%d" % c for c in range(DC)] + [sname], writes=["pb%d" % pb])
                r = rlrot.next()
                S.op("act", lambda e, ps=ps, r=r: e.activation(out=rl[r][:, :N], in_=ps[:, :N], func=AF.Relu),
                     reads=["pb%d" % pb], writes=["rl%d" % r])
                f = g * 4 + j
                S.op("pool", lambda e, r=r, f=f: e.tensor_tensor(out=a_bf[:, f, :N], in0=rl[r][:, :N], in1=rl[r][:, :N], op=ALU.mult),
                     reads=["rl%d" % r], writes=["a_bf%d" % f])
        if hook_dn is not None:
            hook_dn()
        for k in range(8):
            slot, sname = wdn.get(cnt["dn"]); cnt["dn"] += 1
            pb = prot.next(); ps = pbanks[pb]
            S.op("pe", _mm_group([(ps[:, :N], slot[:, f * 128:(f + 1) * 128], a_bf[:, f, :N]) for f in range(32)]),
                 reads=["a_bf%d" % f for f in range(32)] + [sname], writes=["pb%d" % pb])
            S.op("dve", lambda e, ps=ps, k=k: e.tensor_tensor(out=h[:, k, :N], in0=h[:, k, :N], in1=ps[:, :N], op=ALU.add),
                 reads=["pb%d" % pb, hname], writes=[hname])

    tiles = t_tiles(nown)
    out_evs = []

    src_ap = T["src_h"].rearrange("(c p) n -> p c n", p=128)

    def load_tile(ti, part):
        if ti >= len(tiles):
            return
        col0, N = tiles[ti]
        hb = ti % 2
        if part == 0:
            S.dma(PFQ, h_t[hb][:, :, :N], src_ap[:, :, col0:col0 + N], writes=["h_t%d" % hb])
        if layers is None:
            return
        o = o_t[hb]; oname = "o_t%d" % hb
        for cc in (0, 1) if part == 0 else (2, 3):
            ci = candrot.next(); cd = cand[ci]
            for (c0, c1, oap) in T["of_tiles"](col0, N, cc):
                S.dma(PFQ, cd[:, c0:c1, :N], oap, writes=["cand%d" % ci])
            if cc == 0:
                S.op("dve", lambda e, cd=cd, cc=cc: e.tensor_scalar(
                    out=o[:, :, :N], in0=cd[:, :, :N], scalar1=selv[:, cc:cc + 1], scalar2=None, op0=ALU.mult),
                    reads=["cand%d" % ci, "selv"], writes=[oname])
            else:
                S.op("dve", lambda e, cd=cd, cc=cc: e.scalar_tensor_tensor(
                    out=o[:, :, :N], in0=cd[:, :, :N], scalar=selv[:, cc:cc + 1], in1=o[:, :, :N], op0=ALU.mult, op1=ALU.add),
                    reads=["cand%d" % ci, "selv", oname], writes=[oname])

    def do_tile(ti, col0, N):
        hb = ti % 2
        h = h_t[hb]; hname = "h_t%d" % hb
        if ti == 0:
            load_tile(0, 0)
            load_tile(0, 1)
        if layers is None:
            load_tile(ti + 1, 0)
        if layers is not None:
            o = o_t[hb]; oname = "o_t%d" % hb
            for k in range(8):
                pb = prot.next(); ps = pbanks[pb]
                S.op("pe", _mm_group([(ps[:, :N], wout[:, (k * 8 + c) * 128:(k * 8 + c + 1) * 128], o[:, c, :N]) for c in range(8)]),
                     reads=[oname, "wout"], writes=["pb%d" % pb])
                S.op("dve", lambda e, ps=ps, k=k: e.tensor_tensor(out=h[:, k, :N], in0=h[:, k, :N], in1=ps[:, :N], op=ALU.add),
                     reads=["pb%d" % pb, hname], writes=[hname])
            mlp(h, hname, N, 0, hook_up=lambda: load_tile(ti + 1, 0), hook_dn=lambda: load_tile(ti + 1, 1))
            rms(h, hname, N, 2, out32=u32)
            W = 16 + N
            for g in range(4):
                srcw = u32[:, 2 * g:2 * g + 2, :]
                sname = "u32g%d" % g
                sh = 1
                lo = 0
                for lvl in range(g + 1):
                    dst = wt[lvl % 2]
                    S.op("dve", lambda e, dst=dst, srcw=srcw, sh=sh, lo=lo: e.tensor_tensor(
                        out=dst[:, :, lo + sh:W], in0=srcw[:, :, lo + sh:W], in1=srcw[:, :, lo:W - sh], op=ALU.add),
                        reads=([sname] if sname.startswith("wt") else ["u32_%d" % (2 * g), "u32_%d" % (2 * g + 1)]) + ["u32halo"], writes=["wt%d" % (lvl % 2)])
                    srcw = dst
                    sname = "wt%d" % (lvl % 2)
                    lo += sh
                    sh *= 2
                win = srcw
                if ti == 0:
                    for cc in range(2):
                        S.op("dve", lambda e, win=win, g=g, cc=cc: e.tensor_tensor(
                            out=win[:, cc, 16:16 + N], in0=win[:, cc, 16:16 + N], in1=invcnt[:, g, :N], op=ALU.mult),
                            reads=[sname, "invcnt"], writes=[sname])
                    S.op("dve", lambda e, win=win, g=g: e.tensor_tensor(
                        out=d_bf[:, 2 * g:2 * g + 2, :N], in0=win[:, :, 16:16 + N], in1=u32[:, 2 * g:2 * g + 2, 16:16 + N],
                        op=ALU.subtract), reads=[sname] + ["u32_%d" % c for c in range(DC)], writes=["sq_bf"])
                else:
                    S.op("dve", lambda e, win=win, g=g: e.scalar_tensor_tensor(
                        out=d_bf[:, 2 * g:2 * g + 2, :N], in0=win[:, :, 16:16 + N], scalar=1.0 / (2 ** (g + 1)),
                        in1=u32[:, 2 * g:2 * g + 2, 16:16 + N], op0=ALU.mult, op1=ALU.subtract),
                        reads=[sname] + ["u32_%d" % c for c in range(DC)], writes=["sq_bf"])
            for ke in range(8):
                g = ke // 2
                pb = prot.next(); ps = pbanks[pb]
                S.op("pe", _mm_group([(ps[:, :N], poolw[:, ((g * 2 + ke % 2) * 2 + cc) * 128:((g * 2 + ke % 2) * 2 + cc + 1) * 128],
                                       d_bf[:, 2 * g + cc, :N]) for cc in range(2)]),
                     reads=["sq_bf", "poolw"], writes=["pb%d" % pb])
                S.op("dve", lambda e, ps=ps, ke=ke: e.scalar_tensor_tensor(
                    out=h[:, ke, :N], in0=ps[:, :N], scalar=gains[:, 3, ke:ke + 1], in1=h[:, ke, :N], op0=ALU.mult, op1=ALU.add),
                    reads=["pb%d" % pb, hname, "gains"], writes=[hname])
            S.op("pool", lambda e, N=N: e.tensor_copy(out=u32[:, :, 0:16], in_=u32[:, :, N:N + 16]),
                 reads=["u32_%d" % c for c in range(DC)], writes=["u32halo"])
            mlp(h, hname, N, 1)
        if tail == "out":
            dst = T["out_h"].rearrange("(c p) n -> p c n", p=128)
            out_evs.append(S.dma("pool", dst[:, :, col0:col0 + N], h[:, :, :N], reads=[hname], writes=["out_h"]))
        else:
            if layers is not None:
                dst = T["dst_h"].rearrange("(c p) n -> p c n", p=128)
                out_evs.append(S.dma("pool", dst[:, :, col0:col0 + N], h[:, :, :N], reads=[hname], writes=["dst_h"]))
            rms(h, hname, N, 4, out_bf=u_bf)
            out_evs.append(S.dma("pool", T["xu_dst"](ti, N), u_bf[:, :, :N], reads=["u_bf%d" % c for c in range(DC)], writes=["xu%d" % ti]))
            T["after_xu"](ti)

    for ti, (col0, N) in enumerate(tiles):
        do_tile(ti, col0, N)
    return out_evs


def _mm_list(mms):
    def fn(e):
        ins = None
        for (o, l, r, st, sp) in mms:
            ins = e.matmul(o, l, r, start=st, stop=sp)
        return ins
    return fn


def m_tiles(seq):
    return [(0, PRE)] + [(PRE + 512 * i, 512) for i in range(seq // 512)]


def build_mixer(nc, S, es, seq, T, e_idx):
    LP = PRE + seq
    NKT = 1 + seq // 128
    uid = _uid()
    sb = lambda name, shape, dt: es.enter_context(nc.sbuf_tensor(uid + name, shape, dt))
    pbanks = [es.enter_context(nc.psum_tensor(uid + "mb%d" % i, [128, 512], F32)) for i in range(2)]
    prot = Rot(range(2))
    sbanks = [es.enter_context(nc.psum_tensor(uid + "sb%d" % i, [128, 512], F32)) for i in range(2)]
    srot = Rot(range(2))
    hO_t = es.enter_context(nc.psum_tensor(uid + "hO", [128, 512], F32))
    pO = es.enter_context(nc.psum_tensor(uid + "pO", [128, 512], F32))
    pZ = es.enter_context(nc.psum_tensor(uid + "pZ", [128, 512], F32))
    pT = es.enter_context(nc.psum_tensor(uid + "pT", [128, 1024], BF16))

    ones = sb("ones", [128, 128], BF16)
    S.op("pool", lambda e: e.memset(ones[:], 1.0), writes=["ones"])
    ident = sb("ident", [128, 128], BF16)
    S.dma("pool", ident[:], T["ident"], writes=["ident"])
    wqueue = "sp" if T.get("weights_bf16") else "pool"
    win_fm = sb("win_fm", [128, DC, 1024], BF16)
    S.dma(wqueue, win_fm[:], T["win_fm"], writes=["win_fm"])
    win_hi = sb("win_hi", [128, DC, 128], BF16)
    S.dma(wqueue, win_hi[:], T["win_hi"], writes=["win_hi"])
    wq = sb("wq", [128, 2, 256], BF16)
    S.dma(wqueue, wq[:], T["wq"], writes=["wq"])
    wkv = sb("wkv", [128, 2, 256], BF16)
    S.dma(wqueue, wkv[:], T["wkv"], writes=["wkv"])
    mvec = sb("mvec", [128, 16], F32)
    S.dma("sp", mvec[:], T["mvec"], writes=["mvec"])
    masktab = sb("masktab", [128, 896], F32)
    S.dma("sp", masktab[:], T["masktab"], writes=["masktab"])
    triu = sb("triu", [64, 64], F32)
    S.dma("sp", triu[:], T["triu"], writes=["triu"])
    cmask = sb("cmask", [128, 512], F32)
    S.dma("sp", cmask[:], T["cmask"], writes=["cmask"])
    if T.get("after_loads") is not None:
        T["after_loads"]()
    S.op("dve", lambda e: e.tensor_scalar(out=win_fm[:, :, 960:992], in0=win_fm[:, :, 960:992], scalar1=-1.0, scalar2=None, op0=ALU.mult),
         reads=["win_fm"], writes=["win_fm"])
    S.op("dve", lambda e: e.tensor_scalar(out=wq[:, :, 192:224], in0=wq[:, :, 192:224], scalar1=-1.0, scalar2=None, op0=ALU.mult),
         reads=["wq"], writes=["wq"])
    lbv = sb("lbv", [128, 4], F32)
    if e_idx == 0:
        S.op("dve", lambda e: e.memset(lbv[:, 0:1], 0.0), writes=["lbv"])
        S.op("dve", lambda e: e.memset(lbv[:, 1:2], 1.0), reads=["lbv"], writes=["lbv"])
    else:
        ex = sb("lbex", [128, 2], F32)
        S.op("act", lambda e: e.activation(out=ex[:], in_=mvec[:, 11:13], func=AF.Exp), reads=["mvec"], writes=["lbex"])
        S.op("dve", lambda e: e.tensor_tensor(out=lbv[:, 2:3], in0=ex[:, 0:1], in1=ex[:, 1:2], op=ALU.add), reads=["lbex"], writes=["lbv2"])
        S.op("dve", lambda e: e.reciprocal(out=lbv[:, 3:4], in_=lbv[:, 2:3]), reads=["lbv2"], writes=["lbv3"])
        S.op("dve", lambda e: e.tensor_tensor(out=lbv[:, 0:1], in0=ex[:, 1:2], in1=lbv[:, 3:4], op=ALU.mult), reads=["lbex", "lbv3"], writes=["lbv"])
        S.op("dve", lambda e: e.tensor_tensor(out=lbv[:, 1:2], in0=ex[:, 0:1], in1=lbv[:, 3:4], op=ALU.mult), reads=["lbex", "lbv3", "lbv"], writes=["lbv"])

    Kn = sb("Kn", [128, LP], BF16)
    Kr = sb("Kr", [64, LP], BF16)
    Vs = sb("Vs", [128, NKT * 128], BF16)

    ut = sb("ut", [128, DC, 512], BF16)
    cs = sb("cs", [64, 2, 512], F32)
    wk = {}

    def W(name, shape=(128, 512), dt=F32):
        if name not in wk:
            wk[name] = sb("w_" + name, list(shape), dt)
        return wk[name]

    S32 = sb("S32", [128, 128], F32)
    S_bf2 = [sb("S_bf%d" % i, [128, 128], BF16) for i in range(2)]
    gch = [0]
    S.op("pool", lambda e: e.memset(S32[:], 0.0), writes=["S32"])
    S.op("pool", lambda e: e.memset(S_bf2[0][:], 0.0), writes=["S_bf0"])
    Pt = [sb("Pt%d" % i, [128, 512], BF16) for i in range(3)]
    prot_p = Rot(range(3))
    xo = [sb("xo%d" % i, [128, 2, 512], BF16) for i in range(2)]

    def proj(col0, M, N):
        pb = prot.next(); ps = pbanks[pb]
        S.op("pe", _mm_group([(ps[:M, :N], win_fm[:, c, col0:col0 + M], ut[:, c, :N]) for c in range(DC)]),
             reads=["ut", "win_fm"], writes=["mb%d" % pb])
        return pb, ps

    def rstd_from(ps, pbname, N, dim, outname):
        rt = W("rt")
        rs = W(outname)
        S.op("act", lambda e: e.activation(out=rt[:, :N], in_=ps[:, :N], func=AF.Ln, bias=EPS, scale=1.0 / dim),
             reads=[pbname], writes=["rt"])
        S.op("act", lambda e: e.activation(out=rs[:, :N], in_=rt[:, :N], func=AF.Exp, scale=-0.5), reads=["rt"], writes=[outname])
        return rs

    tiles = m_tiles(seq)
    out_evs = []

    handoff = {}

    def prologue(ti, pos0, N):
        nch = N // 64
        xob = xo[ti % 2]; xoname = "xo%d" % (ti % 2)
        S.dma("sp", ut[:, :, :N], T["uf_tile"](ti, pos0, N), writes=["ut"])
        S.dma("sp", cs[:, 0, :N], T["cosT"][:, pos0:pos0 + N], writes=["cs0"])
        S.dma("sp", cs[:, 1, :N], T["sinT"][:, pos0:pos0 + N], writes=["cs1"])
        q32, sg, gate, logf, kk, bb, eb, enb = (W(n) for n in ("q32", "sg", "gate", "logf", "kk", "bb", "eb", "enb"))
        qe, ke, kd = (W(n, dt=BF16) for n in ("qe", "ke", "kd"))
        pb, ps = proj(0, 128, N)
        S.op("act", lambda e, ps=ps: e.activation(out=q32[:, :N], in_=ps[:, :N], func=AF.Silu), reads=["mb%d" % pb], writes=["q32"])
        pb, ps = proj(256, 128, N)
        S.op("act", lambda e, ps=ps: e.activation(out=gate[:, :N], in_=ps[:, :N], func=AF.Silu), reads=["mb%d" % pb], writes=["gate"])
        pb, ps = proj(128, 128, N)
        S.op("act", lambda e, ps=ps: e.activation(out=sg[:, :N], in_=ps[:, :N], func=AF.Sigmoid), reads=["mb%d" % pb], writes=["sg"])
        S.op("dve", lambda e: e.tensor_scalar(out=sg[:, :N], in0=sg[:, :N], scalar1=lbv[:, 1:2], scalar2=lbv[:, 0:1], op0=ALU.mult, op1=ALU.add),
             reads=["sg", "lbv"], writes=["sg"])
        S.op("act", lambda e: e.activation(out=logf[:, :N], in_=sg[:, :N], func=AF.Ln), reads=["sg"], writes=["logf"])
        S.op("dve", lambda e: e.tensor_scalar(out=kk[:, :N], in0=sg[:, :N], scalar1=-1.0, scalar2=1.0, op0=ALU.mult, op1=ALU.add),
             reads=["sg"], writes=["kk"])
        if ti == 0:
            S.op("dve", lambda e: e.memset(logf[:, 0:48], 0.0), reads=["logf"], writes=["logf"])
        S.op("dve", lambda e: e.tensor_tensor_scan(out=bb[:, :N], data0=cmask[:, :N], data1=logf[:, :N], initial=0.0, op0=ALU.mult, op1=ALU.add),
             reads=["cmask", "logf"], writes=["bb"])
        S.op("act", lambda e: e.activation(out=eb[:, :N], in_=bb[:, :N], func=AF.Exp), reads=["bb"], writes=["eb"])
        S.op("act", lambda e: e.activation(out=enb[:, :N], in_=bb[:, :N], func=AF.Exp, scale=-1.0), reads=["bb"], writes=["enb"])
        S.op("dve", lambda e: e.tensor_tensor(out=qe[:, :N], in0=q32[:, :N], in1=eb[:, :N], op=ALU.mult), reads=["q32", "eb"], writes=["qe"])
        S.op("dve", lambda e: e.tensor_tensor(out=ke[:, :N], in0=kk[:, :N], in1=enb[:, :N], op=ALU.mult), reads=["kk", "enb"], writes=["ke"])
        for c in range(nch):
            S.op("dve", lambda e, c=c: e.scalar_tensor_tensor(
                out=kd[:, c * 64:(c + 1) * 64], in0=kk[:, c * 64:(c + 1) * 64], scalar=eb[:, c * 64 + 63:c * 64 + 64],
                in1=enb[:, c * 64:(c + 1) * 64], op0=ALU.mult, op1=ALU.mult), reads=["kk", "eb", "enb"], writes=["kd"])

        def tr_fn(e, nch=nch):
            ins = None
            for c in range(nch):
                ins = e.transpose(pT[:64, c * 128:(c + 1) * 128], kd[:, c * 64:(c + 1) * 64], ident[:])
            return ins
        S.op("pe", tr_fn, reads=["kd", "ident"], writes=["pT"])
        kdT = W("kdT", (64, 1024), BF16)
        S.op("act", lambda e, nch=nch: e.activation(out=kdT[:, :nch * 128], in_=pT[:64, :nch * 128], func=AF.Copy), reads=["pT"], writes=["kdT"])
        Vc = W("Vc", (64, 1024), BF16)
        for j0 in range(0, nch, 4):
            pb = prot.next(); ps = pbanks[pb]
            n4 = min(4, nch - j0)

            def vc_fn(e, ps=ps, j0=j0, n4=n4):
                ins = None
                for jj in range(n4):
                    cj = j0 + jj
                    for c in range(DC):
                        ins = e.matmul(ps[:64, jj * 128:(jj + 1) * 128], ut[:, c, cj * 64:(cj + 1) * 64], win_hi[:, c, :],
                                       start=(c == 0), stop=(c == DC - 1))
                return ins
            S.op("pe", vc_fn, reads=["ut", "win_hi"], writes=["mb%d" % pb])
            S.op("act", lambda e, ps=ps, j0=j0, n4=n4: e.activation(out=Vc[:, j0 * 128:(j0 + n4) * 128], in_=ps[:64, :n4 * 128], func=AF.Copy),
                 reads=["mb%d" % pb], writes=["Vc"])
        hO = hO_t
        ATm = W("ATm", (64, 512), BF16)
        U32 = W("U32", (128, 1024))
        for c in range(nch):
            cs_ = slice(c * 64, (c + 1) * 64)
            pb = prot.next(); ps = pbanks[pb]
            S.op("pe", _mm_group([(ps[:64, :64], ke[:, cs_], qe[:, cs_])]), reads=["ke", "qe"], writes=["mb%d" % pb])
            S.op("dve", lambda e, ps=ps, cs_=cs_: e.tensor_tensor(out=ATm[:, cs_], in0=ps[:64, :64], in1=triu[:], op=ALU.mult),
                 reads=["mb%d" % pb, "triu"], writes=["ATm%d" % c])
        for c in range(nch):
            pb2 = prot.next(); ps2 = pbanks[pb2]
            S.op("pe", _mm_group([(ps2[:, :128], kdT[:, c * 128:(c + 1) * 128], Vc[:, c * 128:(c + 1) * 128])]),
                 reads=["kdT", "Vc"], writes=["mb%d" % pb2])
            S.op("act", lambda e, ps2=ps2, c=c: e.activation(out=U32[:, c * 128:(c + 1) * 128], in_=ps2[:, :128], func=AF.Copy),
                 reads=["mb%d" % pb2], writes=["U32_%d" % c])
        for c in range(nch):
            cs_ = slice(c * 64, (c + 1) * 64)
            g = gch[0]; gch[0] += 1
            Sin = S_bf2[g % 2]; Sout = S_bf2[(g + 1) % 2]
            S.op("pe", _mm_group([(hO[:, cs_], Vc[:, c * 128:(c + 1) * 128], ATm[:, cs_]),
                                  (hO[:, cs_], Sin[:], qe[:, cs_])]),
                 reads=["Vc", "ATm%d" % c, "S_bf%d" % (g % 2), "qe"], writes=["hO"])
            S.op("dve", lambda e, c=c, Sout=Sout: e.scalar_tensor_tensor(
                out=Sout[:], in0=S32[:], scalar=eb[:, c * 64 + 63:c * 64 + 64], in1=U32[:, c * 128:(c + 1) * 128], op0=ALU.mult, op1=ALU.add),
                reads=["S32", "eb", "U32_%d" % c], writes=["S_bf%d" % ((g + 1) % 2)])
            S.op("dve", lambda e, c=c: e.scalar_tensor_tensor(
                out=S32[:], in0=S32[:], scalar=eb[:, c * 64 + 63:c * 64 + 64], in1=U32[:, c * 128:(c + 1) * 128], op0=ALU.mult, op1=ALU.add),
                reads=["S32", "eb", "U32_%d" % c], writes=["S32"])
        hO_res = ["hO"]
        osq = W("osq", dt=BF16)
        S.op("act", lambda e: e.activation(out=osq[:, :N], in_=hO[:, :N], func=AF.Square), reads=hO_res, writes=["osq"])
        pb, ps = prot.next(), None
        ps = pbanks[pb]
        S.op("pe", _mm_group([(ps[:, :N], ones[:], osq[:, :N])]), reads=["osq", "ones"], writes=["mb%d" % pb])
        rs = rstd_from(ps, "mb%d" % pb, N, 128.0, "rs_o")
        t1 = W("t1")
        S.op("dve", lambda e, rs=rs: e.scalar_tensor_tensor(out=t1[:, :N], in0=hO[:, :N], scalar=mvec[:, 10:11], in1=rs[:, :N], op0=ALU.mult, op1=ALU.mult),
             reads=hO_res + ["mvec", "rs_o"], writes=["t1"])
        S.op("pool", lambda e: e.tensor_tensor(out=xob[:, 0, :N], in0=t1[:, :N], in1=gate[:, :N], op=ALU.mult),
             reads=["t1", "gate"], writes=[xoname + "a"])
        if HOLD_MLA and S.stepper is not None and threading.current_thread() is S.stepper.th:
            S.stepper.budget = 10 ** 9
        cq32 = W("cq32", (128, 2, 512)); csq = W("csq", (128, 2, 512), BF16)
        cqn = W("cqn", (128, 2, 512), BF16); ckvn = W("ckvn", (128, 2, 512), BF16)
        for (col0, gcol, dst, dname) in ((384, 0, cqn, "cqn"), (640, 2, ckvn, "ckvn")):
            for r in range(2):
                pb, ps = proj(col0 + 128 * r, 128, N)
                S.op("act", lambda e, ps=ps, r=r: e.activation(out=cq32[:, r, :N], in_=ps[:, :N], func=AF.Copy), reads=["mb%d" % pb], writes=["cq32_%d" % r])
                S.op("act", lambda e, ps=ps, r=r: e.activation(out=csq[:, r, :N], in_=ps[:, :N], func=AF.Square), reads=["mb%d" % pb], writes=["csq_%d" % r])
            pb = prot.next(); ps = pbanks[pb]
            S.op("pe", _mm_group([(ps[:, :N], ones[:], csq[:, r, :N]) for r in range(2)]), reads=["csq_0", "csq_1", "ones"], writes=["mb%d" % pb])
            rs = rstd_from(ps, "mb%d" % pb, N, 256.0, "rs_c")
            for r in range(2):
                S.op("dve", lambda e, r=r, dst=dst, gcol=gcol, rs=rs: e.scalar_tensor_tensor(
                    out=dst[:, r, :N], in0=cq32[:, r, :N], scalar=mvec[:, gcol + r:gcol + r + 1], in1=rs[:, :N], op0=ALU.mult, op1=ALU.mult),
                    reads=["cq32_%d" % r, "mvec", "rs_c"], writes=[dname])

        def qk_finish(jn, jr, jt, gbase, out_n, out_n_name, out_r, out_r_name):
            sqn = W("sqn", dt=BF16); sqr = W("sqr", (64, 512), BF16)
            n32 = W("q32"); r32 = W("r32", (64, 512)); t32 = W("t32", (64, 512))
            ps_n, pbn = jn()
            S.op("dve", lambda e: e.tensor_copy(out=n32[:, :N], in_=ps_n[:, :N]), reads=[pbn], writes=["q32"])
            S.op("act", lambda e: e.activation(out=sqn[:, :N], in_=ps_n[:, :N], func=AF.Square), reads=[pbn], writes=["sqn"])
            ps_r, pbr = jr()
            S.op("dve", lambda e: e.tensor_copy(out=r32[:, :N], in_=ps_r[:64, :N]), reads=[pbr], writes=["r32"])
            S.op("act", lambda e: e.activation(out=sqr[:, :N], in_=ps_r[:64, :N], func=AF.Square), reads=[pbr], writes=["sqr"])
            ps_t, pbt = jt()
            S.op("dve", lambda e: e.tensor_copy(out=t32[:, :N], in_=ps_t[:64, :N]), reads=[pbt], writes=["t32"])
            pb = prot.next(); ps = pbanks[pb]
            S.op("pe", _mm_group([(ps[:, :N], ones[:], sqn[:, :N]), (ps[:, :N], ones[:64, :], sqr[:, :N])]),
                 reads=["sqn", "sqr", "ones"], writes=["mb%d" % pb])
            rs = rstd_from(ps, "mb%d" % pb, N, 192.0, "rs_qk")
            S.op("dve", lambda e: e.scalar_tensor_tensor(out=out_n, in0=n32[:, :N], scalar=mvec[:, gbase:gbase + 1], in1=rs[:, :N],
                                                         op0=ALU.mult, op1=ALU.mult), reads=["q32", "mvec", "rs_qk"], writes=[out_n_name])
            ra = r32; rb = t32
            S.op("dve", lambda e: e.scalar_tensor_tensor(out=ra[:, :N], in0=r32[:, :N], scalar=mvec[:64, gbase + 1:gbase + 2], in1=cs[:, 0, :N],
                                                         op0=ALU.mult, op1=ALU.mult), reads=["r32", "mvec", "cs0"], writes=["r32"])
            S.op("dve", lambda e: e.scalar_tensor_tensor(out=rb[:, :N], in0=t32[:, :N], scalar=mvec[:64, gbase + 2:gbase + 3], in1=cs[:, 1, :N],
                                                         op0=ALU.mult, op1=ALU.mult), reads=["t32", "mvec", "cs1"], writes=["t32"])
            S.op("pool", lambda e: e.tensor_tensor(out=ra[:, :N], in0=ra[:, :N], in1=rb[:, :N], op=ALU.add), reads=["r32", "t32"], writes=["r32"])
            S.op("dve", lambda e: e.tensor_tensor(out=out_r, in0=ra[:, :N], in1=rs[:64, :N], op=ALU.mult), reads=["r32", "rs_qk"], writes=[out_r_name])

        def job_w(wt, rdname, src, c0, c1, M):
            def f():
                pb = prot.next(); ps = pbanks[pb]
                S.op("pe", _mm_group([(ps[:M, :N], wt[:, r, c0:c1], src[:, r, :N]) for r in range(2)]), reads=[rdname, "wq", "wkv"], writes=["mb%d" % pb])
                return ps, "mb%d" % pb
            return f

        def job_p(c0, M):
            def f():
                pb, ps = proj(c0, M, N)
                return ps, "mb%d" % pb
            return f

        qnn = "qn%d" % (ti % 2); qrn = "qr%d" % (ti % 2)
        qn = W(qnn, dt=BF16); qr = W(qrn, (64, 512), BF16)
        handoff[ti] = (qn, qr, qnn, qrn)
        qk_finish(job_w(wq, "cqn", cqn, 0, 128, 128), job_w(wq, "cqn", cqn, 128, 192, 64), job_w(wq, "cqn", cqn, 192, 256, 64),
                  4, qn[:, :N], qnn, qr[:, :N], qrn)
        kvres = "KV_t%d" % ti
        qk_finish(job_w(wkv, "ckvn", ckvn, 0, 128, 128), job_p(896, 64), job_p(960, 64),
                  7, Kn[:, pos0:pos0 + N], kvres + "n", Kr[:, pos0:pos0 + N], kvres + "r")
        pb = prot.next(); ps = pbanks[pb]
        if ti == 0:
            subs = [(0, 64)]
            kt0 = 0
        else:
            subs = [(jj * 128, 128) for jj in range(4)]
            kt0 = 1 + 4 * (ti - 1)

        def v_fn(e, ps=ps, subs=subs):
            ins = None
            for jj, (c0, M) in enumerate(subs):
                for r in range(2):
                    ins = e.matmul(ps[:M, jj * 128:(jj + 1) * 128], ckvn[:, r, c0:c0 + M], wkv[:, r, 128:256], start=(r == 0), stop=(r == 1))
            return ins
        S.op("pe", v_fn, reads=["ckvn", "wkv"], writes=["mb%d" % pb])
        Mv = subs[0][1]
        S.op("act", lambda e, ps=ps, kt0=kt0, Mv=Mv, ns=len(subs): e.activation(
            out=Vs[:Mv, kt0 * 128:(kt0 + ns) * 128], in_=ps[:Mv, :ns * 128], func=AF.Copy), reads=["mb%d" % pb], writes=[kvres + "v"])
        if T.get("tile_hook") is not None:
            T["tile_hook"](ti, len(tiles))
    def attention(ti, pos0, N, stepper):
        xob = xo[ti % 2]; xoname = "xo%d" % (ti % 2)
        qn, qr, qnn, qrn = handoff.pop(ti)
        if ti == 0:
            kts = [(0, 64, "pad", 0)]
        else:
            kts = [(0, 64, "pad", 0)] + [(j, 128, None, 0) for j in range(1, 4 * (ti - 1) + 1)] + \
                  [(4 * (ti - 1) + 1 + jj, 128, "diag", jj) for jj in range(4)]
        nk = len(kts)
        pend = []

        def emit_S(idx):
            j, M, kind, jj = kts[idx]
            kc0 = 0 if j == 0 else PRE + 128 * (j - 1)
            tk = 0 if j == 0 else 1 + (j - 1) // 4
            kres = "KV_t%d" % tk
            pb = srot.next(); ps = sbanks[pb]
            S.op("pe", _mm_group([(ps[:M, :N], Kn[:, kc0:kc0 + M], qn[:, :N]), (ps[:M, :N], Kr[:, kc0:kc0 + M], qr[:, :N])]),
                 reads=[kres + "n", kres + "r", qnn, qrn], writes=["sb%d" % pb])
            p = prot_p.next(); P = Pt[p]
            if kind == "pad":
                S.op("act", lambda e: e.activation(out=P[:M, :N], in_=ps[:M, :N], func=AF.Exp, bias=mvec[:M, 13:14], scale=ATT_SCALE),
                     reads=["sb%d" % pb, "mvec"], writes=["Pt%d" % p])
            elif kind == "diag":
                tmp = W("dtmp")
                off = (3 - jj) * 128
                S.op("dve", lambda e: e.tensor_tensor(out=tmp[:, :N], in0=ps[:, :N], in1=masktab[:, off:off + N], op=ALU.add),
                     reads=["sb%d" % pb, "masktab"], writes=["dtmp"])
                S.op("act", lambda e: e.activation(out=P[:, :N], in_=tmp[:, :N], func=AF.Exp, scale=ATT_SCALE), reads=["dtmp"], writes=["Pt%d" % p])
            else:
                S.op("act", lambda e: e.activation(out=P[:, :N], in_=ps[:, :N], func=AF.Exp, scale=ATT_SCALE), reads=["sb%d" % pb], writes=["Pt%d" % p])
            return (idx, j, M, p, kres)

        def emit_OZ(rec):
            idx, j, M, p, kres = rec
            P = Pt[p]
            S.op("pe", _mm_list([(pO[:, :N], Vs[:M, j * 128:(j + 1) * 128], P[:M, :N], idx == 0, idx == nk - 1),
                                 (pZ[:, :N], ones[:M, :], P[:M, :N], idx == 0, idx == nk - 1)]),
                 reads=[kres + "v", "Pt%d" % p, "ones"], writes=["pO", "pZ"])

        LA = 1
        per_iter = max(1, -(-200 // max(1, nk - 2)))
        for idx in range(nk):
            pend.append(emit_S(idx))
            if len(pend) > LA:
                emit_OZ(pend.pop(0))
            if stepper is not None and INTERLEAVE and idx % STEP_EVERY == STEP_EVERY - 1:
                stepper.step(per_iter * STEP_EVERY)
        while pend:
            emit_OZ(pend.pop(0))
        if stepper is not None:
            stepper.finish()
        rz = W("rz")
        S.op("dve", lambda e: e.reciprocal(out=rz[:, :N], in_=pZ[:, :N]), reads=["pZ"], writes=["rz"])
        S.op("dve", lambda e: e.tensor_tensor(out=xob[:, 1, :N], in0=pO[:, :N], in1=rz[:, :N], op=ALU.mult), reads=["pO", "rz"], writes=[xoname + "b"])
        if ti == 0:
            S.op("pool", lambda e: e.memset(xob[:, :, 0:48], 0.0), reads=[xoname + "a", xoname + "b"], writes=[xoname + "a", xoname + "b"])
        out_evs.append(S.dma("sp", T["xo_dst"](ti, N), xob[:, :, :N], reads=[xoname + "a", xoname + "b"], writes=["xo%d" % ti]))
        T["after_xo"](ti)

    for par in range(2):
        W("qn%d" % par, dt=BF16)
        W("qr%d" % par, (64, 512), BF16)
    prologue(0, *tiles[0])
    for ti, (pos0, N) in enumerate(tiles):
        st = None
        if ti + 1 < len(tiles):
            st = Stepper(lambda ti=ti: prologue(ti + 1, *tiles[ti + 1]))
        S.stepper = st
        attention(ti, pos0, N, st)
        S.stepper = None
    return out_evs


IN_HQ, IN_HF, IN_HI, IN_HG, IN_CQ, IN_CKV, IN_KR = 0, 512, 1024, 1536, 2048, 2304, 2560
BF = ml_dtypes.bfloat16


def _pc(v):
    v = np.asarray(v, np.float32)
    return np.ascontiguousarray(v.reshape(-1, 128).T)


def host_tphase_weights(inp, le, lo, lnext):
    e, o = le // 2, lo // 2
    w = {}
    w["wout_f32"] = np.ascontiguousarray(
        inp["w_out"][e].reshape(8, 128, 8, 128).transpose(1, 2, 0, 3).reshape(128, 8192))
    w["wup_f32"] = np.ascontiguousarray(np.stack([
        inp["w_mlp_up"][l].reshape(8, 128, 8, 512).transpose(2, 1, 0, 3).reshape(8, 128, 4096) for l in (le, lo)]))
    w["wdn_f32"] = np.ascontiguousarray(np.stack([
        inp["w_mlp_down"][l].reshape(32, 128, 8, 128).transpose(2, 1, 0, 3).reshape(8, 128, 4096) for l in (le, lo)]))
    w["poolw_f32"] = np.ascontiguousarray(
        inp["pool_w"][o].reshape(4, 2, 128, 2, 128).transpose(2, 0, 3, 1, 4).reshape(128, 2048))
    g = np.zeros((128, 5, 8), np.float32)
    g[:, 0] = _pc(inp["mlp_norm"][le]); g[:, 1] = _pc(inp["mlp_norm"][lo])
    g[:, 2] = _pc(inp["mix_norm"][lo]); g[:, 3] = _pc(inp["pool_scale"][o])
    if lnext is not None:
        g[:, 4] = _pc(inp["mix_norm"][lnext])
    w["gains"] = g
    return w


def host_invcnt(c):
    t = np.zeros((128, 4, PRE), np.float32)
    for g in range(4):
        wdw = 2 ** (g + 1)
        t[:, g, :] = 1.0 / wdw
        if c == 0:
            for col in range(48, PRE):
                pos = col - 48
                t[:, g, col] = 1.0 / min(pos + 1, wdw)
    return t


def host_mixer_weights(inp, e, hd):
    w_in = inp["w_in"][e]
    cols = np.concatenate([
        np.arange(IN_HQ + hd * 128, IN_HQ + (hd + 1) * 128), np.arange(IN_HF + hd * 128, IN_HF + (hd + 1) * 128),
        np.arange(IN_HG + hd * 128, IN_HG + (hd + 1) * 128), np.arange(IN_CQ, IN_CQ + 256), np.arange(IN_CKV, IN_CKV + 256),
        np.arange(IN_KR, IN_KR + 64), np.arange(IN_KR + 32, IN_KR + 64), np.arange(IN_KR, IN_KR + 32)])
    w = {}
    w["win_fm"] = np.ascontiguousarray(w_in[:, cols].reshape(8, 128, 1024).transpose(1, 0, 2))
    w["win_hi"] = np.ascontiguousarray(w_in[:, IN_HI + hd * 128:IN_HI + (hd + 1) * 128].reshape(8, 128, 128).transpose(1, 0, 2))
    qb = hd * 192
    qcols = np.concatenate([np.arange(qb, qb + 192), np.arange(qb + 160, qb + 192), np.arange(qb + 128, qb + 160)])
    w["wq"] = np.ascontiguousarray(inp["w_q_up"][e][:, qcols].reshape(2, 128, 256).transpose(1, 0, 2))
    w["wkv"] = np.ascontiguousarray(inp["w_kv_up"][e][:, hd * 256:(hd + 1) * 256].reshape(2, 128, 256).transpose(1, 0, 2))
    mv = np.zeros((128, 16), np.float32)
    mv[:, 0:2] = _pc(inp["mla_q_a_norm"][e]); mv[:, 2:4] = _pc(inp["mla_kv_a_norm"][e])
    for base, g in ((4, inp["q_norm"][e]), (7, inp["k_norm"][e])):
        g = np.asarray(g, np.float32)
        mv[:, base] = g[0:128]
        mv[:64, base + 1] = g[128:192]
        mv[:32, base + 2] = g[160:192]; mv[32:64, base + 2] = g[128:160]
    mv[:, 10] = inp["hgrn_out_norm"][e]
    mv[:, 11] = inp["hgrn_lb"][0][hd * 128:(hd + 1) * 128]
    mv[:, 12] = inp["hgrn_lb"][1][hd * 128:(hd + 1) * 128]
    mv[:48, 13] = NEG
    w["mvec"] = mv
    return w


def host_tables(seq):
    LP = PRE + seq
    pos = np.maximum(np.arange(LP) - 48, 0).astype(np.float32)
    inv = (np.float32(10000.0) ** (-(np.arange(32, dtype=np.float32)) / np.float32(32))).astype(np.float32)
    ang = (pos[:, None] * inv[None, :]).astype(np.float32)
    t = {}
    t["cosT"] = np.ascontiguousarray(np.tile(np.cos(ang).astype(np.float32).T, (2, 1)))
    t["sinT"] = np.ascontiguousarray(np.tile(np.sin(ang).astype(np.float32).T, (2, 1)))
    kc = (np.arange(128) // 64)[:, None]
    x = np.arange(896)[None, :]
    t["masktab"] = np.where(kc <= x // 64 - 6, 0.0, NEG).astype(np.float32)
    si = np.arange(64)
    t["triu"] = (si[:, None] <= si[None, :]).astype(np.float32)
    cm = np.ones((128, 512), np.float32); cm[:, ::64] = 0.0
    t["cmask"] = cm
    t["ident"] = np.eye(128, dtype=np.float32)
    return t


import contextlib


def _dram(nc, name, shape, dt, kind):
    return nc.dram_tensor(name, list(shape), dt, kind=kind).ap()


GROUPS = [[0, 1, 2, 3], [4, 5, 6, 7]]
M_W = (("win_fm", [128, 8, 1024]), ("win_hi", [128, 8, 128]), ("wq", [128, 2, 256]), ("wkv", [128, 2, 256]), ("mvec", [128, 16]))
M_TAB = (("cosT", None), ("sinT", None), ("masktab", [128, 896]), ("triu", [64, 64]), ("cmask", [128, 512]), ("ident", [128, 128]))


XB = 2048


def build_fused(seq, stop=None):
    nown = seq // 4
    ncol = PRE + nown
    LP = PRE + seq
    NTt = nown // 512
    NB = seq // XB
    nc = bass.Bass("TRN2", target_bir_lowering=False)
    xT = _dram(nc, "xT", [D, ncol], F32, "ExternalInput")
    sel = _dram(nc, "sel", [128, 4], F32, "ExternalInput")
    invcnt = _dram(nc, "invcnt", [128, 4, PRE], F32, "ExternalInput")
    g0 = _dram(nc, "gains_p0", [128, 5, 8], F32, "ExternalInput")
    tabs = {}
    for name, shape in M_TAB:
        tabs[name] = _dram(nc, name, shape or [64, LP], F32, "ExternalInput")
    mw = []
    for e in range(2):
        mw.append({name: _dram(nc, "%s_e%d" % (name, e), shape, F32, "ExternalInput") for name, shape in M_W})
    tw = []
    for t in range(2):
        d = {}
        d["wout_f32"] = _dram(nc, "wout_f32_t%d" % t, [128, 8192], F32, "ExternalInput")
        d["poolw_f32"] = _dram(nc, "poolw_f32_t%d" % t, [128, 2048], F32, "ExternalInput")
        d["wup_f32"] = _dram(nc, "wup_f32_t%d" % t, [2, 8, 128, 4096], F32, "ExternalInput")
        d["wdn_f32"] = _dram(nc, "wdn_f32_t%d" % t, [2, 8, 128, 4096], F32, "ExternalInput")
        d["gains"] = _dram(nc, "gains_t%d" % t, [128, 5, 8], F32, "ExternalInput")
        d["wup_bf"] = _dram(nc, "wup_bf_t%d" % t, [2, 8, 128, 4096], BF16, "Internal")
        d["wdn_bf"] = _dram(nc, "wdn_bf_t%d" % t, [2, 8, 128, 4096], BF16, "Internal")
        d["wout_bf"] = _dram(nc, "wout_bf_t%d" % t, [128, 8192], BF16, "Internal")
        d["poolw_bf"] = _dram(nc, "poolw_bf_t%d" % t, [128, 2048], BF16, "Internal")
        d["invcnt"] = invcnt
        d["sel"] = sel
        tw.append(d)
    out_h = _dram(nc, "out_h", [D, ncol], F32, "ExternalOutput")
    hT = _dram(nc, "hT", [D, ncol], F32, "Internal")
    xu_pre = _dram(nc, "xu_pre", [D, PRE], BF16, "Internal")
    uf_pre = _dram(nc, "uf_pre", [4 * D, PRE], BF16, "Internal")
    xu_blk = [_dram(nc, "xu_blk%d" % t, [D, 512], BF16, "Internal") for t in range(NTt)]
    uf_blk = [_dram(nc, "uf_blk%d" % t, [4 * D, 512], BF16, "Internal") for t in range(NTt)]
    xo_pre = _dram(nc, "xo_pre", [256, PRE], BF16, "Internal")
    of_pre = _dram(nc, "of_pre", [4 * 256, PRE], BF16, "Internal")
    xo_blk = [_dram(nc, "xo_blk%d" % k, [256, XB], BF16, "Internal") for k in range(NB)]
    of_blk = [_dram(nc, "of_blk%d" % k, [4 * 256, XB], BF16, "Internal") for k in range(NB)]

    S = Sched(nc).open()
    MWN = ("win_fm", "win_hi", "wq", "wkv")
    mw_bf = [{name: _dram(nc, "%s_bf_e%d" % (name, e), dict(M_W)[name], BF16, "Internal") for name in MWN} for e in range(2)]

    def emit_m_conversions(e):
        for name in MWN:
            S.dma("pool", mw_bf[e][name], mw[e][name], writes=["mwbf"])

    def xu_dst(ti, N):
        t = xu_pre if ti == 0 else xu_blk[ti - 1]
        return t.rearrange("(c p) n -> p c n", p=128)

    def after_xu(ti):
        if ti == 0:
            S.collective("AllGather", [xu_pre], [uf_pre], GROUPS, reads=["xu0"], writes=["ufpre"])
        else:
            S.collective("AllGather", [xu_blk[ti - 1]], [uf_blk[ti - 1]], GROUPS, reads=["xu%d" % ti], writes=["ufblk%d" % (ti - 1)])

    def uf_tile(ti, pos0, N):
        if ti == 0:
            return uf_pre.rearrange("(g c p) n -> g p c n", g=4, p=128)[0]
        i = ti - 1
        return uf_blk[i % NTt].rearrange("(g c p) n -> g p c n", g=4, p=128)[i // NTt]

    def xo_dst(ti, N):
        if ti == 0:
            return xo_pre.rearrange("(j p) n -> p j n", p=128)
        i = ti - 1
        off = (i % 4) * 512
        return xo_blk[i // 4].rearrange("(j p) n -> p j n", p=128)[:, :, off:off + 512]

    def after_xo(ti):
        if ti == 0:
            S.collective("AllGather", [xo_pre], [of_pre], GROUPS, reads=["xo0"], writes=["ofpre"])
        elif (ti - 1) % 4 == 3:
            k = (ti - 1) // 4
            S.collective("AllGather", [xo_blk[k]], [of_blk[k]], GROUPS,
                         reads=["xo%d" % (4 * k + 1 + j) for j in range(4)], writes=["ofblk%d" % k])

    def of_tiles(col0, N, cc):
        if col0 == 0 and cc == 0:
            src, off = of_pre, 0
        else:
            q = cc * nown + col0 - PRE
            src, off = of_blk[q // XB], q % XB
        v = src.rearrange("(h j p) n -> j p h n", h=4, p=128)
        return [(j * 4, j * 4 + 4, v[j][:, :, off:off + N]) for j in range(2)]

    def run_phase(fn):
        with contextlib.ExitStack() as es:
            evs = fn(es)
            with nc.Block() as block:
                S.emit(block)
        return evs

    def dump(src, shape, dt):
        dbg = _dram(nc, "dbg_out", shape, dt, "ExternalOutput")
        ev = S.dma("sp", dbg, src, writes=["dbg"])
        with nc.Block() as block:
            S.finish("sp", [ev])
            S.emit(block)
        S.close()
        return nc

    emit_m_conversions(0)
    run_phase(lambda es: build_tphase(nc, S, es, nown, dict(src_h=xT, gains=g0, xu_dst=xu_dst, after_xu=after_xu), None, "xu"))
    S.barrier()
    if stop == "ag1":
        return dump(uf_blk[0], [4 * D, 512], BF16)
    final_evs = None
    for e in range(2):
        Tm = dict(mw[e]); Tm.update(mw_bf[e]); Tm["weights_bf16"] = True
        Tm.update(tabs); Tm["uf_tile"] = uf_tile; Tm["xo_dst"] = xo_dst; Tm["after_xo"] = after_xo
        conv_items = t_conversion_items(tw[e])

        def tile_hook(ti, ntiles, conv_items=conv_items):
            n = len(conv_items) if ti == ntiles - 1 else min(len(conv_items), -(-34 // ntiles))
            for _ in range(n):
                dst, src, res = conv_items.pop(0)
                S.dma("pool", dst, src, writes=[res])
        Tm["tile_hook"] = tile_hook
        run_phase(lambda es, e=e, Tm=Tm: build_mixer(nc, S, es, seq, Tm, e))
        S.barrier()
        if stop == "ag2":
            return dump(of_blk[0], [4 * 256, XB], BF16)
        Tt = dict(tw[e]); Tt["of_tiles"] = of_tiles
        Tt["src_h"] = xT if e == 0 else hT
        if e == 0:
            Tt["dst_h"] = hT; Tt["xu_dst"] = xu_dst; Tt["after_xu"] = after_xu
            Tt["after_loads"] = lambda: emit_m_conversions(1)
        else:
            Tt["out_h"] = out_h
        final_evs = run_phase(lambda es, e=e, Tt=Tt: build_tphase(nc, S, es, nown, Tt, (2 * e, 2 * e + 1), "xu" if e == 0 else "out", convert=False))
        S.barrier()
    with nc.Block() as block:
        S.finish("sp", final_evs)
        S.emit(block)
    S.close()
    return nc


def kernel(_stop=None, **inputs):
    inp = {k: np.asarray(v) for k, v in inputs.items()}
    x = inp["x"].astype(np.float32)
    B, seq, _ = x.shape
    nown = seq // 4
    ncol = PRE + nown
    LP = PRE + seq
    tabs = host_tables(seq)
    g0 = np.zeros((128, 5, 8), np.float32); g0[:, 4] = _pc(inp["mix_norm"][0])
    tws = [host_tphase_weights(inp, 0, 1, 2), host_tphase_weights(inp, 2, 3, None)]
    in_maps = []
    for r in range(8):
        b, c = divmod(r, 4)
        P = np.zeros((LP, D), np.float32)
        P[48:64] = inp["meta_tokens"]
        P[64:] = x[b]
        m = dict(xT=np.ascontiguousarray(P[c * nown:c * nown + ncol].T), gains_p0=g0, invcnt=host_invcnt(c))
        selv = np.zeros((128, 4), np.float32); selv[:, c] = 1.0
        m["sel"] = selv
        m.update(tabs)
        for e in range(2):
            for k, v in host_mixer_weights(inp, e, c).items():
                m["%s_e%d" % (k, e)] = v
        for t in range(2):
            for k, v in tws[t].items():
                m["%s_t%d" % (k, t)] = v
        in_maps.append(m)
    if _stop is not None:
        ncd = build_fused(seq, _stop)
        used = set(t.name for t in ncd.m.functions[0].allocations if hasattr(t, "name")) if False else None
        return run_bass_kernel_spmd(ncd, in_maps, core_ids=list(range(8))).results
    res = run_bass_kernel_spmd(build_fused(seq), in_maps, core_ids=list(range(8))).results
    out = np.zeros((B, seq, D), np.float32)
    for r in range(8):
        b, c = divmod(r, 4)
        out[b, c * nown:(c + 1) * nown] = res[r]["out_h"][:, PRE:].T
    return out
```

```python
import numpy as np
import ml_dtypes
import concourse.bass as bass
import concourse.mybir as mybir
from concourse.bass_utils import run_bass_kernel_spmd

F32 = mybir.dt.float32
BF16 = mybir.dt.bfloat16
ALU = mybir.AluOpType
AF = mybir.ActivationFunctionType

ENGINES = ("pe", "act", "dve", "pool", "sp")


import threading


class Stepper:
    def __init__(self, fn):
        self.fn = fn
        self.go = threading.Semaphore(0)
        self.back = threading.Semaphore(0)
        self.done = False
        self.started = False
        self.budget = 0
        self.exc = None
        self.th = threading.Thread(target=self._run, daemon=True)

    def _run(self):
        self.go.acquire()
        try:
            self.fn()
        except BaseException as e:
            self.exc = e
        self.done = True
        self.back.release()

    def pause_point(self):
        self.budget -= 1
        if self.budget <= 0:
            self.back.release()
            self.go.acquire()

    def step(self, n):
        if self.done:
            return
        self.budget = n
        if not self.started:
            self.started = True
            self.th.start()
        self.go.release()
        self.back.acquire()
        if self.exc is not None:
            raise self.exc

    def finish(self):
        while not self.done:
            self.step(10 ** 9)


class Sched:
    NDMA = 8

    def __init__(self, nc):
        self.nc = nc
        self.ops = {e: [] for e in ENGINES}
        self.cnt = {e: 0 for e in ENGINES}
        self.sem = {}
        self.dsem = {}
        self.dcnt = {"sp": 0, "pool": 0, "act": 0}
        self.last_w = {}
        self.readers = {}
        self.waited = {e: {} for e in ENGINES}
        self._stack = []
        self.stepper = None

    def _pause(self):
        st = self.stepper
        if st is not None and threading.current_thread() is st.th:
            st.pause_point()

    def open(self):
        import contextlib
        self._es = contextlib.ExitStack()
        for e in ("pe", "act", "dve", "pool"):
            self.sem[e] = self._es.enter_context(self.nc.semaphore("s_" + e))
        for q in ("sp", "pool", "act"):
            self.dsem[q] = [self._es.enter_context(self.nc.semaphore("d_%s%d" % (q, i)))
                            for i in range(self.NDMA)]
        return self

    def close(self):
        self._es.close()

    def barrier(self, keep=(), skip_cc=False):
        evs = []
        for e in ("pe", "act", "dve", "pool"):
            if self.cnt[e]:
                evs.append((self.sem[e], self.cnt[e]))
        n = 0 if skip_cc else self.cnt.get("cc", 0)
        for k in range(min(n, self.NCC)):
            last_i = n - 1 - ((n - 1 - k) % self.NCC)
            evs.append((self.ccsem[k], last_i // self.NCC + 1))
        for q in ("sp", "pool", "act"):
            n = self.dcnt[q]
            for k in range(min(n, self.NDMA)):
                last_i = n - 1 - ((n - 1 - k) % self.NDMA)
                evs.append((self.dsem[q][k], 16 * (last_i // self.NDMA + 1)))
        for e in ENGINES:
            waits = self._waits(e, evs)
            if waits:
                self.ops[e].append((waits, None, None, 0))
        self.last_w = {k: v for k, v in self.last_w.items() if keep and k.startswith(tuple(keep))}
        self.readers = {}

    def _deps(self, reads, writes):
        evs = []
        for r in reads:
            if r in self.last_w:
                evs.append(self.last_w[r])
        for w in writes:
            if w in self.last_w:
                evs.append(self.last_w[w])
            evs.extend(self.readers.get(w, ()))
        return evs

    def _commit(self, ev, reads, writes):
        for r in reads:
            self.readers.setdefault(r, []).append(ev)
        for w in writes:
            self.last_w[w] = ev
            self.readers[w] = []

    def _waits(self, eng, evs):
        out = []
        wd = self.waited[eng]
        best = {}
        for (s, v) in evs:
            key = id(s)
            if wd.get(key, 0) >= v:
                continue
            if key not in best or best[key][1] < v:
                best[key] = (s, v)
        for key, (s, v) in best.items():
            wd[key] = v
            out.append((s, v))
        return out

    def op(self, eng, fn, reads=(), writes=()):
        evs = self._deps(reads, writes)
        if eng == "pe":
            evs = [ev for ev in evs if ev[0] is not self.sem["pe"]]
        waits = self._waits(eng, evs)
        self.cnt[eng] += 1
        ev = (self.sem[eng], self.cnt[eng])
        self.ops[eng].append((waits, fn, ev[0], 1))
        self._commit(ev, reads, writes)
        self._pause()
        return ev

    def dma(self, q, out, in_, reads=(), writes=(), **kw):
        i = self.dcnt[q]
        self.dcnt[q] += 1
        s = self.dsem[q][i % self.NDMA]
        prev = 16 * (i // self.NDMA)
        evs = self._deps(reads, writes)
        if prev:
            evs.append((s, prev))
        waits = self._waits(q, evs)
        ev = (s, prev + 16)

        def fn(e, out=out, in_=in_, kw=kw):
            return e.dma_start(out=out, in_=in_, **kw)
        self.ops[q].append((waits, fn, s, 16))
        self._commit(ev, reads, writes)
        self._pause()
        return ev

    NCC = 4

    def collective(self, kind, ins, outs, groups, reads=(), writes=()):
        if "cc" not in self.cnt:
            self.ccsem = [self._es.enter_context(self.nc.semaphore("s_cc%d" % i)) for i in range(self.NCC)]
            self.cnt["cc"] = 0
        i = self.cnt["cc"]
        self.cnt["cc"] += 1
        s_ = self.ccsem[i % self.NCC]
        prev = i // self.NCC
        evs = self._deps(reads, writes)
        if prev:
            evs.append((s_, prev))
        waits = self._waits("pool", evs)
        ev = (s_, prev + 1)

        def fn(e):
            return e.collective_compute(kind, ALU.bypass, replica_groups=groups, ins=ins, outs=outs)
        self.ops["pool"].append((waits, fn, s_, None))
        self._commit(ev, reads, writes)
        return ev

    def finish(self, eng, evs):
        waits = self._waits(eng, evs)
        self.ops[eng].append((waits, None, None, 0))

    def emit(self, block):
        nc = self.nc

        def run(eng_name):
            def body(e):
                for (waits, fn, s, inc) in self.ops[eng_name]:
                    for (ws, wv) in waits:
                        e.wait_ge(ws, wv)
                    if fn is not None:
                        ins = fn(e)
                        if inc is None:
                            ins.then_inc(s)
                        else:
                            ins.then_inc(s, inc)
                self.ops[eng_name] = []
            return body
        block.tensor(run("pe"))
        block.scalar(run("act"))
        block.vector(run("dve"))
        block.gpsimd(run("pool"))
        block.sync(run("sp"))


D = 1024
DC = 8
DFF = 4096
EPS = 1e-6
PRE = 64
SEQ_FULL = 16384
ATT_SCALE = 192.0 ** -0.5
INTERLEAVE = False
PFQ = "act"
STEP_EVERY = 1
HOLD_MLA = True
NEG = -30000.0


_UID = [0]


def _uid():
    _UID[0] += 1
    return "u%d_" % _UID[0]


class Rot:
    def __init__(self, names):
        self.names = list(names)
        self.i = 0

    def next(self):
        n = self.names[self.i % len(self.names)]
        self.i += 1
        return n


class WStream:
    def __init__(self, S, name, slots, items, look=1):
        self.S, self.name, self.slots, self.items, self.look = S, name, slots, items, look
        self.emitted = 0

    def _emit_upto(self, i):
        while self.emitted <= min(i, len(self.items) - 1):
            k = self.emitted
            slot = self.slots[k % len(self.slots)]
            self.S.dma("sp", slot[:], self.items[k][0], reads=[self.items[k][1]],
                       writes=["%s%d" % (self.name, k % len(self.slots))])
            self.emitted += 1

    def get(self, i):
        self._emit_upto(i + self.look)
        k = i % len(self.slots)
        return self.slots[k], "%s%d" % (self.name, k)


def _mm_group(mms):
    def fn(e):
        ins = None
        n = len(mms)
        for i, (o, l, r) in enumerate(mms):
            ins = e.matmul(o, l, r, start=(i == 0), stop=(i == n - 1))
        return ins
    return fn


class Ctx:
    pass


def t_tiles(nown):
    return [(0, PRE)] + [(PRE + 512 * i, 512) for i in range(nown // 512)]


def t_conversion_items(T):
    items = []
    if T.get("wout_bf") is not None:
        items.append((T["wout_bf"], T["wout_f32"], "woutbf"))
        items.append((T["poolw_bf"], T["poolw_f32"], "poolwbf"))
    for li in range(2):
        for g in range(8):
            items.append((T["wup_bf"][li, g], T["wup_f32"][li, g], "wupbf%d_%d" % (li, g)))
            items.append((T["wdn_bf"][li, g], T["wdn_f32"][li, g], "wdnbf%d_%d" % (li, g)))
    return items


def emit_t_conversions(S, T):
    for li in range(2):
        for g in range(8):
            S.dma("pool", T["wup_bf"][li, g], T["wup_f32"][li, g], writes=["wupbf%d_%d" % (li, g)])
            S.dma("pool", T["wdn_bf"][li, g], T["wdn_f32"][li, g], writes=["wdnbf%d_%d" % (li, g)])


def build_tphase(nc, S, es, nown, T, layers, tail, convert=True):
    ncol = PRE + nown
    uid = _uid()
    sb = lambda name, shape, dt: es.enter_context(nc.sbuf_tensor(uid + name, shape, dt))
    pbanks = [es.enter_context(nc.psum_tensor(uid + "pb%d" % i, [128, 512], F32)) for i in range(8)]
    prot = Rot(range(8))

    ones = sb("ones", [128, 128], BF16)
    S.op("pool", lambda e: e.memset(ones[:], 1.0), writes=["ones"])
    gains = sb("gains", [128, 5, DC], F32)
    S.dma("sp", gains[:], T["gains"], writes=["gains"])

    h_t = [sb("h_t%d" % i, [128, DC, 512], F32) for i in range(2)]
    u_bf = sb("u_bf", [128, DC, 512], BF16)
    sq_bf = sb("sq_bf", [128, DC, 512], BF16)
    rt = sb("rt", [128, 512], F32)
    rstd = sb("rstd", [128, 512], F32)

    if layers is not None:
        o_t = [sb("o_t%d" % i, [128, DC, 512], BF16) for i in range(2)]
        wout = sb("wout", [128, 8 * 8 * 128], BF16)
        poolw = sb("poolw", [128, 4 * 2 * 2 * 128], BF16)
        if T.get("wout_bf") is not None:
            S.dma("sp", wout[:], T["wout_bf"], writes=["wout"])
            S.dma("sp", poolw[:], T["poolw_bf"], writes=["poolw"])
        else:
            S.dma("pool", wout[:], T["wout_f32"], writes=["wout"])
            S.dma("pool", poolw[:], T["poolw_f32"], writes=["poolw"])
        if convert:
            emit_t_conversions(S, T)
        if T.get("after_loads") is not None:
            T["after_loads"]()
        if T.get("sel") is not None:
            selv = sb("selv", [128, 4], F32)
            S.dma("sp", selv[:], T["sel"], writes=["selv"])
            cand = [sb("cand%d" % i, [128, DC, 512], BF16) for i in range(2)]
            candrot = Rot(range(2))
        invcnt = sb("invcnt", [128, 4, PRE], F32)
        S.dma("sp", invcnt[:], T["invcnt"], writes=["invcnt"])
        a_bf = sb("a_bf", [128, 32, 512], BF16)
        rl = [sb("rl%d" % i, [128, 512], F32) for i in range(2)]
        rlrot = Rot(range(2))
        u32 = sb("u32", [128, DC, 16 + 512], F32)
        wt = [sb("wt%d" % i, [128, 2, 16 + 512], F32) for i in range(2)]
        d_bf = sq_bf
        S.op("pool", lambda e: e.memset(u32[:, :, 0:16], 0.0), writes=["u32halo"])
        wup_slots = [sb("wup%d" % i, [128, 8 * 512], BF16) for i in range(2)]
        wdn_slots = [sb("wdn%d" % i, [128, 32 * 128], BF16) for i in range(2)]
        tiles = t_tiles(nown)
        up_items, dn_items = [], []
        for _ in tiles:
            for li in range(2):
                for g in range(8):
                    up_items.append((T["wup_bf"][li, g], "wupbf%d_%d" % (li, g)))
                for k in range(8):
                    dn_items.append((T["wdn_bf"][li, k], "wdnbf%d_%d" % (li, k)))
        wup = WStream(S, "wup", wup_slots, up_items, look=1)
        wdn = WStream(S, "wdn", wdn_slots, dn_items, look=1)
        cnt = {"up": 0, "dn": 0}

    def rms(h, hname, N, gain_idx, out_bf=None, out32=None):
        S.op("act", lambda e: e.activation(out=sq_bf[:, :, :N], in_=h[:, :, :N], func=AF.Square),
             reads=[hname], writes=["sq_bf"])
        pb = prot.next()
        ps = pbanks[pb]
        S.op("pe", _mm_group([(ps[:, :N], ones[:], sq_bf[:, c, :N]) for c in range(DC)]),
             reads=["sq_bf", "ones"], writes=["pb%d" % pb])
        S.op("act", lambda e: e.activation(out=rt[:, :N], in_=ps[:, :N], func=AF.Ln, bias=EPS, scale=1.0 / D),
             reads=["pb%d" % pb], writes=["rt"])
        S.op("act", lambda e: e.activation(out=rstd[:, :N], in_=rt[:, :N], func=AF.Exp, scale=-0.5), reads=["rt"], writes=["rstd"])
        if out_bf is not None:
            for c in range(DC):
                S.op("dve", lambda e, c=c: e.scalar_tensor_tensor(
                    out=out_bf[:, c, :N], in0=h[:, c, :N], scalar=gains[:, gain_idx, c:c + 1], in1=rstd[:, :N],
                    op0=ALU.mult, op1=ALU.mult), reads=[hname, "gains", "rstd"], writes=["u_bf%d" % c])
        if out32 is not None:
            for c in range(DC):
                S.op("dve", lambda e, c=c: e.scalar_tensor_tensor(
                    out=out32[:, c, 16:16 + N], in0=h[:, c, :N], scalar=gains[:, gain_idx, c:c + 1], in1=rstd[:, :N],
                    op0=ALU.mult, op1=ALU.mult), reads=[hname, "gains", "rstd"], writes=["u32_%d" % c])

    def mlp(h, hname, N, li, hook_up=None, hook_dn=None):
        rms(h, hname, N, li, out_bf=u_bf)
        if hook_up is not None:
            hook_up()
        for g in range(8):
            slot, sname = wup.get(cnt["up"]); cnt["up"] += 1
            for j in range(4):
                pb = prot.next(); ps = pbanks[pb]
                S.op("pe", _mm_group([(ps[:, :N], slot[:, c * 512 + j * 128:c * 512 + (j + 1) * 128], u_bf[:, c, :N])
                                      for c in range(DC)]), reads=["u_bf%d" % c for c in range(DC)] + [sname], writes=["pb%d" % pb])
                r = rlrot.next()
                S.op("act", lambda e, ps=ps, r=r: e.activation(out=rl[r][:, :N], in_=ps[:, :N], func=AF.Relu),
                     reads=["pb%d" % pb], writes=["rl%d" % r])
                f = g * 4 + j
                S.op("pool", lambda e, r=r, f=f: e.tensor_tensor(out=a_bf[:, f, :N], in0=rl[r][:, :N], in1=rl[r][:, :N], op=ALU.mult),
                     reads=["rl%d" % r], writes=["a_bf%d" % f])
        if hook_dn is not None:
            hook_dn()
        for k in range(8):
            slot, sname = wdn.get(cnt["dn"]); cnt["dn"] += 1
            pb = prot.next(); ps = pbanks[pb]
            S.op("pe", _mm_group([(ps[:, :N], slot[:, f * 128:(f + 1) * 128], a_bf[:, f, :N]) for f in range(32)]),
                 reads=["a_bf%d" % f for f in range(32)] + [sname], writes=["pb%d" % pb])
            S.op("dve", lambda e, ps=ps, k=k: e.tensor_tensor(out=h[:, k, :N], in0=h[:, k, :N], in1=ps[:, :N], op=ALU.add),
                 reads=["pb%d" % pb, hname], writes=[hname])

    tiles = t_tiles(nown)
    out_evs = []

    src_ap = T["src_h"].rearrange("(c p) n -> p c n", p=128)

    def load_tile(ti, part):
        if ti >= len(tiles):
            return
        col0, N = tiles[ti]
        hb = ti % 2
        if part == 0:
            S.dma(PFQ, h_t[hb][:, :, :N], src_ap[:, :, col0:col0 + N], writes=["h_t%d" % hb])
        if layers is None:
            return
        o = o_t[hb]; oname = "o_t%d" % hb
        for cc in (0, 1) if part == 0 else (2, 3):
            ci = candrot.next(); cd = cand[ci]
            for (c0, c1, oap, ores) in T["of_tiles"](col0, N, cc):
                S.dma(PFQ, cd[:, c0:c1, :N], oap, reads=[ores], writes=["cand%d" % ci])
            if cc == 0:
                S.op("dve", lambda e, cd=cd, cc=cc: e.tensor_scalar(
                    out=o[:, :, :N], in0=cd[:, :, :N], scalar1=selv[:, cc:cc + 1], scalar2=None, op0=ALU.mult),
                    reads=["cand%d" % ci, "selv"], writes=[oname])
            else:
                S.op("dve", lambda e, cd=cd, cc=cc: e.scalar_tensor_tensor(
                    out=o[:, :, :N], in0=cd[:, :, :N], scalar=selv[:, cc:cc + 1], in1=o[:, :, :N], op0=ALU.mult, op1=ALU.add),
                    reads=["cand%d" % ci, "selv", oname], writes=[oname])

    def do_tile(ti, col0, N):
        hb = ti % 2
        h = h_t[hb]; hname = "h_t%d" % hb
        if ti == 0:
            load_tile(0, 0)
            load_tile(0, 1)
        if layers is None:
            load_tile(ti + 1, 0)
        if layers is not None:
            o = o_t[hb]; oname = "o_t%d" % hb
            for k in range(8):
                pb = prot.next(); ps = pbanks[pb]
                S.op("pe", _mm_group([(ps[:, :N], wout[:, (k * 8 + c) * 128:(k * 8 + c + 1) * 128], o[:, c, :N]) for c in range(8)]),
                     reads=[oname, "wout"], writes=["pb%d" % pb])
                S.op("dve", lambda e, ps=ps, k=k: e.tensor_tensor(out=h[:, k, :N], in0=h[:, k, :N], in1=ps[:, :N], op=ALU.add),
                     reads=["pb%d" % pb, hname], writes=[hname])
            mlp(h, hname, N, 0, hook_up=lambda: load_tile(ti + 1, 0), hook_dn=lambda: load_tile(ti + 1, 1))
            rms(h, hname, N, 2, out32=u32)
            W = 16 + N
            for g in range(4):
                srcw = u32[:, 2 * g:2 * g + 2, :]
                sname = "u32g%d" % g
                sh = 1
                lo = 0
                for lvl in range(g + 1):
                    dst = wt[lvl % 2]
                    S.op("dve", lambda e, dst=dst, srcw=srcw, sh=sh, lo=lo: e.tensor_tensor(
                        out=dst[:, :, lo + sh:W], in0=srcw[:, :, lo + sh:W], in1=srcw[:, :, lo:W - sh], op=ALU.add),
                        reads=([sname] if sname.startswith("wt") else ["u32_%d" % (2 * g), "u32_%d" % (2 * g + 1)]) + ["u32halo"], writes=["wt%d" % (lvl % 2)])
                    srcw = dst
                    sname = "wt%d" % (lvl % 2)
                    lo += sh
                    sh *= 2
                win = srcw
                if ti == 0:
                    for cc in range(2):
                        S.op("dve", lambda e, win=win, g=g, cc=cc: e.tensor_tensor(
                            out=win[:, cc, 16:16 + N], in0=win[:, cc, 16:16 + N], in1=invcnt[:, g, :N], op=ALU.mult),
                            reads=[sname, "invcnt"], writes=[sname])
                    S.op("dve", lambda e, win=win, g=g: e.tensor_tensor(
                        out=d_bf[:, 2 * g:2 * g + 2, :N], in0=win[:, :, 16:16 + N], in1=u32[:, 2 * g:2 * g + 2, 16:16 + N],
                        op=ALU.subtract), reads=[sname] + ["u32_%d" % c for c in range(DC)], writes=["sq_bf"])
                else:
                    S.op("dve", lambda e, win=win, g=g: e.scalar_tensor_tensor(
                        out=d_bf[:, 2 * g:2 * g + 2, :N], in0=win[:, :, 16:16 + N], scalar=1.0 / (2 ** (g + 1)),
                        in1=u32[:, 2 * g:2 * g + 2, 16:16 + N], op0=ALU.mult, op1=ALU.subtract),
                        reads=[sname] + ["u32_%d" % c for c in range(DC)], writes=["sq_bf"])
            for ke in range(8):
                g = ke // 2
                pb = prot.next(); ps = pbanks[pb]
                S.op("pe", _mm_group([(ps[:, :N], poolw[:, ((g * 2 + ke % 2) * 2 + cc) * 128:((g * 2 + ke % 2) * 2 + cc + 1) * 128],
                                       d_bf[:, 2 * g + cc, :N]) for cc in range(2)]),
                     reads=["sq_bf", "poolw"], writes=["pb%d" % pb])
                S.op("dve", lambda e, ps=ps, ke=ke: e.scalar_tensor_tensor(
                    out=h[:, ke, :N], in0=ps[:, :N], scalar=gains[:, 3, ke:ke + 1], in1=h[:, ke, :N], op0=ALU.mult, op1=ALU.add),
                    reads=["pb%d" % pb, hname, "gains"], writes=[hname])
            S.op("pool", lambda e, N=N: e.tensor_copy(out=u32[:, :, 0:16], in_=u32[:, :, N:N + 16]),
                 reads=["u32_%d" % c for c in range(DC)], writes=["u32halo"])
            mlp(h, hname, N, 1)
        if tail == "out":
            dst = T["out_h"].rearrange("(c p) n -> p c n", p=128)
            out_evs.append(S.dma("pool", dst[:, :, col0:col0 + N], h[:, :, :N], reads=[hname], writes=["out_h"]))
        else:
            if layers is not None:
                dst = T["dst_h"].rearrange("(c p) n -> p c n", p=128)
                out_evs.append(S.dma("pool", dst[:, :, col0:col0 + N], h[:, :, :N], reads=[hname], writes=["dst_h"]))
            rms(h, hname, N, 4, out_bf=u_bf)
            out_evs.append(S.dma("pool", T["xu_dst"](ti, N), u_bf[:, :, :N], reads=["u_bf%d" % c for c in range(DC)], writes=["xu%d" % ti]))
            T["after_xu"](ti)

    for ti, (col0, N) in enumerate(tiles):
        do_tile(ti, col0, N)
    return out_evs


def _mm_list(mms):
    def fn(e):
        ins = None
        for (o, l, r, st, sp) in mms:
            ins = e.matmul(o, l, r, start=st, stop=sp)
        return ins
    return fn


def m_tiles(seq):
    return [(0, PRE)] + [(PRE + 512 * i, 512) for i in range(seq // 512)]


def build_mixer(nc, S, es, seq, T, e_idx):
    LP = PRE + seq
    NKT = 1 + seq // 128
    uid = _uid()
    sb = lambda name, shape, dt: es.enter_context(nc.sbuf_tensor(uid + name, shape, dt))
    pbanks = [es.enter_context(nc.psum_tensor(uid + "mb%d" % i, [128, 512], F32)) for i in range(2)]
    prot = Rot(range(2))
    sbanks = [es.enter_context(nc.psum_tensor(uid + "sb%d" % i, [128, 512], F32)) for i in range(2)]
    srot = Rot(range(2))
    hO_t = es.enter_context(nc.psum_tensor(uid + "hO", [128, 512], F32))
    pO = es.enter_context(nc.psum_tensor(uid + "pO", [128, 512], F32))
    pZ = es.enter_context(nc.psum_tensor(uid + "pZ", [128, 512], F32))
    pT = es.enter_context(nc.psum_tensor(uid + "pT", [128, 1024], BF16))

    ones = sb("ones", [128, 128], BF16)
    S.op("pool", lambda e: e.memset(ones[:], 1.0), writes=["ones"])
    ident = sb("ident", [128, 128], BF16)
    S.dma("pool", ident[:], T["ident"], writes=["ident"])
    wqueue = "sp" if T.get("weights_bf16") else "pool"
    win_fm = sb("win_fm", [128, DC, 1024], BF16)
    S.dma(wqueue, win_fm[:], T["win_fm"], writes=["win_fm"])
    win_hi = sb("win_hi", [128, DC, 128], BF16)
    S.dma(wqueue, win_hi[:], T["win_hi"], writes=["win_hi"])
    wq = sb("wq", [128, 2, 256], BF16)
    S.dma(wqueue, wq[:], T["wq"], writes=["wq"])
    wkv = sb("wkv", [128, 2, 256], BF16)
    S.dma(wqueue, wkv[:], T["wkv"], writes=["wkv"])
    mvec = sb("mvec", [128, 16], F32)
    S.dma("sp", mvec[:], T["mvec"], writes=["mvec"])
    masktab = sb("masktab", [128, 896], F32)
    S.dma("sp", masktab[:], T["masktab"], writes=["masktab"])
    triu = sb("triu", [64, 64], F32)
    S.dma("sp", triu[:], T["triu"], writes=["triu"])
    cmask = sb("cmask", [128, 512], F32)
    S.dma("sp", cmask[:], T["cmask"], writes=["cmask"])
    if T.get("after_loads") is not None:
        T["after_loads"]()
    S.op("dve", lambda e: e.tensor_scalar(out=win_fm[:, :, 960:992], in0=win_fm[:, :, 960:992], scalar1=-1.0, scalar2=None, op0=ALU.mult),
         reads=["win_fm"], writes=["win_fm"])
    S.op("dve", lambda e: e.tensor_scalar(out=wq[:, :, 192:224], in0=wq[:, :, 192:224], scalar1=-1.0, scalar2=None, op0=ALU.mult),
         reads=["wq"], writes=["wq"])
    lbv = sb("lbv", [128, 4], F32)
    if e_idx == 0:
        S.op("dve", lambda e: e.memset(lbv[:, 0:1], 0.0), writes=["lbv"])
        S.op("dve", lambda e: e.memset(lbv[:, 1:2], 1.0), reads=["lbv"], writes=["lbv"])
    else:
        ex = sb("lbex", [128, 2], F32)
        S.op("act", lambda e: e.activation(out=ex[:], in_=mvec[:, 11:13], func=AF.Exp), reads=["mvec"], writes=["lbex"])
        S.op("dve", lambda e: e.tensor_tensor(out=lbv[:, 2:3], in0=ex[:, 0:1], in1=ex[:, 1:2], op=ALU.add), reads=["lbex"], writes=["lbv2"])
        S.op("dve", lambda e: e.reciprocal(out=lbv[:, 3:4], in_=lbv[:, 2:3]), reads=["lbv2"], writes=["lbv3"])
        S.op("dve", lambda e: e.tensor_tensor(out=lbv[:, 0:1], in0=ex[:, 1:2], in1=lbv[:, 3:4], op=ALU.mult), reads=["lbex", "lbv3"], writes=["lbv"])
        S.op("dve", lambda e: e.tensor_tensor(out=lbv[:, 1:2], in0=ex[:, 0:1], in1=lbv[:, 3:4], op=ALU.mult), reads=["lbex", "lbv3", "lbv"], writes=["lbv"])

    Kn = sb("Kn", [128, LP], BF16)
    Kr = sb("Kr", [64, LP], BF16)
    Vs = sb("Vs", [128, NKT * 128], BF16)

    ut = sb("ut", [128, DC, 512], BF16)
    cs = sb("cs", [64, 2, 512], F32)
    wk = {}

    def W(name, shape=(128, 512), dt=F32):
        if name not in wk:
            wk[name] = sb("w_" + name, list(shape), dt)
        return wk[name]

    S32 = sb("S32", [128, 128], F32)
    S_bf2 = [sb("S_bf%d" % i, [128, 128], BF16) for i in range(2)]
    gch = [0]
    S.op("pool", lambda e: e.memset(S32[:], 0.0), writes=["S32"])
    S.op("pool", lambda e: e.memset(S_bf2[0][:], 0.0), writes=["S_bf0"])
    Pt = [sb("Pt%d" % i, [128, 512], BF16) for i in range(3)]
    prot_p = Rot(range(3))
    xo = [sb("xo%d" % i, [128, 2, 512], BF16) for i in range(2)]

    def proj(col0, M, N):
        pb = prot.next(); ps = pbanks[pb]
        S.op("pe", _mm_group([(ps[:M, :N], win_fm[:, c, col0:col0 + M], ut[:, c, :N]) for c in range(DC)]),
             reads=["ut", "win_fm"], writes=["mb%d" % pb])
        return pb, ps

    def rstd_from(ps, pbname, N, dim, outname):
        rt = W("rt")
        rs = W(outname)
        S.op("act", lambda e: e.activation(out=rt[:, :N], in_=ps[:, :N], func=AF.Ln, bias=EPS, scale=1.0 / dim),
             reads=[pbname], writes=["rt"])
        S.op("act", lambda e: e.activation(out=rs[:, :N], in_=rt[:, :N], func=AF.Exp, scale=-0.5), reads=["rt"], writes=[outname])
        return rs

    tiles = m_tiles(seq)
    out_evs = []

    handoff = {}

    def prologue(ti, pos0, N):
        nch = N // 64
        xob = xo[ti % 2]; xoname = "xo%d" % (ti % 2)
        S.dma("sp", ut[:, :, :N], T["uf_tile"](ti, pos0, N), reads=[T["uf_res"](ti)], writes=["ut"])
        S.dma("sp", cs[:, 0, :N], T["cosT"][:, pos0:pos0 + N], writes=["cs0"])
        S.dma("sp", cs[:, 1, :N], T["sinT"][:, pos0:pos0 + N], writes=["cs1"])
        q32, sg, gate, logf, kk, bb, eb, enb = (W(n) for n in ("q32", "sg", "gate", "logf", "kk", "bb", "eb", "enb"))
        qe, ke, kd = (W(n, dt=BF16) for n in ("qe", "ke", "kd"))
        pb, ps = proj(0, 128, N)
        S.op("act", lambda e, ps=ps: e.activation(out=q32[:, :N], in_=ps[:, :N], func=AF.Silu), reads=["mb%d" % pb], writes=["q32"])
        pb, ps = proj(256, 128, N)
        S.op("act", lambda e, ps=ps: e.activation(out=gate[:, :N], in_=ps[:, :N], func=AF.Silu), reads=["mb%d" % pb], writes=["gate"])
        pb, ps = proj(128, 128, N)
        S.op("act", lambda e, ps=ps: e.activation(out=sg[:, :N], in_=ps[:, :N], func=AF.Sigmoid), reads=["mb%d" % pb], writes=["sg"])
        S.op("dve", lambda e: e.tensor_scalar(out=sg[:, :N], in0=sg[:, :N], scalar1=lbv[:, 1:2], scalar2=lbv[:, 0:1], op0=ALU.mult, op1=ALU.add),
             reads=["sg", "lbv"], writes=["sg"])
        S.op("act", lambda e: e.activation(out=logf[:, :N], in_=sg[:, :N], func=AF.Ln), reads=["sg"], writes=["logf"])
        S.op("dve", lambda e: e.tensor_scalar(out=kk[:, :N], in0=sg[:, :N], scalar1=-1.0, scalar2=1.0, op0=ALU.mult, op1=ALU.add),
             reads=["sg"], writes=["kk"])
        if ti == 0:
            S.op("dve", lambda e: e.memset(logf[:, 0:48], 0.0), reads=["logf"], writes=["logf"])
        S.op("dve", lambda e: e.tensor_tensor_scan(out=bb[:, :N], data0=cmask[:, :N], data1=logf[:, :N], initial=0.0, op0=ALU.mult, op1=ALU.add),
             reads=["cmask", "logf"], writes=["bb"])
        S.op("act", lambda e: e.activation(out=eb[:, :N], in_=bb[:, :N], func=AF.Exp), reads=["bb"], writes=["eb"])
        S.op("act", lambda e: e.activation(out=enb[:, :N], in_=bb[:, :N], func=AF.Exp, scale=-1.0), reads=["bb"], writes=["enb"])
        S.op("dve", lambda e: e.tensor_tensor(out=qe[:, :N], in0=q32[:, :N], in1=eb[:, :N], op=ALU.mult), reads=["q32", "eb"], writes=["qe"])
        S.op("dve", lambda e: e.tensor_tensor(out=ke[:, :N], in0=kk[:, :N], in1=enb[:, :N], op=ALU.mult), reads=["kk", "enb"], writes=["ke"])
        for c in range(nch):
            S.op("dve", lambda e, c=c: e.scalar_tensor_tensor(
                out=kd[:, c * 64:(c + 1) * 64], in0=kk[:, c * 64:(c + 1) * 64], scalar=eb[:, c * 64 + 63:c * 64 + 64],
                in1=enb[:, c * 64:(c + 1) * 64], op0=ALU.mult, op1=ALU.mult), reads=["kk", "eb", "enb"], writes=["kd"])

        def tr_fn(e, nch=nch):
            ins = None
            for c in range(nch):
                ins = e.transpose(pT[:64, c * 128:(c + 1) * 128], kd[:, c * 64:(c + 1) * 64], ident[:])
            return ins
        S.op("pe", tr_fn, reads=["kd", "ident"], writes=["pT"])
        kdT = W("kdT", (64, 1024), BF16)
        S.op("act", lambda e, nch=nch: e.activation(out=kdT[:, :nch * 128], in_=pT[:64, :nch * 128], func=AF.Copy), reads=["pT"], writes=["kdT"])
        Vc = W("Vc", (64, 1024), BF16)
        for j0 in range(0, nch, 4):
            pb = prot.next(); ps = pbanks[pb]
            n4 = min(4, nch - j0)

            def vc_fn(e, ps=ps, j0=j0, n4=n4):
                ins = None
                for jj in range(n4):
                    cj = j0 + jj
                    for c in range(DC):
                        ins = e.matmul(ps[:64, jj * 128:(jj + 1) * 128], ut[:, c, cj * 64:(cj + 1) * 64], win_hi[:, c, :],
                                       start=(c == 0), stop=(c == DC - 1))
                return ins
            S.op("pe", vc_fn, reads=["ut", "win_hi"], writes=["mb%d" % pb])
            S.op("act", lambda e, ps=ps, j0=j0, n4=n4: e.activation(out=Vc[:, j0 * 128:(j0 + n4) * 128], in_=ps[:64, :n4 * 128], func=AF.Copy),
                 reads=["mb%d" % pb], writes=["Vc"])
        hO = hO_t
        ATm = W("ATm", (64, 512), BF16)
        U32 = W("U32", (128, 1024))
        for c in range(nch):
            cs_ = slice(c * 64, (c + 1) * 64)
            pb = prot.next(); ps = pbanks[pb]
            S.op("pe", _mm_group([(ps[:64, :64], ke[:, cs_], qe[:, cs_])]), reads=["ke", "qe"], writes=["mb%d" % pb])
            S.op("dve", lambda e, ps=ps, cs_=cs_: e.tensor_tensor(out=ATm[:, cs_], in0=ps[:64, :64], in1=triu[:], op=ALU.mult),
                 reads=["mb%d" % pb, "triu"], writes=["ATm%d" % c])
        for c in range(nch):
            pb2 = prot.next(); ps2 = pbanks[pb2]
            S.op("pe", _mm_group([(ps2[:, :128], kdT[:, c * 128:(c + 1) * 128], Vc[:, c * 128:(c + 1) * 128])]),
                 reads=["kdT", "Vc"], writes=["mb%d" % pb2])
            S.op("act", lambda e, ps2=ps2, c=c: e.activation(out=U32[:, c * 128:(c + 1) * 128], in_=ps2[:, :128], func=AF.Copy),
                 reads=["mb%d" % pb2], writes=["U32_%d" % c])
        for c in range(nch):
            cs_ = slice(c * 64, (c + 1) * 64)
            g = gch[0]; gch[0] += 1
            Sin = S_bf2[g % 2]; Sout = S_bf2[(g + 1) % 2]
            S.op("pe", _mm_group([(hO[:, cs_], Vc[:, c * 128:(c + 1) * 128], ATm[:, cs_]),
                                  (hO[:, cs_], Sin[:], qe[:, cs_])]),
                 reads=["Vc", "ATm%d" % c, "S_bf%d" % (g % 2), "qe"], writes=["hO"])
            S.op("dve", lambda e, c=c, Sout=Sout: e.scalar_tensor_tensor(
                out=Sout[:], in0=S32[:], scalar=eb[:, c * 64 + 63:c * 64 + 64], in1=U32[:, c * 128:(c + 1) * 128], op0=ALU.mult, op1=ALU.add),
                reads=["S32", "eb", "U32_%d" % c], writes=["S_bf%d" % ((g + 1) % 2)])
            S.op("dve", lambda e, c=c: e.scalar_tensor_tensor(
                out=S32[:], in0=S32[:], scalar=eb[:, c * 64 + 63:c * 64 + 64], in1=U32[:, c * 128:(c + 1) * 128], op0=ALU.mult, op1=ALU.add),
                reads=["S32", "eb", "U32_%d" % c], writes=["S32"])
        hO_res = ["hO"]
        osq = W("osq", dt=BF16)
        S.op("act", lambda e: e.activation(out=osq[:, :N], in_=hO[:, :N], func=AF.Square), reads=hO_res, writes=["osq"])
        pb, ps = prot.next(), None
        ps = pbanks[pb]
        S.op("pe", _mm_group([(ps[:, :N], ones[:], osq[:, :N])]), reads=["osq", "ones"], writes=["mb%d" % pb])
        rs = rstd_from(ps, "mb%d" % pb, N, 128.0, "rs_o")
        t1 = W("t1")
        S.op("dve", lambda e, rs=rs: e.scalar_tensor_tensor(out=t1[:, :N], in0=hO[:, :N], scalar=mvec[:, 10:11], in1=rs[:, :N], op0=ALU.mult, op1=ALU.mult),
             reads=hO_res + ["mvec", "rs_o"], writes=["t1"])
        S.op("pool", lambda e: e.tensor_tensor(out=xob[:, 0, :N], in0=t1[:, :N], in1=gate[:, :N], op=ALU.mult),
             reads=["t1", "gate"], writes=[xoname + "a"])
        if HOLD_MLA and S.stepper is not None and threading.current_thread() is S.stepper.th:
            S.stepper.budget = 10 ** 9
        cq32 = W("cq32", (128, 2, 512)); csq = W("csq", (128, 2, 512), BF16)
        cqn = W("cqn", (128, 2, 512), BF16); ckvn = W("ckvn", (128, 2, 512), BF16)
        for (col0, gcol, dst, dname) in ((384, 0, cqn, "cqn"), (640, 2, ckvn, "ckvn")):
            for r in range(2):
                pb, ps = proj(col0 + 128 * r, 128, N)
                S.op("act", lambda e, ps=ps, r=r: e.activation(out=cq32[:, r, :N], in_=ps[:, :N], func=AF.Copy), reads=["mb%d" % pb], writes=["cq32_%d" % r])
                S.op("act", lambda e, ps=ps, r=r: e.activation(out=csq[:, r, :N], in_=ps[:, :N], func=AF.Square), reads=["mb%d" % pb], writes=["csq_%d" % r])
            pb = prot.next(); ps = pbanks[pb]
            S.op("pe", _mm_group([(ps[:, :N], ones[:], csq[:, r, :N]) for r in range(2)]), reads=["csq_0", "csq_1", "ones"], writes=["mb%d" % pb])
            rs = rstd_from(ps, "mb%d" % pb, N, 256.0, "rs_c")
            for r in range(2):
                S.op("dve", lambda e, r=r, dst=dst, gcol=gcol, rs=rs: e.scalar_tensor_tensor(
                    out=dst[:, r, :N], in0=cq32[:, r, :N], scalar=mvec[:, gcol + r:gcol + r + 1], in1=rs[:, :N], op0=ALU.mult, op1=ALU.mult),
                    reads=["cq32_%d" % r, "mvec", "rs_c"], writes=[dname])

        def qk_finish(jn, jr, jt, gbase, out_n, out_n_name, out_r, out_r_name):
            sqn = W("sqn", dt=BF16); sqr = W("sqr", (64, 512), BF16)
            n32 = W("q32"); r32 = W("r32", (64, 512)); t32 = W("t32", (64, 512))
            ps_n, pbn = jn()
            S.op("dve", lambda e: e.tensor_copy(out=n32[:, :N], in_=ps_n[:, :N]), reads=[pbn], writes=["q32"])
            S.op("act", lambda e: e.activation(out=sqn[:, :N], in_=ps_n[:, :N], func=AF.Square), reads=[pbn], writes=["sqn"])
            ps_r, pbr = jr()
            S.op("dve", lambda e: e.tensor_copy(out=r32[:, :N], in_=ps_r[:64, :N]), reads=[pbr], writes=["r32"])
            S.op("act", lambda e: e.activation(out=sqr[:, :N], in_=ps_r[:64, :N], func=AF.Square), reads=[pbr], writes=["sqr"])
            ps_t, pbt = jt()
            S.op("dve", lambda e: e.tensor_copy(out=t32[:, :N], in_=ps_t[:64, :N]), reads=[pbt], writes=["t32"])
            pb = prot.next(); ps = pbanks[pb]
            S.op("pe", _mm_group([(ps[:, :N], ones[:], sqn[:, :N]), (ps[:, :N], ones[:64, :], sqr[:, :N])]),
                 reads=["sqn", "sqr", "ones"], writes=["mb%d" % pb])
            rs = rstd_from(ps, "mb%d" % pb, N, 192.0, "rs_qk")
            S.op("dve", lambda e: e.scalar_tensor_tensor(out=out_n, in0=n32[:, :N], scalar=mvec[:, gbase:gbase + 1], in1=rs[:, :N],
                                                         op0=ALU.mult, op1=ALU.mult), reads=["q32", "mvec", "rs_qk"], writes=[out_n_name])
            ra = r32; rb = t32
            S.op("dve", lambda e: e.scalar_tensor_tensor(out=ra[:, :N], in0=r32[:, :N], scalar=mvec[:64, gbase + 1:gbase + 2], in1=cs[:, 0, :N],
                                                         op0=ALU.mult, op1=ALU.mult), reads=["r32", "mvec", "cs0"], writes=["r32"])
            S.op("dve", lambda e: e.scalar_tensor_tensor(out=rb[:, :N], in0=t32[:, :N], scalar=mvec[:64, gbase + 2:gbase + 3], in1=cs[:, 1, :N],
                                                         op0=ALU.mult, op1=ALU.mult), reads=["t32", "mvec", "cs1"], writes=["t32"])
            S.op("pool", lambda e: e.tensor_tensor(out=ra[:, :N], in0=ra[:, :N], in1=rb[:, :N], op=ALU.add), reads=["r32", "t32"], writes=["r32"])
            S.op("dve", lambda e: e.tensor_tensor(out=out_r, in0=ra[:, :N], in1=rs[:64, :N], op=ALU.mult), reads=["r32", "rs_qk"], writes=[out_r_name])

        def job_w(wt, rdname, src, c0, c1, M):
            def f():
                pb = prot.next(); ps = pbanks[pb]
                S.op("pe", _mm_group([(ps[:M, :N], wt[:, r, c0:c1], src[:, r, :N]) for r in range(2)]), reads=[rdname, "wq", "wkv"], writes=["mb%d" % pb])
                return ps, "mb%d" % pb
            return f

        def job_p(c0, M):
            def f():
                pb, ps = proj(c0, M, N)
                return ps, "mb%d" % pb
            return f

        qnn = "qn%d" % (ti % 2); qrn = "qr%d" % (ti % 2)
        qn = W(qnn, dt=BF16); qr = W(qrn, (64, 512), BF16)
        handoff[ti] = (qn, qr, qnn, qrn)
        qk_finish(job_w(wq, "cqn", cqn, 0, 128, 128), job_w(wq, "cqn", cqn, 128, 192, 64), job_w(wq, "cqn", cqn, 192, 256, 64),
                  4, qn[:, :N], qnn, qr[:, :N], qrn)
        kvres = "KV_t%d" % ti
        qk_finish(job_w(wkv, "ckvn", ckvn, 0, 128, 128), job_p(896, 64), job_p(960, 64),
                  7, Kn[:, pos0:pos0 + N], kvres + "n", Kr[:, pos0:pos0 + N], kvres + "r")
        pb = prot.next(); ps = pbanks[pb]
        if ti == 0:
            subs = [(0, 64)]
            kt0 = 0
        else:
            subs = [(jj * 128, 128) for jj in range(4)]
            kt0 = 1 + 4 * (ti - 1)

        def v_fn(e, ps=ps, subs=subs):
            ins = None
            for jj, (c0, M) in enumerate(subs):
                for r in range(2):
                    ins = e.matmul(ps[:M, jj * 128:(jj + 1) * 128], ckvn[:, r, c0:c0 + M], wkv[:, r, 128:256], start=(r == 0), stop=(r == 1))
            return ins
        S.op("pe", v_fn, reads=["ckvn", "wkv"], writes=["mb%d" % pb])
        Mv = subs[0][1]
        S.op("act", lambda e, ps=ps, kt0=kt0, Mv=Mv, ns=len(subs): e.activation(
            out=Vs[:Mv, kt0 * 128:(kt0 + ns) * 128], in_=ps[:Mv, :ns * 128], func=AF.Copy), reads=["mb%d" % pb], writes=[kvres + "v"])
        if T.get("tile_hook") is not None:
            T["tile_hook"](ti, len(tiles))
    def attention(ti, pos0, N, stepper):
        xob = xo[ti % 2]; xoname = "xo%d" % (ti % 2)
        qn, qr, qnn, qrn = handoff.pop(ti)
        if ti == 0:
            kts = [(0, 64, "pad", 0)]
        else:
            kts = [(0, 64, "pad", 0)] + [(j, 128, None, 0) for j in range(1, 4 * (ti - 1) + 1)] + \
                  [(4 * (ti - 1) + 1 + jj, 128, "diag", jj) for jj in range(4)]
        nk = len(kts)
        pend = []

        def emit_S(idx):
            j, M, kind, jj = kts[idx]
            kc0 = 0 if j == 0 else PRE + 128 * (j - 1)
            tk = 0 if j == 0 else 1 + (j - 1) // 4
            kres = "KV_t%d" % tk
            pb = srot.next(); ps = sbanks[pb]
            S.op("pe", _mm_group([(ps[:M, :N], Kn[:, kc0:kc0 + M], qn[:, :N]), (ps[:M, :N], Kr[:, kc0:kc0 + M], qr[:, :N])]),
                 reads=[kres + "n", kres + "r", qnn, qrn], writes=["sb%d" % pb])
            p = prot_p.next(); P = Pt[p]
            if kind == "pad":
                S.op("act", lambda e: e.activation(out=P[:M, :N], in_=ps[:M, :N], func=AF.Exp, bias=mvec[:M, 13:14], scale=ATT_SCALE),
                     reads=["sb%d" % pb, "mvec"], writes=["Pt%d" % p])
            elif kind == "diag":
                tmp = W("dtmp")
                off = (3 - jj) * 128
                S.op("dve", lambda e: e.tensor_tensor(out=tmp[:, :N], in0=ps[:, :N], in1=masktab[:, off:off + N], op=ALU.add),
                     reads=["sb%d" % pb, "masktab"], writes=["dtmp"])
                S.op("act", lambda e: e.activation(out=P[:, :N], in_=tmp[:, :N], func=AF.Exp, scale=ATT_SCALE), reads=["dtmp"], writes=["Pt%d" % p])
            else:
                S.op("act", lambda e: e.activation(out=P[:, :N], in_=ps[:, :N], func=AF.Exp, scale=ATT_SCALE), reads=["sb%d" % pb], writes=["Pt%d" % p])
            return (idx, j, M, p, kres)

        def emit_OZ(rec):
            idx, j, M, p, kres = rec
            P = Pt[p]
            S.op("pe", _mm_list([(pO[:, :N], Vs[:M, j * 128:(j + 1) * 128], P[:M, :N], idx == 0, idx == nk - 1),
                                 (pZ[:, :N], ones[:M, :], P[:M, :N], idx == 0, idx == nk - 1)]),
                 reads=[kres + "v", "Pt%d" % p, "ones"], writes=["pO", "pZ"])

        LA = 1
        per_iter = max(1, -(-200 // max(1, nk - 2)))
        for idx in range(nk):
            pend.append(emit_S(idx))
            if len(pend) > LA:
                emit_OZ(pend.pop(0))
            if stepper is not None and INTERLEAVE and idx % STEP_EVERY == STEP_EVERY - 1:
                stepper.step(per_iter * STEP_EVERY)
        while pend:
            emit_OZ(pend.pop(0))
        if stepper is not None:
            stepper.finish()
        rz = W("rz")
        S.op("dve", lambda e: e.reciprocal(out=rz[:, :N], in_=pZ[:, :N]), reads=["pZ"], writes=["rz"])
        S.op("dve", lambda e: e.tensor_tensor(out=xob[:, 1, :N], in0=pO[:, :N], in1=rz[:, :N], op=ALU.mult), reads=["pO", "rz"], writes=[xoname + "b"])
        if ti == 0:
            S.op("pool", lambda e: e.memset(xob[:, :, 0:48], 0.0), reads=[xoname + "a", xoname + "b"], writes=[xoname + "a", xoname + "b"])
        out_evs.append(S.dma("sp", T["xo_dst"](ti, N), xob[:, :, :N], reads=[xoname + "a", xoname + "b"], writes=["xo%d" % ti]))
        T["after_xo"](ti)

    for par in range(2):
        W("qn%d" % par, dt=BF16)
        W("qr%d" % par, (64, 512), BF16)
    prologue(0, *tiles[0])
    for ti, (pos0, N) in enumerate(tiles):
        st = None
        if ti + 1 < len(tiles):
            st = Stepper(lambda ti=ti: prologue(ti + 1, *tiles[ti + 1]))
        S.stepper = st
        attention(ti, pos0, N, st)
        S.stepper = None
    return out_evs


IN_HQ, IN_HF, IN_HI, IN_HG, IN_CQ, IN_CKV, IN_KR = 0, 512, 1024, 1536, 2048, 2304, 2560
BF = ml_dtypes.bfloat16


def _pc(v):
    v = np.asarray(v, np.float32)
    return np.ascontiguousarray(v.reshape(-1, 128).T)


def host_tphase_weights(inp, le, lo, lnext):
    e, o = le // 2, lo // 2
    w = {}
    w["wout_f32"] = np.ascontiguousarray(
        inp["w_out"][e].reshape(8, 128, 8, 128).transpose(1, 2, 0, 3).reshape(128, 8192))
    w["wup_f32"] = np.ascontiguousarray(np.stack([
        inp["w_mlp_up"][l].reshape(8, 128, 8, 512).transpose(2, 1, 0, 3).reshape(8, 128, 4096) for l in (le, lo)]))
    w["wdn_f32"] = np.ascontiguousarray(np.stack([
        inp["w_mlp_down"][l].reshape(32, 128, 8, 128).transpose(2, 1, 0, 3).reshape(8, 128, 4096) for l in (le, lo)]))
    w["poolw_f32"] = np.ascontiguousarray(
        inp["pool_w"][o].reshape(4, 2, 128, 2, 128).transpose(2, 0, 3, 1, 4).reshape(128, 2048))
    g = np.zeros((128, 5, 8), np.float32)
    g[:, 0] = _pc(inp["mlp_norm"][le]); g[:, 1] = _pc(inp["mlp_norm"][lo])
    g[:, 2] = _pc(inp["mix_norm"][lo]); g[:, 3] = _pc(inp["pool_scale"][o])
    if lnext is not None:
        g[:, 4] = _pc(inp["mix_norm"][lnext])
    w["gains"] = g
    return w


def host_invcnt(c):
    t = np.zeros((128, 4, PRE), np.float32)
    for g in range(4):
        wdw = 2 ** (g + 1)
        t[:, g, :] = 1.0 / wdw
        if c == 0:
            for col in range(48, PRE):
                pos = col - 48
                t[:, g, col] = 1.0 / min(pos + 1, wdw)
    return t


def host_mixer_weights(inp, e, hd):
    w_in = inp["w_in"][e]
    cols = np.concatenate([
        np.arange(IN_HQ + hd * 128, IN_HQ + (hd + 1) * 128), np.arange(IN_HF + hd * 128, IN_HF + (hd + 1) * 128),
        np.arange(IN_HG + hd * 128, IN_HG + (hd + 1) * 128), np.arange(IN_CQ, IN_CQ + 256), np.arange(IN_CKV, IN_CKV + 256),
        np.arange(IN_KR, IN_KR + 64), np.arange(IN_KR + 32, IN_KR + 64), np.arange(IN_KR, IN_KR + 32)])
    w = {}
    w["win_fm"] = np.ascontiguousarray(w_in[:, cols].reshape(8, 128, 1024).transpose(1, 0, 2))
    w["win_hi"] = np.ascontiguousarray(w_in[:, IN_HI + hd * 128:IN_HI + (hd + 1) * 128].reshape(8, 128, 128).transpose(1, 0, 2))
    qb = hd * 192
    qcols = np.concatenate([np.arange(qb, qb + 192), np.arange(qb + 160, qb + 192), np.arange(qb + 128, qb + 160)])
    w["wq"] = np.ascontiguousarray(inp["w_q_up"][e][:, qcols].reshape(2, 128, 256).transpose(1, 0, 2))
    w["wkv"] = np.ascontiguousarray(inp["w_kv_up"][e][:, hd * 256:(hd + 1) * 256].reshape(2, 128, 256).transpose(1, 0, 2))
    mv = np.zeros((128, 16), np.float32)
    mv[:, 0:2] = _pc(inp["mla_q_a_norm"][e]); mv[:, 2:4] = _pc(inp["mla_kv_a_norm"][e])
    for base, g in ((4, inp["q_norm"][e]), (7, inp["k_norm"][e])):
        g = np.asarray(g, np.float32)
        mv[:, base] = g[0:128]
        mv[:64, base + 1] = g[128:192]
        mv[:32, base + 2] = g[160:192]; mv[32:64, base + 2] = g[128:160]
    mv[:, 10] = inp["hgrn_out_norm"][e]
    mv[:, 11] = inp["hgrn_lb"][0][hd * 128:(hd + 1) * 128]
    mv[:, 12] = inp["hgrn_lb"][1][hd * 128:(hd + 1) * 128]
    mv[:48, 13] = NEG
    w["mvec"] = mv
    return w


def host_tables(seq):
    LP = PRE + seq
    pos = np.maximum(np.arange(LP) - 48, 0).astype(np.float32)
    inv = (np.float32(10000.0) ** (-(np.arange(32, dtype=np.float32)) / np.float32(32))).astype(np.float32)
    ang = (pos[:, None] * inv[None, :]).astype(np.float32)
    t = {}
    t["cosT"] = np.ascontiguousarray(np.tile(np.cos(ang).astype(np.float32).T, (2, 1)))
    t["sinT"] = np.ascontiguousarray(np.tile(np.sin(ang).astype(np.float32).T, (2, 1)))
    kc = (np.arange(128) // 64)[:, None]
    x = np.arange(896)[None, :]
    t["masktab"] = np.where(kc <= x // 64 - 6, 0.0, NEG).astype(np.float32)
    si = np.arange(64)
    t["triu"] = (si[:, None] <= si[None, :]).astype(np.float32)
    cm = np.ones((128, 512), np.float32); cm[:, ::64] = 0.0
    t["cmask"] = cm
    t["ident"] = np.eye(128, dtype=np.float32)
    return t


import contextlib


def _dram(nc, name, shape, dt, kind):
    return nc.dram_tensor(name, list(shape), dt, kind=kind).ap()


GROUPS = [[0, 1, 2, 3], [4, 5, 6, 7]]
M_W = (("win_fm", [128, 8, 1024]), ("win_hi", [128, 8, 128]), ("wq", [128, 2, 256]), ("wkv", [128, 2, 256]), ("mvec", [128, 16]))
M_TAB = (("cosT", None), ("sinT", None), ("masktab", [128, 896]), ("triu", [64, 64]), ("cmask", [128, 512]), ("ident", [128, 128]))


XB = 2048


def build_fused(seq, stop=None):
    nown = seq // 4
    ncol = PRE + nown
    LP = PRE + seq
    NTt = nown // 512
    NB = seq // XB
    nc = bass.Bass("TRN2", target_bir_lowering=False)
    xT = _dram(nc, "xT", [D, ncol], F32, "ExternalInput")
    sel = _dram(nc, "sel", [128, 4], F32, "ExternalInput")
    invcnt = _dram(nc, "invcnt", [128, 4, PRE], F32, "ExternalInput")
    g0 = _dram(nc, "gains_p0", [128, 5, 8], F32, "ExternalInput")
    tabs = {}
    for name, shape in M_TAB:
        tabs[name] = _dram(nc, name, shape or [64, LP], F32, "ExternalInput")
    mw = []
    for e in range(2):
        mw.append({name: _dram(nc, "%s_e%d" % (name, e), shape, F32, "ExternalInput") for name, shape in M_W})
    tw = []
    for t in range(2):
        d = {}
        d["wout_f32"] = _dram(nc, "wout_f32_t%d" % t, [128, 8192], F32, "ExternalInput")
        d["poolw_f32"] = _dram(nc, "poolw_f32_t%d" % t, [128, 2048], F32, "ExternalInput")
        d["wup_f32"] = _dram(nc, "wup_f32_t%d" % t, [2, 8, 128, 4096], F32, "ExternalInput")
        d["wdn_f32"] = _dram(nc, "wdn_f32_t%d" % t, [2, 8, 128, 4096], F32, "ExternalInput")
        d["gains"] = _dram(nc, "gains_t%d" % t, [128, 5, 8], F32, "ExternalInput")
        d["wup_bf"] = _dram(nc, "wup_bf_t%d" % t, [2, 8, 128, 4096], BF16, "Internal")
        d["wdn_bf"] = _dram(nc, "wdn_bf_t%d" % t, [2, 8, 128, 4096], BF16, "Internal")
        d["wout_bf"] = _dram(nc, "wout_bf_t%d" % t, [128, 8192], BF16, "Internal")
        d["poolw_bf"] = _dram(nc, "poolw_bf_t%d" % t, [128, 2048], BF16, "Internal")
        d["invcnt"] = invcnt
        d["sel"] = sel
        tw.append(d)
    out_h = _dram(nc, "out_h", [D, ncol], F32, "ExternalOutput")
    hT = _dram(nc, "hT", [D, ncol], F32, "Internal")
    xu_pre = _dram(nc, "xu_pre", [D, PRE], BF16, "Internal")
    uf_pre = _dram(nc, "uf_pre", [4 * D, PRE], BF16, "Internal")
    xu_blk = [_dram(nc, "xu_blk%d" % t, [D, 512], BF16, "Internal") for t in range(NTt)]
    uf_blk = [_dram(nc, "uf_blk%d" % t, [4 * D, 512], BF16, "Internal") for t in range(NTt)]
    xo_pre = _dram(nc, "xo_pre", [256, PRE], BF16, "Internal")
    of_pre = _dram(nc, "of_pre", [4 * 256, PRE], BF16, "Internal")
    xo_blk = [_dram(nc, "xo_blk%d" % k, [256, XB], BF16, "Internal") for k in range(NB)]
    of_blk = [_dram(nc, "of_blk%d" % k, [4 * 256, XB], BF16, "Internal") for k in range(NB)]

    S = Sched(nc).open()
    MWN = ("win_fm", "win_hi", "wq", "wkv")
    mw_bf = [{name: _dram(nc, "%s_bf_e%d" % (name, e), dict(M_W)[name], BF16, "Internal") for name in MWN} for e in range(2)]

    def emit_m_conversions(e):
        for name in MWN:
            S.dma("pool", mw_bf[e][name], mw[e][name], writes=["mwbf"])

    def xu_dst(ti, N):
        t = xu_pre if ti == 0 else xu_blk[ti - 1]
        return t.rearrange("(c p) n -> p c n", p=128)

    def after_xu(ti):
        if ti == 0:
            S.collective("AllGather", [xu_pre], [uf_pre], GROUPS, reads=["xu0"], writes=["ufpre"])
        else:
            S.collective("AllGather", [xu_blk[ti - 1]], [uf_blk[ti - 1]], GROUPS, reads=["xu%d" % ti], writes=["ufblk%d" % (ti - 1)])

    def uf_tile(ti, pos0, N):
        if ti == 0:
            return uf_pre.rearrange("(g c p) n -> g p c n", g=4, p=128)[0]
        i = ti - 1
        return uf_blk[i % NTt].rearrange("(g c p) n -> g p c n", g=4, p=128)[i // NTt]

    def xo_dst(ti, N):
        if ti == 0:
            return xo_pre.rearrange("(j p) n -> p j n", p=128)
        i = ti - 1
        off = (i % 4) * 512
        return xo_blk[i // 4].rearrange("(j p) n -> p j n", p=128)[:, :, off:off + 512]

    def after_xo(ti):
        if ti == 0:
            S.collective("AllGather", [xo_pre], [of_pre], GROUPS, reads=["xo0"], writes=["ofpre"])
        elif (ti - 1) % 4 == 3:
            k = (ti - 1) // 4
            S.collective("AllGather", [xo_blk[k]], [of_blk[k]], GROUPS,
                         reads=["xo%d" % (4 * k + 1 + j) for j in range(4)], writes=["ofblk%d" % k])

    def of_tiles(col0, N, cc):
        if col0 == 0 and cc == 0:
            src, off, res = of_pre, 0, "ofpre"
        else:
            q = cc * nown + col0 - PRE
            src, off, res = of_blk[q // XB], q % XB, "ofblk%d" % (q // XB)
        v = src.rearrange("(h j p) n -> j p h n", h=4, p=128)
        return [(j * 4, j * 4 + 4, v[j][:, :, off:off + N], res) for j in range(2)]

    def uf_res(ti):
        return "ufpre" if ti == 0 else "ufblk%d" % ((ti - 1) % NTt)

    def run_phase(fn):
        with contextlib.ExitStack() as es:
            evs = fn(es)
            with nc.Block() as block:
                S.emit(block)
        return evs

    def dump(src, shape, dt):
        dbg = _dram(nc, "dbg_out", shape, dt, "ExternalOutput")
        ev = S.dma("sp", dbg, src, writes=["dbg"])
        with nc.Block() as block:
            S.finish("sp", [ev])
            S.emit(block)
        S.close()
        return nc

    emit_m_conversions(0)
    run_phase(lambda es: build_tphase(nc, S, es, nown, dict(src_h=xT, gains=g0, xu_dst=xu_dst, after_xu=after_xu), None, "xu"))
    S.barrier(keep=("uf",), skip_cc=True)
    if stop == "ag1":
        return dump(uf_blk[0], [4 * D, 512], BF16)
    final_evs = None
    for e in range(2):
        Tm = dict(mw[e]); Tm.update(mw_bf[e]); Tm["weights_bf16"] = True
        Tm.update(tabs); Tm["uf_tile"] = uf_tile; Tm["uf_res"] = uf_res; Tm["xo_dst"] = xo_dst; Tm["after_xo"] = after_xo
        conv_items = t_conversion_items(tw[e])

        def tile_hook(ti, ntiles, conv_items=conv_items):
            n = len(conv_items) if ti == ntiles - 1 else min(len(conv_items), -(-34 // ntiles))
            for _ in range(n):
                dst, src, res = conv_items.pop(0)
                S.dma("pool", dst, src, writes=[res])
        Tm["tile_hook"] = tile_hook
        run_phase(lambda es, e=e, Tm=Tm: build_mixer(nc, S, es, seq, Tm, e))
        S.barrier(keep=("of",), skip_cc=True)
        if stop == "ag2":
            return dump(of_blk[0], [4 * 256, XB], BF16)
        Tt = dict(tw[e]); Tt["of_tiles"] = of_tiles
        Tt["src_h"] = xT if e == 0 else hT
        if e == 0:
            Tt["dst_h"] = hT; Tt["xu_dst"] = xu_dst; Tt["after_xu"] = after_xu
            Tt["after_loads"] = lambda: emit_m_conversions(1)
        else:
            Tt["out_h"] = out_h
        final_evs = run_phase(lambda es, e=e, Tt=Tt: build_tphase(nc, S, es, nown, Tt, (2 * e, 2 * e + 1), "xu" if e == 0 else "out", convert=False))
        if e == 0:
            S.barrier(keep=("uf",), skip_cc=True)
        else:
            S.barrier()
    with nc.Block() as block:
        S.finish("sp", final_evs)
        S.emit(block)
    S.close()
    return nc


def kernel(_stop=None, **inputs):
    inp = {k: np.asarray(v) for k, v in inputs.items()}
    x = inp["x"].astype(np.float32)
    B, seq, _ = x.shape
    nown = seq // 4
    ncol = PRE + nown
    LP = PRE + seq
    tabs = host_tables(seq)
    g0 = np.zeros((128, 5, 8), np.float32); g0[:, 4] = _pc(inp["mix_norm"][0])
    tws = [host_tphase_weights(inp, 0, 1, 2), host_tphase_weights(inp, 2, 3, None)]
    in_maps = []
    for r in range(8):
        b, c = divmod(r, 4)
        P = np.zeros((LP, D), np.float32)
        P[48:64] = inp["meta_tokens"]
        P[64:] = x[b]
        m = dict(xT=np.ascontiguousarray(P[c * nown:c * nown + ncol].T), gains_p0=g0, invcnt=host_invcnt(c))
        selv = np.zeros((128, 4), np.float32); selv[:, c] = 1.0
        m["sel"] = selv
        m.update(tabs)
        for e in range(2):
            for k, v in host_mixer_weights(inp, e, c).items():
                m["%s_e%d" % (k, e)] = v
        for t in range(2):
            for k, v in tws[t].items():
                m["%s_t%d" % (k, t)] = v
        in_maps.append(m)
    if _stop is not None:
        ncd = build_fused(seq, _stop)
        used = set(t.name for t in ncd.m.functions[0].allocations if hasattr(t, "name")) if False else None
        return run_bass_kernel_spmd(ncd, in_maps, core_ids=list(range(8))).results
    res = run_bass_kernel_spmd(build_fused(seq), in_maps, core_ids=list(range(8))).results
    out = np.zeros((B, seq, D), np.float32)
    for r in range(8):
        b, c = divmod(r, 4)
        out[b, c * nown:(c + 1) * nown] = res[r]["out_h"][:, PRE:].T
    return out
```
